# Optimizing a Trainium2 kernel written in Bass

```python
import math
import jax, jax.numpy as jnp
from jax import lax
import numpy as np


D_MODEL = 1024
BATCH = 8
SEQ = 2048
DEPTH = 2
DEC_BATCH = 128
DEC_SEQ = 4
PAST_LEN = 16384
PAGE_SIZE = 128

N_HEADS_A = 8
N_KV_A = 2
HEAD_DIM = 64
GQA_GROUP = N_HEADS_A // N_KV_A
WINDOW = 128
ATTN_WIDTH = N_HEADS_A * HEAD_DIM
KV_WIDTH = N_KV_A * HEAD_DIM
N_BUCKETS = 32
MAX_DISTANCE = 128
POOL_WINDOWS = (2, 4, 8, 16)
N_POOL_GROUPS = len(POOL_WINDOWS)
POOL_GROUP = 128
POOL_WIDTH = N_POOL_GROUPS * POOL_GROUP
POOL_BUF = max(POOL_WINDOWS) - 1
IN_EVEN = ATTN_WIDTH + 2 * KV_WIDTH + POOL_WIDTH
CHUNK = 128
SGU_WIDTH = 1024
SGU_GROUPS = 4
SGU_GROUP_W = SGU_WIDTH // SGU_GROUPS
D_FF = 2816
N_EVEN = (DEPTH + 1) // 2
N_ODD = DEPTH // 2
EPS = 1e-6
NEG = -1e30

kernel_name = 'hybrid_swa_pool_sgu_macaron_step'


def rmsnorm(x, g):
    xf = x.astype(jnp.float32)
    y = xf * lax.rsqrt(jnp.mean(xf * xf, axis=-1, keepdims=True) + EPS)
    return (y * g.astype(jnp.float32)).astype(x.dtype)


def macaron_half(x, g, wg, wu, wd):
    h = rmsnorm(x, g)
    return x + 0.5 * ((jax.nn.silu(h @ wg) * (h @ wu)) @ wd)


def t5_bucket(dist):
    n = jnp.maximum(dist, 0)
    max_exact = N_BUCKETS // 2
    nf = jnp.maximum(n, 1).astype(jnp.float32)
    large = max_exact + (jnp.log(nf / max_exact) / math.log(MAX_DISTANCE / max_exact)
                         * (N_BUCKETS - max_exact)).astype(jnp.int32)
    large = jnp.minimum(large, N_BUCKETS - 1)
    return jnp.where(n < max_exact, n, large)


def band_attention(q, k, v, dist, valid, sink, rel_bias):
    B, N, Q = q.shape[:3]
    S = k.shape[2]
    qg = q.reshape(B, N, Q, N_KV_A, GQA_GROUP, HEAD_DIM)
    logits = jnp.einsum('bnqkgd,bnskd->bnkgqs', qg, k).astype(jnp.float32) * (HEAD_DIM ** -0.5)
    bias = rel_bias[t5_bucket(dist)].astype(jnp.float32)
    bias = bias.transpose(2, 0, 1).reshape(N_KV_A, GQA_GROUP, Q, S)
    logits = jnp.where(valid[None, :, None, None], logits + bias, NEG)
    s = sink.astype(jnp.float32).reshape(N_KV_A, GQA_GROUP, 1, 1)
    m = jnp.maximum(jnp.max(logits, axis=-1, keepdims=True), s)
    p = jnp.exp(logits - m)
    p = p / (jnp.sum(p, axis=-1, keepdims=True) + jnp.exp(s - m))
    out = jnp.einsum('bnkgqs,bnskd->bnqkgd', p.astype(v.dtype), v)
    return out.reshape(B, N, Q, ATTN_WIDTH)


def attn_prompt(q, k, v, sink, rel_bias):
    B, S = q.shape[:2]
    nb = S // WINDOW
    qb = q.reshape(B, nb, WINDOW, N_HEADS_A, HEAD_DIM)

    def band(t):
        tp = jnp.concatenate([jnp.zeros_like(t[:, :WINDOW]), t], axis=1)
        tp = tp.reshape(B, nb + 1, WINDOW, N_KV_A, HEAD_DIM)
        return jnp.concatenate([tp[:, :-1], tp[:, 1:]], axis=2)

    kb, vb = band(k), band(v)
    qi = jnp.arange(WINDOW)[:, None] + WINDOW
    kj = jnp.arange(2 * WINDOW)[None, :]
    dist = qi - kj
    blk = jnp.arange(nb)[:, None, None]
    valid = (dist >= 0) & (dist < WINDOW) & (blk * WINDOW + kj - WINDOW >= 0)
    out = band_attention(qb, kb, vb, dist, valid, sink, rel_bias)
    return out.reshape(B, S, ATTN_WIDTH)


def attn_sample(q, k, v, buf_k, buf_v, sink, rel_bias):
    B, T = q.shape[:2]
    kk = jnp.concatenate([buf_k.astype(k.dtype), k], axis=1)
    vv = jnp.concatenate([buf_v.astype(v.dtype), v], axis=1)
    dist = (WINDOW + jnp.arange(T))[:, None] - jnp.arange(WINDOW + T)[None, :]
    valid = ((dist >= 0) & (dist < WINDOW))[None]
    out = band_attention(q[:, None], kk[:, None], vv[:, None], dist, valid, sink, rel_bias)
    return out.reshape(B, T, ATTN_WIDTH), kk[:, -WINDOW:], vv[:, -WINDOW:]


def pool_mix(u, ctx, pos0, w_pool, scale):
    B, T = u.shape[:2]
    z = jnp.concatenate([ctx.astype(u.dtype), u], axis=1)
    zf = z.astype(jnp.float32)
    cs = jnp.concatenate([jnp.zeros_like(zf[:, :1]), jnp.cumsum(zf, axis=1)], axis=1)
    end = cs[:, POOL_BUF + 1:]
    pos = pos0 + jnp.arange(T)
    means = []
    for g, w in enumerate(POOL_WINDOWS):
        sl = slice(g * POOL_GROUP, (g + 1) * POOL_GROUP)
        start = cs[:, POOL_BUF + 1 - w: POOL_BUF + 1 - w + T, sl]
        cnt = jnp.minimum(pos + 1, w).astype(jnp.float32)[None, :, None]
        means.append((end[..., sl] - start) / cnt)
    d = jnp.concatenate(means, axis=-1) - zf[:, POOL_BUF:]
    d = d.astype(u.dtype).reshape(B, T, N_POOL_GROUPS, POOL_GROUP)
    y = jnp.einsum('btgc,gcd->btgd', d, w_pool).reshape(B, T, POOL_WIDTH)
    return y * scale, z[:, -POOL_BUF:]


def even_mixer(h, prompt, buf_k, buf_v, buf_pool, w_in, w_out, sink, rel_bias, w_pool, pool_scale):
    B, T = h.shape[:2]
    proj = h @ w_in
    q, k, v, up = jnp.split(proj, [ATTN_WIDTH, ATTN_WIDTH + KV_WIDTH, ATTN_WIDTH + 2 * KV_WIDTH], axis=-1)
    q = q.reshape(B, T, N_HEADS_A, HEAD_DIM)
    k = k.reshape(B, T, N_KV_A, HEAD_DIM)
    v = v.reshape(B, T, N_KV_A, HEAD_DIM)
    if prompt:
        a = attn_prompt(q, k, v, sink, rel_bias)
        nk, nv = k[:, -WINDOW:], v[:, -WINDOW:]
        p, npool = pool_mix(up, jnp.zeros((B, POOL_BUF, POOL_WIDTH), up.dtype), 0, w_pool, pool_scale)
    else:
        a, nk, nv = attn_sample(q, k, v, buf_k, buf_v, sink, rel_bias)
        p, npool = pool_mix(up, buf_pool, PAST_LEN, w_pool, pool_scale)
    y = jnp.concatenate([a, p], axis=-1) @ w_out
    return y, nk, nv, npool


def odd_mixer(h, w_in, g_v, w_s, b_s, w_out):
    B, T = h.shape[:2]
    uv = jax.nn.gelu(h @ w_in)
    u, v = jnp.split(uv, 2, axis=-1)
    v = rmsnorm(v, g_v)
    L = min(T, CHUNK)
    nc = T // L
    vc = v.reshape(B, nc, L, SGU_GROUPS, SGU_GROUP_W)
    w = jnp.tril(w_s[:, :L, :L])
    mixed = jnp.einsum('gts,bnsgc->bntgc', w, vc) + b_s[:, :L].T[:, :, None]
    y = (u * mixed.reshape(B, T, SGU_WIDTH)) @ w_out
    return y, v


def setup_inputs(seed: int = 0) -> dict:
    key = jax.random.key(seed)
    ks = jax.random.split(key, 21)

    def nrm(k, shape, s):
        return jax.random.normal(k, shape, jnp.float32) * s

    return {
        'x_prompt': nrm(ks[0], (BATCH, SEQ, D_MODEL), 1.0),
        'x_sample': nrm(ks[1], (DEC_BATCH, DEC_SEQ, D_MODEL), 1.0),
        'state_win_k': nrm(ks[2], (N_EVEN, DEC_BATCH, WINDOW, N_KV_A, HEAD_DIM), 1.0),
        'state_win_v': nrm(ks[3], (N_EVEN, DEC_BATCH, WINDOW, N_KV_A, HEAD_DIM), 1.0),
        'state_pool': nrm(ks[4], (N_EVEN, DEC_BATCH, POOL_BUF, POOL_WIDTH), 1.0),
        'rel_bias': nrm(ks[5], (N_BUCKETS, N_HEADS_A), 0.5),
        'norm_gains': 1.0 + nrm(ks[6], (DEPTH, 3, D_MODEL), 0.05),
        'final_gain': 1.0 + nrm(ks[7], (D_MODEL,), 0.05),
        'ffn_gate': nrm(ks[8], (DEPTH, 2, D_MODEL, D_FF), D_MODEL ** -0.5),
        'ffn_up': nrm(ks[9], (DEPTH, 2, D_MODEL, D_FF), D_MODEL ** -0.5),
        'ffn_down': nrm(ks[10], (DEPTH, 2, D_FF, D_MODEL), D_FF ** -0.5),
        'w_in_even': nrm(ks[11], (N_EVEN, D_MODEL, IN_EVEN), D_MODEL ** -0.5),
        'w_out_even': nrm(ks[12], (N_EVEN, ATTN_WIDTH + POOL_WIDTH, D_MODEL), (ATTN_WIDTH + POOL_WIDTH) ** -0.5),
        'attn_sinks': nrm(ks[13], (N_EVEN, N_HEADS_A), 1.0),
        'w_pool': nrm(ks[14], (N_EVEN, N_POOL_GROUPS, POOL_GROUP, POOL_GROUP), POOL_GROUP ** -0.5),
        'pool_scale': 1.0 + nrm(ks[15], (N_EVEN, POOL_WIDTH), 0.1),
        'w_in_odd': nrm(ks[16], (N_ODD, D_MODEL, 2 * SGU_WIDTH), D_MODEL ** -0.5),
        'sgu_norm': 1.0 + nrm(ks[17], (N_ODD, SGU_WIDTH), 0.05),
        'w_spatial': nrm(ks[18], (N_ODD, SGU_GROUPS, CHUNK, CHUNK), CHUNK ** -0.5),
        'b_spatial': 1.0 + nrm(ks[19], (N_ODD, SGU_GROUPS, CHUNK), 0.1),
        'w_out_odd': nrm(ks[20], (N_ODD, SGU_WIDTH, D_MODEL), SGU_WIDTH ** -0.5),
    }


def reference(x_prompt, x_sample, state_win_k, state_win_v, state_pool, rel_bias, norm_gains, final_gain,
              ffn_gate, ffn_up, ffn_down, w_in_even, w_out_even, attn_sinks, w_pool, pool_scale,
              w_in_odd, sgu_norm, w_spatial, b_spatial, w_out_odd):
    xp, xs = x_prompt, x_sample
    kp_l, vp_l, pp_l, ks_l, vs_l, ps_l, sv_l = [], [], [], [], [], [], []
    for l in range(DEPTH):
        fa = (norm_gains[l, 0], ffn_gate[l, 0], ffn_up[l, 0], ffn_down[l, 0])
        fb = (norm_gains[l, 2], ffn_gate[l, 1], ffn_up[l, 1], ffn_down[l, 1])
        xp, xs = macaron_half(xp, *fa), macaron_half(xs, *fa)
        hp, hs = rmsnorm(xp, norm_gains[l, 1]), rmsnorm(xs, norm_gains[l, 1])
        if l % 2 == 0:
            e = l // 2
            wts = (w_in_even[e], w_out_even[e], attn_sinks[e], rel_bias, w_pool[e], pool_scale[e])
            yp, kp, vp, pp = even_mixer(hp, True, None, None, None, *wts)
            ys, k_s, v_s, p_s = even_mixer(hs, False, state_win_k[e], state_win_v[e], state_pool[e], *wts)
            kp_l.append(kp); vp_l.append(vp); pp_l.append(pp)
            ks_l.append(k_s); vs_l.append(v_s); ps_l.append(p_s)
        else:
            o = l // 2
            wts = (w_in_odd[o], sgu_norm[o], w_spatial[o], b_spatial[o], w_out_odd[o])
            yp, _ = odd_mixer(hp, *wts)
            ys, sv = odd_mixer(hs, *wts)
            sv_l.append(sv)
        xp, xs = xp + yp, xs + ys
        xp, xs = macaron_half(xp, *fb), macaron_half(xs, *fb)
    y_prompt = rmsnorm(xp, final_gain)
    y_sample = rmsnorm(xs, final_gain)
    new_win_k_prompt = jnp.stack(kp_l)
    new_win_v_prompt = jnp.stack(vp_l)
    new_pool_prompt = jnp.stack(pp_l)
    new_win_k_sample = jnp.stack(ks_l)
    new_win_v_sample = jnp.stack(vs_l)
    new_pool_sample = jnp.stack(ps_l)
    new_sgu_v_sample = jnp.stack(sv_l)
    return (y_prompt, y_sample, new_win_k_prompt, new_win_v_prompt, new_pool_prompt,
            new_win_k_sample, new_win_v_sample, new_pool_sample, new_sgu_v_sample)
```

```python
import os
import numpy as np
from contextlib import ExitStack
import concourse.bass as bass
import concourse.mybir as mybir
from concourse.bass_utils import run_bass_kernel_spmd

F32 = mybir.dt.float32
BF16 = mybir.dt.bfloat16
AF = mybir.ActivationFunctionType
ALU = mybir.AluOpType

NCORES = 8
D = 1024
DT = 8
FF = 2816
FT = 22
SEQ = 2048
NS = 64
NB_S = 16
NTOK = SEQ + NS
TTS = [(0, 512), (512, 512), (1024, 512), (1536, 512), (2048, 64)]
BLKS = [(b * 128, 128) for b in range(16)] + [(2048, 64)]
EPS = 1e-6
STAGE = int(os.environ.get("MK_STAGE", "99"))
SUB = float(os.environ.get("MK_SUB", "99"))

ENGS = ("pe", "act", "dve", "pool", "sp")


class Res:
    __slots__ = ("name", "lw", "rd")

    def __init__(self, name=""):
        self.name = name
        self.lw = None
        self.rd = []


class DmaSem:
    __slots__ = ("name", "cnt", "h", "q")

    def __init__(self, name):
        self.name = name
        self.cnt = 0
        self.h = None
        self.q = None


class Sched:
    def __init__(self):
        self.q = {e: [] for e in ENGS}
        self.cnt = {e: 0 for e in ENGS}
        self.waited = {e: {} for e in ENGS}
        self.dsems = []

    def dsem(self, name):
        s = DmaSem(name)
        self.dsems.append(s)
        return s

    def _collect(self, eng, reads, writes, is_dma=False):
        need = {}
        for r in reads:
            if r.lw is not None:
                k, v = r.lw
                if v > need.get(k, 0):
                    need[k] = v
        for w in writes:
            if w.lw is not None:
                k, v = w.lw
                if v > need.get(k, 0):
                    need[k] = v
            for (k, v) in w.rd:
                if v > need.get(k, 0):
                    need[k] = v
        out = []
        wd = self.waited[eng]
        for k, v in need.items():
            if k == "pe" and eng == "pe" and not is_dma:
                continue
            if wd.get(k, 0) >= v:
                continue
            wd[k] = v
            out.append((k, v))
        return out

    def op(self, eng, fn, reads=(), writes=(), signal=True):
        waits = self._collect(eng, reads, writes)
        if signal:
            self.cnt[eng] += 1
            val = self.cnt[eng]
        else:
            val = self.cnt[eng] + 1
        e = (eng, val)
        for r in reads:
            r.rd.append(e)
        for w in writes:
            w.lw = e
            w.rd = []
        self.q[eng].append((waits, fn, (eng, 1) if signal else None))

    def dma(self, qeng, fn, sem, reads=(), writes=()):
        waits = self._collect(qeng, reads, writes, is_dma=True)
        assert sem.q in (None, qeng)
        sem.q = qeng
        sem.cnt += 16
        e = (sem, sem.cnt)
        for r in reads:
            r.rd.append(e)
        for w in writes:
            w.lw = e
            w.rd = []
        self.q[qeng].append((waits, fn, (sem, 16)))

    def barrier(self, engs=("pe", "act", "dve", "sp")):
        ev = [(e, self.cnt[e]) for e in engs if e != "sp" and self.cnt[e] > 0]
        ev += [(s, s.cnt) for s in self.dsems if s.cnt > 0 and s.q in engs]
        for e in engs:
            wd = self.waited[e]
            waits = []
            for k, v in ev:
                if k == e:
                    continue
                if wd.get(k, 0) >= v:
                    continue
                wd[k] = v
                waits.append((k, v))
            if waits:
                self.q[e].append((waits, None, None))

    def emit(self, nc, stack):
        esem = {}
        for e in ("pe", "act", "dve", "pool"):
            esem[e] = stack.enter_context(nc.semaphore("s_" + e))
        for s in self.dsems:
            s.h = stack.enter_context(nc.semaphore("d_" + s.name))
        fin = [(s, s.cnt) for s in self.dsems if s.cnt > 0]
        for e in ("pe", "act", "dve", "pool"):
            if self.cnt[e] > 0:
                fin.append((e, self.cnt[e]))

        def hof(k):
            return esem[k] if isinstance(k, str) else k.h

        def run(eng_name, eng):
            for waits, fn, sig in self.q[eng_name]:
                for k, v in waits:
                    eng.wait_ge(hof(k), v)
                if fn is None:
                    continue
                inst = fn(eng)
                if sig is not None:
                    inst.then_inc(hof(sig[0]), sig[1])
            if eng_name == "sp":
                for k, v in fin:
                    eng.wait_ge(hof(k), v)

        block = stack.enter_context(nc.Block())

        @block.tensor
        def _(e):
            run("pe", e)

        @block.scalar
        def _(e):
            run("act", e)

        @block.vector
        def _(e):
            run("dve", e)

        @block.gpsimd
        def _(e):
            run("pool", e)

        @block.sync
        def _(e):
            run("sp", e)


def t5_bucket_np(n):
    n = np.maximum(n, 0)
    max_exact = 16
    nf = np.maximum(n, 1).astype(np.float32)
    large = max_exact + (np.log(nf / np.float32(max_exact)) / np.float32(np.log(128 / max_exact))
                         * np.float32(32 - max_exact)).astype(np.int32)
    large = np.minimum(large, 31)
    return np.where(n < max_exact, n, large)


def host_consts():
    c = {}
    c["ident"] = np.eye(128, dtype=np.float32)
    oh = np.zeros((32, 384), np.float32)
    n = np.arange(384) - 128
    valid = (n >= 0) & (n < 128)
    bk = t5_bucket_np(n)
    oh[bk[valid], np.arange(384)[valid]] = 1.0
    c["onehot"] = oh
    c["trimask"] = np.triu(np.ones((128, 128), np.float32))
    m = np.zeros((64, 64), np.float32)
    for b in range(16):
        for s in range(4):
            for t in range(s, 4):
                m[4 * b + s, 4 * b + t] = 1.0
    c["bdmask"] = m
    ic = np.zeros((128, 4, 16), np.float32)
    for g, w in enumerate((2, 4, 8, 16)):
        ic[:, g, :] = 1.0 / np.minimum(np.arange(16) + 1, w)
    c["invc"] = ic
    sel = np.zeros((4, 64), np.float32)
    for b in range(16):
        for t in range(4):
            sel[t, 4 * b + t] = 1.0
    c["sel4"] = sel
    A = np.zeros((128, 4, 128), np.float32)
    B = np.zeros((128, 4, 128), np.float32)
    A0 = np.zeros((128, 4, 128), np.float32)
    for g, w in enumerate((2, 4, 8, 16)):
        for t in range(128):
            for sx in range(t - w + 1, t + 1):
                if sx >= 0:
                    A[sx, g, t] += 1.0 / w
                    A0[sx, g, t] += 1.0 / min(t + 1, w)
                else:
                    B[128 + sx, g, t] += 1.0 / w
            A[t, g, t] -= 1.0
            A0[t, g, t] -= 1.0
    c["poolA"], c["poolB"], c["poolA0"] = A, B, A0
    return c


CONST_SHAPES = {"ident": [128, 128], "onehot": [32, 384], "trimask": [128, 128],
                "bdmask": [64, 64], "invc": [128, 4, 16], "sel4": [4, 64],
                "poolA": [128, 4, 128], "poolB": [128, 4, 128], "poolA0": [128, 4, 128]}

W_SPECS = [("rel_bias", [32, 8]), ("norm_gains", [2, 3, 1024]), ("final_gain", [1024]),
           ("ffn_gate", [2, 2, 1024, 2816]), ("ffn_up", [2, 2, 1024, 2816]), ("ffn_down", [2, 2, 2816, 1024]),
           ("w_in_even", [1, 1024, 1280]), ("w_out_even", [1, 1024, 1024]), ("attn_sinks", [1, 8]),
           ("w_pool", [1, 4, 128, 128]), ("pool_scale", [1, 512]), ("w_in_odd", [1, 1024, 2048]),
           ("sgu_norm", [1, 1024]), ("w_spatial", [1, 4, 128, 128]), ("b_spatial", [1, 4, 128]),
           ("w_out_odd", [1, 1024, 1024])]


def build_program():
    nc = bass.Bass("TRN2", target_bir_lowering=False)
    din = {}

    def inp(name, shape):
        din[name] = nc.dram_tensor(name, list(shape), F32, kind="ExternalInput").ap()

    inp("x_prompt", [SEQ, D])
    inp("x_sample", [NS, D])
    inp("state_win_k", [NB_S, 128, 128])
    inp("state_win_v", [NB_S, 128, 128])
    inp("state_pool", [NB_S, 15, 512])
    for n_, s_ in W_SPECS:
        inp(n_, s_)
    for n_, s_ in CONST_SHAPES.items():
        inp("c_" + n_, s_)

    def outp(name, shape):
        return nc.dram_tensor(name, list(shape), F32, kind="ExternalOutput").ap()

    o_yp = outp("y_prompt", [SEQ, D])
    o_ys = outp("y_sample", [NS, D])
    o_kp = outp("nk_p", [128, 128])
    o_vp = outp("nv_p", [128, 128])
    o_pp = outp("np_p", [15, 512])
    o_ks = outp("nk_s", [NB_S, 128, 128])
    o_vs = outp("nv_s", [NB_S, 128, 128])
    o_ps = outp("np_s", [NB_S, 15, 512])
    o_sv = outp("sgu_v", [NS, D])
    scr = nc.dram_tensor("scr_bias", [8, 49664], F32, kind="Internal").ap()

    S = Sched()
    st = ExitStack()
    with st:
        sb = lambda name, shape, dt: st.enter_context(nc.sbuf_tensor(name, shape, dt))
        xT = sb("xT", [128, DT, NTOK], F32)
        hT = sb("hT", [128, DT, NTOK], BF16)
        ARENA_E = 26624
        arena = sb("arena", [128, ARENA_E], BF16)
        NRING = 12
        ring = sb("ring", [128, NRING, 1024], BF16)
        tmpa = sb("tmpa", [128, 8, 512], F32)
        ident = sb("ident", [128, 128], F32)
        ones_f = sb("ones_f", [128, 128], F32)
        ones_b = sb("ones_b", [128, 128], BF16)
        cols = sb("cols", [128, 64], F32)
        epsc = sb("epsc", [128, 1], F32)
        EB = sb("EB", [128, 2, 8, 128], F32)
        psum = st.enter_context(nc.psum_tensor("psum", [128, 8, 512], F32))

        r_ps = [Res(f"ps{i}") for i in range(8)]
        r_x = [[Res(f"x{d}_{t}") for t in range(5)] for d in range(DT)]
        r_h = [[Res(f"h{d}_{t}") for t in range(5)] for d in range(DT)]
        r_tmp = [Res(f"tmp{i}") for i in range(8)]
        r_const = Res("const")
        r_cols = Res("cols")
        r_ring = [Res(f"ring{i}") for i in range(NRING)]
        s_ring = [S.dsem(f"ring{i}") for i in range(NRING)]
        ring_pos = [0]

        def ps(b):
            return psum[:, b, :]

        def ld(qe, out, in_, sem, writes=(), reads=()):
            S.dma(qe, lambda e: e.dma_start(out=out, in_=in_), sem, reads=reads, writes=writes)

        def wunit(pairs):
            i = ring_pos[0] % NRING
            ring_pos[0] += 1
            for (dst_fn, src) in pairs:
                ld("pool", dst_fn(ring[:, i, :]), src, s_ring[i], writes=[r_ring[i]])
            return i

        def mm(out, lhsT, rhs, start, stop, reads, writes, signal=None):
            if signal is None:
                signal = stop
            S.op("pe", lambda e: e.matmul(out, lhsT=lhsT, rhs=rhs, start=start, stop=stop),
                 reads=reads, writes=writes, signal=signal)

        s_c = S.dsem("consts")
        ld("sp", ident[:], din["c_ident"], s_c, writes=[r_const])
        S.op("dve", lambda e: e.memset(ones_f[:], 1.0), writes=[r_const])
        S.op("dve", lambda e: e.memset(ones_b[:], 1.0), writes=[r_const])
        S.op("dve", lambda e: e.memset(epsc[:], EPS), writes=[r_const])
        prow = tmpa[:, 0, :].bitcast(F32)[0:52, 0:128]
        s_p = S.dsem("prow")
        ld("sp", prow[0:48, :], din["norm_gains"].rearrange("l i (dt p) -> (l i dt) p", p=128), s_p, writes=[r_tmp[0]])
        ld("sp", prow[48:52, :], din["pool_scale"].rearrange("o (g p) -> (o g) p", p=128), s_p, writes=[r_tmp[0]])
        S.op("pe", lambda e: e.transpose(out=psum[:, 7, 0:52], in_=prow, identity=ident[0:52, 0:52]),
             reads=[r_tmp[0], r_const], writes=[r_ps[7]])
        S.op("dve", lambda e: e.tensor_copy(out=cols[:, 0:52], in_=psum[:, 7, 0:52]), reads=[r_ps[7]], writes=[r_cols])

        def gain_col(l, i, d):
            j = (l * 3 + i) * 8 + d
            return cols[:, j:j + 1]

        NXS = 8
        xs = arena[:, 0:2048 * NXS].bitcast(F32).rearrange("p (a f) -> p a f", a=NXS)
        r_xs = [Res(f"xs{i}") for i in range(NXS)]
        s_xs = [S.dsem(f"xs{i}") for i in range(NXS)]

        def load_x(norm=None, pre_cb=None):
            for bi, (t0, n) in enumerate(BLKS):
                sl = bi % NXS
                src = din["x_prompt"][t0:t0 + n, :] if bi < 16 else din["x_sample"][:, :]
                ld("sp", xs[0:n, sl, :], src, s_xs[sl], writes=[r_xs[sl]])
                tt = min(t0 // 512, 4)
                for half in range(2):
                    pb = 4 + half
                    for j in range(4):
                        d = half * 4 + j
                        S.op("pe", lambda e, pb=pb, j=j, d=d, n=n, sl=sl: e.transpose(
                            out=psum[:, pb, j * 128:j * 128 + n], in_=xs[0:n, sl, d * 128:(d + 1) * 128],
                            identity=ident[0:n, 0:n]),
                            reads=[r_xs[sl], r_const], writes=[r_ps[pb]], signal=(j == 3))
                    dst = xT[:, half * 4:(half + 1) * 4, t0:t0 + n]
                    srcp = psum[:, pb, :].rearrange("p (j t) -> p j t", j=4)[:, :, 0:n]
                    wr = [r_x[half * 4 + j][tt] for j in range(4)]
                    if half == 0:
                        S.op("dve", lambda e, dst=dst, srcp=srcp: e.tensor_copy(out=dst, in_=srcp), reads=[r_ps[pb]], writes=wr)
                    else:
                        S.op("act", lambda e, dst=dst, srcp=srcp: e.copy(out=dst, in_=srcp), reads=[r_ps[pb]], writes=wr)
                if bi == 3 and pre_cb is not None:
                    pre_cb()
                if norm is not None and (bi % 4 == 3 or bi == 16):
                    norm_stats(tt)
                    if tt >= 1:
                        norm_apply(norm[0], norm[1], tt - 1)
            if norm is not None:
                norm_apply(norm[0], norm[1], 4)
                normed[0] = tuple(norm)

        sq_bf = tmpa[:, 0:2, :].bitcast(BF16)

        def norm_stats(tt):
            t0, n = TTS[tt]
            pb = 6 + (tt % 2)
            for d in range(DT):
                ti = d % 2
                sq = sq_bf[:, ti, 0:n]
                S.op("act", lambda e, sq=sq, d=d: e.activation(out=sq, in_=xT[:, d, t0:t0 + n], func=AF.Square),
                     reads=[r_x[d][tt]], writes=[r_tmp[ti]])
                mm(psum[:, pb, 0:n], ones_b[:], sq, d == 0, d == DT - 1, [r_tmp[ti], r_const], [r_ps[pb]], signal=True)
            ri = 2 + (tt % 2)
            sd = tmpa[:, ri, 0:n]
            S.op("act", lambda e: e.activation(out=sd, in_=psum[:, pb, 0:n], func=AF.Ln, bias=epsc[:], scale=1.0 / D),
                 reads=[r_ps[pb], r_const], writes=[r_tmp[ri]])
            S.op("act", lambda e: e.activation(out=sd, in_=sd, func=AF.Exp, scale=-0.5), reads=[r_tmp[ri]], writes=[r_tmp[ri]])

        def norm_apply(l, i, tt):
            t0, n = TTS[tt]
            ri = 2 + (tt % 2)
            sd = tmpa[:, ri, 0:n]
            for d in range(DT):
                S.op("dve", lambda e, d=d: e.scalar_tensor_tensor(
                    out=hT[:, d, t0:t0 + n], in0=xT[:, d, t0:t0 + n], scalar=gain_col(l, i, d), in1=sd,
                    op0=ALU.mult, op1=ALU.mult),
                    reads=[r_x[d][tt], r_tmp[ri], r_cols], writes=[r_h[d][tt]])

        normed = [None]

        def norm_to_h(l, i):
            if normed[0] == (l, i):
                normed[0] = None
                return
            assert normed[0] is None
            for tt in range(5):
                norm_stats(tt)
                norm_apply(l, i, tt)

        def fused_tail(emit_tile, nxt, extra=None):
            for tt in range(5):
                emit_tile(tt)
                if nxt is not None:
                    if tt >= 1:
                        norm_stats(tt - 1)
                    if tt >= 2:
                        norm_apply(nxt[0], nxt[1], tt - 2)
                if extra is not None and tt >= 1:
                    extra(tt - 1)
            if nxt is not None:
                norm_stats(4)
                norm_apply(nxt[0], nxt[1], 3)
                norm_apply(nxt[0], nxt[1], 4)
                normed[0] = tuple(nxt)
            if extra is not None:
                extra(4)

        CH = 5
        hid = arena[:, 0:2 * CH * NTOK].rearrange("p (b c t) -> p b c t", b=2, c=CH)
        r_hid = [[[Res() for _ in range(5)] for _ in range(CH)] for _ in range(2)]

        def ffn(l, j, nxt=None, mid_cb=None, after_norm=None, extra=None):
            wg, wu, wd = din["ffn_gate"][l, j], din["ffn_up"][l, j], din["ffn_down"][l, j]
            norm_to_h(l, 0 if j == 0 else 2)
            if after_norm is not None:
                after_norm()
            chunks = [[0, 1, 2, 3], [4, 5, 6, 7], [8, 9, 10, 11], [12, 13, 14, 15, 16], [17, 18, 19, 20, 21]]

            def gu(ci):
                hb = ci % 2
                for fi, ft in enumerate(chunks[ci]):
                    ug = wunit([(lambda r: r.rearrange("p (k f) -> p k f", k=8),
                                 wg[:, ft * 128:(ft + 1) * 128].rearrange("(k p) f -> p k f", p=128))])
                    uu = wunit([(lambda r: r.rearrange("p (k f) -> p k f", k=8),
                                 wu[:, ft * 128:(ft + 1) * 128].rearrange("(k p) f -> p k f", p=128))])
                    for tt, (t0, n) in enumerate(TTS):
                        par = (fi * 5 + tt) % 2
                        pg, pu = 2 * par, 2 * par + 1
                        for (un, pb) in ((ug, pg), (uu, pu)):
                            for k in range(DT):
                                mm(psum[:, pb, 0:n], ring[:, un, k * 128:(k + 1) * 128], hT[:, k, t0:t0 + n],
                                   k == 0, k == DT - 1, [r_ring[un], r_h[k][tt]], [r_ps[pb]])
                        ti = 4 + par
                        sg = tmpa[:, ti, 0:n]
                        S.op("act", lambda e, sg=sg, pg=pg, n=n: e.activation(out=sg, in_=psum[:, pg, 0:n], func=AF.Silu),
                             reads=[r_ps[pg]], writes=[r_tmp[ti]])
                        S.op("dve", lambda e, sg=sg, pu=pu, n=n, hb=hb, fi=fi, t0=t0: e.tensor_tensor(
                            out=hid[:, hb, fi, t0:t0 + n], in0=sg, in1=psum[:, pu, 0:n], op=ALU.mult),
                            reads=[r_tmp[ti], r_ps[pu]], writes=[r_hid[hb][fi][tt]])

            def down(ci, last=False):
                hb = ci % 2
                fts = chunks[ci]
                us = [wunit([(lambda r: r, wd[ft * 128:(ft + 1) * 128, :])]) for ft in fts]
                cnt = [0]

                dbanks = (4, 5, 0, 1) if last else (4, 5, 6, 7)

                def group(d, tt):
                    t0, n = TTS[tt]
                    pb = dbanks[cnt[0] % 4]
                    cnt[0] += 1
                    for fi in range(len(fts)):
                        mm(psum[:, pb, 0:n], ring[:, us[fi], d * 128:(d + 1) * 128], hid[:, hb, fi, t0:t0 + n],
                           fi == 0, fi == len(fts) - 1, [r_ring[us[fi]], r_hid[hb][fi][tt]], [r_ps[pb]])
                    S.op("dve", lambda e: e.scalar_tensor_tensor(
                        out=xT[:, d, t0:t0 + n], in0=psum[:, pb, 0:n], scalar=0.5, in1=xT[:, d, t0:t0 + n],
                        op0=ALU.mult, op1=ALU.add),
                        reads=[r_ps[pb], r_x[d][tt]], writes=[r_x[d][tt]])

                if not last:
                    for d in range(DT):
                        for tt in range(5):
                            group(d, tt)
                else:
                    fused_tail(lambda tt: [group(d, tt) for d in range(DT)], nxt, extra)

            for ci in range(len(chunks)):
                gu(ci)
                if ci > 0:
                    down(ci - 1)
                if ci == 2 and mid_cb is not None:
                    mid_cb()
            down(len(chunks) - 1, last=True)

        def A_bf(off, n):
            return arena[:, off:off + n]

        def A_f32(off, n_f32):
            return arena[:, off:off + 2 * n_f32].bitcast(F32)

        tmp_bf = tmpa[:, 4:6, :].bitcast(BF16).rearrange("p a (b f) -> p (a b) f", b=2)

        def evac_add_x(pb, d, tt, t0, n):
            S.op("dve", lambda e: e.tensor_tensor(out=xT[:, d, t0:t0 + n], in0=psum[:, pb, 0:n], in1=xT[:, d, t0:t0 + n], op=ALU.add),
                 reads=[r_ps[pb], r_x[d][tt]], writes=[r_x[d][tt]])

        def kunit(w2d, c0, ncol=128):
            return wunit([(lambda r: r.rearrange("p (k f) -> p k f", k=8)[:, :, 0:ncol],
                           w2d[:, c0:c0 + ncol].rearrange("(k p) f -> p k f", p=128))])

        def ru(i):
            return ring[:, i, :].rearrange("p (k f) -> p k f", k=8)

        es_bc = sb("es_bc", [128, 4, 128], F32)
        tmpb = sb("tmpb", [128, 2, 512], F32)
        r_tmpb = [Res("tmpb0"), Res("tmpb1")]
        r_EB = Res("EB")
        r_es = Res("es")

        prep_state = {}

        def prep_bias():
            rb = tmpa[0:32, 4, 0:8]
            oh = tmpa[0:32, 5, 0:384]
            Lh = tmpa[0:32, 6:8, :].rearrange("p a f -> p (a f)").rearrange("p (h m) -> p h m", h=8)
            ers = [arena[:, 21120:21888].bitcast(F32), arena[:, 21888:22656].bitcast(F32)]
            r_er = [Res("er0"), Res("er1")]
            s_b = S.dsem("biasld")
            ld("sp", rb, din["rel_bias"], s_b, writes=[r_tmp[4]])
            ld("sp", oh, din["c_onehot"], S.dsem("ohld"), writes=[r_tmp[5]])
            S.op("act", lambda e: e.activation(out=rb, in_=rb, func=AF.Exp), reads=[r_tmp[4]], writes=[r_tmp[4]])
            S.op("dve", lambda e: e.tensor_copy(out=Lh, in_=rb.unsqueeze(2).broadcast_to([32, 8, 128])),
                 reads=[r_tmp[4]], writes=[r_tmp[6], r_tmp[7]])
            s_scr = S.dsem("scrw")
            r_scr = Res("scr")
            for h in range(8):
                pb = 2 + (h % 2)
                ti = h % 2
                mm(psum[:, pb, 0:384], Lh[:, h, :], oh, True, True, [r_tmp[5], r_tmp[6], r_tmp[7]], [r_ps[pb]])
                er = ers[ti]
                S.op("dve", lambda e, er=er, pb=pb: e.tensor_copy(out=er, in_=psum[:, pb, 0:384]), reads=[r_ps[pb]], writes=[r_er[ti]])
                ld("pool", scr[h, 0:128 * 384].rearrange("(k i) -> k i", i=384), er, s_scr, reads=[r_er[ti]], writes=[r_scr])
            prep_state["r_scr"] = r_scr

        def prep_bias_b():
            r_scr = prep_state["r_scr"]
            s_eb = S.dsem("ebld")
            for kb, off in ((0, 128), (1, 256)):
                src = scr[:, off:off + 128 * 383].rearrange("h (k i) -> k h i", i=383)[:, :, 0:128]
                ld("pool", EB[:, kb, :, :], src, s_eb, reads=[r_scr], writes=[r_EB])
            es8 = cols[:, 56:60]
            s_es = S.dsem("esld")
            for g in range(2):
                ld("sp", es8[g * 64:(g + 1) * 64, :], din["attn_sinks"][0, 4 * g:4 * g + 4].partition_broadcast(64), s_es, writes=[r_es])
            S.op("act", lambda e: e.activation(out=es8, in_=es8, func=AF.Exp), reads=[r_es], writes=[r_es])
            S.op("dve", lambda e: e.tensor_copy(out=es_bc[:], in_=es8.unsqueeze(2).broadcast_to([128, 4, 128])),
                 reads=[r_es], writes=[r_es])

        def odd_mixer(l, nxt=None):
            o = l // 2
            S.barrier()
            norm_to_h(l, 1)
            w_in, w_out = din["w_in_odd"][o], din["w_out_odd"][o]
            uT = A_bf(0, 8 * NTOK).rearrange("p (c t) -> p c t", c=8)
            vt32 = A_f32(16896, 2048).rearrange("p (a f) -> p a f", a=2)
            vn = A_bf(20992, 2048).rearrange("p (a f) -> p a f", a=2)
            wsT = A_bf(23040, 512).rearrange("p (g t) -> p g t", g=4)
            wsbd = A_bf(23552, 256).rearrange("p (g t) -> p g t", g=4)
            sgn = A_f32(23808, 1024)
            trim = A_f32(25856, 128)
            bdm = A_f32(26112, 64)
            sel4f = A_f32(26240, 64)
            sel4b = A_bf(26368, 64)
            bsp = tmpa[:, 6, :].rearrange("p (g t) -> p g t", g=4)
            wst = tmpa[:, 7, :].rearrange("p (g t) -> p g t", g=4)
            r_u = [[Res() for _ in range(5)] for _ in range(8)]
            r_vt = [Res(), Res()]
            r_vn = [Res(), Res()]
            r_c = Res("oddc")
            r_ws = Res("wsT")
            s_c2 = S.dsem("oddc")
            ld("sp", wst, w_sp_ap(o), S.dsem("wst"), writes=[r_tmp[7]])
            ld("sp", bsp, din["b_spatial"][o].partition_broadcast(128), S.dsem("bsp"), writes=[r_tmp[6]])
            ld("sp", sgn, din["sgu_norm"][o].partition_broadcast(128), s_c2, writes=[r_c])
            ld("sp", trim, din["c_trimask"], s_c2, writes=[r_c])
            ld("sp", bdm[0:64, :], din["c_bdmask"], s_c2, writes=[r_c])
            ld("sp", sel4f[0:4, :], din["c_sel4"], s_c2, writes=[r_c])
            cnt = 0
            for ft in range(8):
                un = kunit(w_in, ft * 128)
                for tt, (t0, n) in enumerate(TTS):
                    pb = cnt % 4
                    cnt += 1
                    for k in range(DT):
                        mm(psum[:, pb, 0:n], ru(un)[:, k, :], hT[:, k, t0:t0 + n], k == 0, k == DT - 1,
                           [r_ring[un], r_h[k][tt]], [r_ps[pb]])
                    S.op("act", lambda e, pb=pb, ft=ft, t0=t0, n=n: e.activation(out=uT[:, ft, t0:t0 + n], in_=psum[:, pb, 0:n],
                                                                                func=AF.Gelu_apprx_tanh),
                         reads=[r_ps[pb]], writes=[r_u[ft][tt]])
            S.op("dve", lambda e: e.tensor_copy(out=sel4b[0:4, :], in_=sel4f[0:4, :]), reads=[r_c], writes=[r_c])
            for g in range(4):
                S.op("pe", lambda e, g=g: e.transpose(out=psum[:, 6, g * 128:(g + 1) * 128], in_=wst[:, g, :], identity=ident[:]),
                     reads=[r_tmp[7], r_const], writes=[r_ps[6]], signal=(g == 3))
            S.op("dve", lambda e: e.tensor_tensor(out=wsT, in0=psum[:, 6, :].rearrange("p (g t) -> p g t", g=4),
                                                  in1=trim.unsqueeze(1).broadcast_to([128, 4, 128]), op=ALU.mult),
                 reads=[r_ps[6], r_c], writes=[r_ws])
            for g in range(4):
                mm(psum[0:64, 7, g * 64:(g + 1) * 64], sel4b[0:4, :],
                   wsT[0:4, g, 0:4].unsqueeze(1).broadcast_to([4, 16, 4]), True, True, [r_ws, r_c], [r_ps[7]], signal=(g == 3))
            S.op("dve", lambda e: e.tensor_tensor(out=wsbd[0:64, :, :], in0=psum[0:64, 7, 0:256].rearrange("p (g t) -> p g t", g=4),
                                                  in1=bdm[0:64, :].unsqueeze(1).broadcast_to([64, 4, 64]), op=ALU.mult),
                 reads=[r_ps[7], r_c], writes=[r_ws])
            uv = [wunit([(lambda r: r, w_in[k * 128:(k + 1) * 128, 1024:2048])]) for k in range(8)]
            junk = tmpa[:, 4, :].rearrange("p (a f) -> p a f", a=1)
            junk2 = tmpa[:, 4:6, :]
            r_ssq = [Res(), Res()]
            r_mix = [[Res(), Res()], [Res(), Res()]]
            s_sv = S.dsem("sv")
            def v_stage1(bi):
                t0, n = BLKS[bi]
                sl = bi % 2
                tt = min(t0 // 512, 4)
                vb = (0, 1) if sl == 0 else (4, 5)
                sbk = (2, 3) if sl == 0 else (6, 7)
                ssq = cols[:, 60 + 2 * sl:62 + 2 * sl]
                for half in range(2):
                    pb = vb[half]
                    for k in range(8):
                        mm(psum[0:n, pb, :], hT[:, k, t0:t0 + n], ring[:, uv[k], half * 512:(half + 1) * 512],
                           k == 0, k == 7, [r_ring[uv[k]], r_h[k][tt]], [r_ps[pb]])
                    S.op("act", lambda e, pb=pb, n=n, sl=sl, half=half: e.activation(
                        out=vt32[0:n, sl, half * 512:(half + 1) * 512], in_=psum[0:n, pb, :], func=AF.Gelu_apprx_tanh),
                        reads=[r_ps[pb]], writes=[r_vt[sl]])
                S.op("act", lambda e, n=n, sl=sl, ssq=ssq: e.activation(
                    out=junk2[0:n, :, :], in_=vt32[0:n, sl, :].rearrange("p (a f) -> p a f", a=2), func=AF.Square,
                    accum_out=ssq[0:n, 0:1]),
                    reads=[r_vt[sl]], writes=[r_ssq[sl]])
                S.op("act", lambda e, n=n, ssq=ssq: e.activation(out=ssq[0:n, 1:2], in_=ssq[0:n, 0:1], func=AF.Sqrt, bias=epsc[0:n, :], scale=1.0 / 1024),
                     reads=[r_ssq[sl], r_const], writes=[r_ssq[sl]])
                S.op("dve", lambda e, n=n, ssq=ssq: e.reciprocal(out=ssq[0:n, 1:2], in_=ssq[0:n, 1:2]), reads=[r_ssq[sl]], writes=[r_ssq[sl]])
                if bi < 16:
                    S.op("dve", lambda e, n=n, sl=sl, ssq=ssq: e.scalar_tensor_tensor(
                        out=vn[0:n, sl, :], in0=vt32[0:n, sl, :], scalar=ssq[0:n, 1:2], in1=sgn[0:n, :], op0=ALU.mult, op1=ALU.mult),
                        reads=[r_vt[sl], r_ssq[sl], r_c], writes=[r_vn[sl]])
                else:
                    S.op("dve", lambda e, n=n, sl=sl, ssq=ssq: e.scalar_tensor_tensor(
                        out=vt32[0:n, sl, :], in0=vt32[0:n, sl, :], scalar=ssq[0:n, 1:2], in1=sgn[0:n, :], op0=ALU.mult, op1=ALU.mult),
                        reads=[r_vt[sl], r_ssq[sl], r_c], writes=[r_vt[sl]])
                    S.op("dve", lambda e, n=n, sl=sl: e.tensor_copy(out=vn[0:n, sl, :], in_=vt32[0:n, sl, :]),
                         reads=[r_vt[sl]], writes=[r_vn[sl]])
                    ld("sp", o_sv[:, :], vt32[0:n, sl, :], s_sv, reads=[r_vt[sl]])

            def v_stage2(bi):
                t0, n = BLKS[bi]
                sl = bi % 2
                tt = min(t0 // 512, 4)
                vb = (0, 1) if sl == 0 else (4, 5)
                sbk = (2, 3) if sl == 0 else (6, 7)
                for half in range(2):
                    pb = sbk[half]
                    for c4 in range(4):
                        ct = half * 4 + c4
                        g = ct // 2
                        rhs = wsT[0:n, g, 0:n] if bi < 16 else wsbd[0:n, g, 0:n]
                        mm(psum[:, pb, c4 * 128:c4 * 128 + n], vn[0:n, sl, ct * 128:(ct + 1) * 128], rhs, True, True,
                           [r_vn[sl], r_ws], [r_ps[pb]], signal=(c4 == 3))
                    ti = 2 * sl + half
                    rm = r_tmp[ti]
                    mix = tmpa[:, ti, :].rearrange("p (c t) -> p c t", c=4)
                    pv = psum[:, pb, :].rearrange("p (c t) -> p c t", c=4)
                    for gg in range(2):
                        g = half * 2 + gg
                        if bi < 16:
                            in1 = bsp[:, g:g + 1, 0:n].broadcast_to([128, 2, n])
                            S.op("dve", lambda e, mix=mix, pv=pv, gg=gg, n=n, in1=in1: e.tensor_tensor(
                                out=mix[:, 2 * gg:2 * gg + 2, 0:n], in0=pv[:, 2 * gg:2 * gg + 2, 0:n], in1=in1, op=ALU.add),
                                reads=[r_ps[pb], r_tmp[6]], writes=[rm])
                        else:
                            in1 = bsp[:, g, 0:4].unsqueeze(1).unsqueeze(1).broadcast_to([128, 2, 16, 4])
                            S.op("dve", lambda e, mix=mix, pv=pv, gg=gg, n=n, in1=in1: e.tensor_tensor(
                                out=mix[:, 2 * gg:2 * gg + 2, 0:n].rearrange("p c (b t) -> p c b t", t=4),
                                in0=pv[:, 2 * gg:2 * gg + 2, 0:n].rearrange("p c (b t) -> p c b t", t=4), in1=in1, op=ALU.add),
                                reads=[r_ps[pb], r_tmp[6]], writes=[rm])
                    S.op("dve", lambda e, mix=mix, half=half, t0=t0, n=n: e.tensor_tensor(
                        out=uT[:, half * 4:(half + 1) * 4, t0:t0 + n], in0=mix[:, :, 0:n], in1=uT[:, half * 4:(half + 1) * 4, t0:t0 + n],
                        op=ALU.mult),
                        reads=[rm] + [r_u[half * 4 + c][tt] for c in range(4)],
                        writes=[r_u[half * 4 + c][tt] for c in range(4)])

            v_stage1(0)
            for bi in range(len(BLKS)):
                if bi + 1 < len(BLKS):
                    v_stage1(bi + 1)
                v_stage2(bi)
            ous = [kunit(w_out, d * 128) for d in range(DT)]
            ocnt = [0]

            def oproj_tile(tt):
                t0, n = TTS[tt]
                for d in range(DT):
                    pb = (4, 5, 0, 1)[ocnt[0] % 4]
                    ocnt[0] += 1
                    for k in range(8):
                        mm(psum[:, pb, 0:n], ru(ous[d])[:, k, :], uT[:, k, t0:t0 + n], k == 0, k == 7, [r_ring[ous[d]], r_u[k][tt]], [r_ps[pb]])
                    evac_add_x(pb, d, tt, t0, n)

            fused_tail(oproj_tile, nxt)
            S.barrier()

        def w_sp_ap(o):
            return din["w_spatial"][o].rearrange("g t s -> t g s")

        def even_mixer(l, nxt=None):
            ev = l // 2
            S.barrier()
            norm_to_h(l, 1)
            w_in, w_out = din["w_in_even"][ev], din["w_out_even"][ev]
            if SUB < 2:
                return
            dT = A_bf(0, 4 * NTOK).rearrange("p (g t) -> p g t", g=4)
            utb = A_bf(8448, 3 * 512).rearrange("p (a c) -> p a c", a=3)
            ut32 = A_f32(9984, 512)
            poolA = A_bf(11008, 512).rearrange("p (g t) -> p g t", g=4)
            poolB = A_bf(11520, 512).rearrange("p (g t) -> p g t", g=4)
            poolA0 = A_f32(12032, 512).rearrange("p (g t) -> p g t", g=4)
            Us = A_f32(14784, 4 * 16 * 19).rearrange("p (g b t) -> p g b t", g=4, b=16)
            Ws1 = A_f32(17216, 304).rearrange("p (b t) -> p b t", b=16)
            Ws2 = A_f32(17824, 304).rearrange("p (b t) -> p b t", b=16)
            invc = A_f32(18432, 64).rearrange("p (g t) -> p g t", g=4)
            wpool = A_bf(25280, 512).rearrange("p (g d) -> p g d", g=4)
            ctxs = A_f32(19072, 1024).rearrange("p (a f) -> p a f", a=2)
            uptok = A_f32(19072, 1024).rearrange("p (a f) -> p a f", a=2)
            t16 = A_f32(18560, 16)
            r_dT = [[Res() for _ in range(5)] for _ in range(4)]
            r_utb = [Res() for _ in range(3)]
            r_ut32 = Res("ut32")
            r_pm = Res("poolmats")
            r_Us = [Res() for _ in range(4)]
            r_Ws = [Res(), Res()]
            r_pc = Res("poolc")
            r_ctx = [Res(), Res()]
            r_upt = Res("uptok")
            s_pc = S.dsem("poolc")
            s_ctx = [S.dsem("ctx0"), S.dsem("ctx1")]
            ld("sp", poolA0, din["c_poolA0"], s_pc, writes=[r_pc])
            s_wp = S.dsem("wpool")
            r_wp = Res("wpool")
            ld("pool", wpool, din["w_pool"][ev].rearrange("g c d -> c g d"), s_wp, writes=[r_wp])
            s_o = S.dsem("outs")
            ld("sp", o_ps[:, 0:11, :], din["state_pool"][:, 4:15, :], s_o)
            kbuf = A_bf(21184, 2048).rearrange("p (b c) -> p b c", b=16)
            skst = tmpa[:, 6, :].rearrange("p (a c) -> p a c", a=4)
            r_kb = Res("kbuf")
            def ctx_prep():
                for hb in range(2):
                    for g in range(4):
                        S.op("pe", lambda e, g=g, hb=hb: e.transpose(out=psum[:, 6, g * 128:g * 128 + 120], in_=ctxs[0:120, hb, g * 128:(g + 1) * 128],
                                                                    identity=ident[0:120, 0:120]),
                             reads=[r_ctx[hb], r_const], writes=[r_ps[6]], signal=(g == 3))
                    S.op("dve", lambda e, hb=hb: e.tensor_copy(
                        out=Us[:, :, hb * 8:(hb + 1) * 8, 0:15],
                        in_=psum[:, 6, :].rearrange("p (g t) -> p g t", g=4)[:, :, 0:120].rearrange("p g (b r) -> p g b r", r=15)),
                        reads=[r_ps[6]], writes=r_Us)

            for hb in range(2):
                ld("sp", ctxs[0:120, hb, :], din["state_pool"][hb * 8:(hb + 1) * 8].rearrange("b r c -> (b r) c"), s_ctx[hb], writes=[r_ctx[hb]])
            s_sk = [S.dsem(f"sk{i}") for i in range(4)]
            r_sk = [Res() for _ in range(4)]

            def kbuf_prep(b4):
                for j in range(4):
                    b = b4 * 4 + j
                    ld("sp", skst[:, j, :], din["state_win_k"][b], s_sk[j], writes=[r_sk[j]])
                    S.op("pe", lambda e, j=j: e.transpose(out=psum[:, 7, j * 128:(j + 1) * 128], in_=skst[:, j, :], identity=ident[:]),
                         reads=[r_sk[j], r_const], writes=[r_ps[7]], signal=True)
                S.op("dve", lambda e: e.tensor_copy(out=kbuf[:, b4 * 4:(b4 + 1) * 4, :], in_=psum[:, 7, :].rearrange("p (a c) -> p a c", a=4)),
                     reads=[r_ps[7]], writes=[r_kb])

            if SUB < 2.2:
                return
            uns = [kunit(w_in, 768 + g * 128) for g in range(4)]
            urs = [wunit([(lambda r: r.rearrange("p (a c) -> p a c", a=2),
                           w_in[2 * i * 128:(2 * i + 2) * 128, 768:1280].rearrange("(a p) c -> p a c", p=128))]) for i in range(4)]
            s_pm = S.dsem("poolmats")
            allhid = [r for a_ in r_hid for b_ in a_ for r in b_]
            ld("pool", poolA, din["c_poolA"], s_pm, writes=[r_pm] + allhid)
            ld("pool", poolB, din["c_poolB"], s_pm, writes=[r_pm])
            pcnt = [0]

            def pool_proj(g, tt):
                t0, n = TTS[tt]
                pb = 6 + (pcnt[0] % 2)
                pcnt[0] += 1
                mm(psum[:, pb, 0:n], wpool[:, g, :], dT[:, g, t0:t0 + n], True, True, [r_wp, r_dT[g][tt]], [r_ps[pb]])
                S.op("act", lambda e: e.activation(out=dT[:, g, t0:t0 + n], in_=psum[:, pb, 0:n], func=AF.Identity,
                                                   scale=cols[:, 48 + g:49 + g]),
                     reads=[r_ps[pb], r_cols], writes=[r_dT[g][tt]])

            def pool_block_a(b):
                t0 = b * 128
                tt = b // 4
                bx = b % 2
                ui = b % 3
                for k in range(DT):
                    un = urs[k // 2]
                    mm(psum[:, bx, :], hT[:, k, t0:t0 + 128], ring[:, un, (k % 2) * 512:(k % 2 + 1) * 512], k == 0, k == DT - 1,
                       [r_ring[un], r_h[k][tt]], [r_ps[bx]])
                S.op("act", lambda e: e.copy(out=utb[:, ui, :], in_=psum[:, bx, :]), reads=[r_ps[bx]], writes=[r_utb[ui]])
                if b == 0:
                    S.op("act", lambda e: e.copy(out=ut32, in_=psum[:, bx, :]), reads=[r_ps[bx]], writes=[r_ut32])
                if b == 15:
                    S.op("act", lambda e: e.copy(out=uptok[:, 0, :], in_=psum[:, bx, :]), reads=[r_ps[bx]], writes=[r_upt])

            def pool_block_b(b):
                t0 = b * 128
                tt = b // 4
                by = 2 + (b % 2)
                ui = b % 3
                for g in range(4):
                    gc = slice(g * 128, (g + 1) * 128)
                    if b == 0:
                        mm(psum[:, by, gc], ut32[:, gc], poolA0[:, g, :], True, True, [r_ut32, r_pc], [r_ps[by]], signal=(g == 3))
                    else:
                        mm(psum[:, by, gc], utb[:, ui, gc], poolA[:, g, :], True, False, [r_utb[ui], r_pm], [r_ps[by]], signal=False)
                        mm(psum[:, by, gc], utb[:, (b - 1) % 3, gc], poolB[:, g, :], False, True, [r_utb[(b - 1) % 3], r_pm], [r_ps[by]],
                           signal=(g == 3))
                S.op("dve", lambda e: e.tensor_copy(out=dT[:, :, t0:t0 + 128], in_=psum[:, by, :].rearrange("p (g t) -> p g t", g=4)),
                     reads=[r_ps[by]], writes=[r_dT[g][tt] for g in range(4)])

            def pool_step(g, tt):
                w = 2 ** (g + 1)
                un = uns[g]
                t0, n = TTS[tt]
                pb = g
                for k in range(DT):
                    mm(psum[:, pb, 0:n], ru(un)[:, k, :], hT[:, k, t0:t0 + n], k == 0, k == DT - 1, [r_ring[un], r_h[k][tt]], [r_ps[pb]])
                Ug = Us[:, g, :, :]
                S.op("act", lambda e: e.copy(out=Ug[:, :, 15:19], in_=psum[:, pb, 0:64].rearrange("p (b t) -> p b t", t=4)),
                     reads=[r_ps[pb]], writes=[r_Us[g]])
                bufs = [Ws1, Ws2]
                src, rs = Ug, r_Us[g]
                sh = 1
                lo = 1
                for step in range(g + 1):
                    dst, rd = bufs[step % 2], r_Ws[step % 2]
                    S.op("dve", lambda e, dst=dst, src=src, lo=lo, sh=sh: e.tensor_tensor(
                        out=dst[:, :, lo:19], in0=src[:, :, lo:19], in1=src[:, :, lo - sh:19 - sh], op=ALU.add),
                        reads=[rs], writes=[rd])
                    src, rs = dst, rd
                    sh *= 2
                    lo += sh
                S.op("dve", lambda e, src=src: e.scalar_tensor_tensor(
                    out=dT[:, g, 2048:2112].rearrange("p (b t) -> p b t", t=4), in0=src[:, :, 15:19], scalar=1.0 / w,
                    in1=Ug[:, :, 15:19], op0=ALU.mult, op1=ALU.subtract),
                    reads=[rs, r_Us[g]], writes=[r_dT[g][4]])

            pool_block_a(0)
            for b in range(16):
                if b + 1 < 16:
                    pool_block_a(b + 1)
                pool_block_b(b)
                if b % 4 == 1 and b >= 5:
                    for g in range(4):
                        pool_proj(g, b // 4 - 1)
                if b % 4 == 2:
                    kbuf_prep(b // 4)
                if b == 11:
                    ctx_prep()
            for g in range(4):
                pool_step(g, 4)
            for g in range(4):
                pool_proj(g, 3)
            for g in range(4):
                for (row, t0, n, tt) in ((1, 2048, 64, 4),):
                    pb = 4 + row
                    for k in range(DT):
                        mm(psum[0:n, pb, g * 128:(g + 1) * 128], hT[:, k, t0:t0 + n], ru(uns[g])[:, k, :], k == 0, k == DT - 1,
                           [r_ring[uns[g]], r_h[k][tt]], [r_ps[pb]])
            for g in range(4):
                pool_proj(g, 4)
            for row, n in ((1, 64),):
                S.op("act", lambda e, row=row, n=n: e.copy(out=uptok[0:n, row, :], in_=psum[0:n, 4 + row, :]), reads=[r_ps[4 + row]], writes=[r_upt])
            ld("sp", o_pp[:, :], uptok[113:128, 0, :], s_o, reads=[r_upt])
            for b in range(16):
                ld("sp", o_ps[b, 11:15, :], uptok[4 * b:4 * b + 4, 1, :], s_o, reads=[r_upt])
            if SUB < 3:
                return
            q = A_bf(8448, 4 * NTOK).rearrange("p (h t) -> p h t", h=4)
            kT = A_bf(16896, NTOK)
            vtok = A_bf(19008, 17 * 128).rearrange("p (b c) -> p b c", b=17)
            vbuf = A_bf(23232, 2048).rearrange("p (b c) -> p b c", b=16)
            EBS = A_f32(25280, 512).rearrange("p (h t) -> p h t", h=8)
            tokst = tmpa[:, 7, :].rearrange("p (a c) -> p a c", a=4)
            PT = tmp_bf
            r_q = [[Res() for _ in range(5)] for _ in range(4)]
            r_k = [Res() for _ in range(5)]
            r_vt = [Res() for _ in range(17)]
            r_vb, r_ebs, r_tok = Res(), Res(), [Res() for _ in range(4)]
            S.barrier()
            s_vb = S.dsem("vbuf")
            ld("sp", o_ks[:, 0:124, :], din["state_win_k"][:, 4:128, :], s_o)
            ld("sp", o_vs[:, 0:124, :], din["state_win_v"][:, 4:128, :], s_o)
            if SUB < 3.3:
                return
            cnt = 0
            for hh in range(4):
                un = wunit([(lambda r: r.rearrange("p (k f) -> p k f", k=8)[:, :, 0:64],
                             w_in[:, hh * 64:(hh + 1) * 64].rearrange("(k p) f -> p k f", p=128)),
                            (lambda r: r.rearrange("p (k f) -> p k f", k=8)[:, :, 64:128],
                             w_in[:, (4 + hh) * 64:(5 + hh) * 64].rearrange("(k p) f -> p k f", p=128))])
                for tt, (t0, n) in enumerate(TTS):
                    pb = cnt % 4
                    cnt += 1
                    for k in range(DT):
                        mm(psum[:, pb, 0:n], ru(un)[:, k, :], hT[:, k, t0:t0 + n], k == 0, k == DT - 1, [r_ring[un], r_h[k][tt]], [r_ps[pb]])
                    S.op("act", lambda e, pb=pb, hh=hh, t0=t0, n=n: e.mul(out=q[:, hh, t0:t0 + n], in_=psum[:, pb, 0:n], mul=0.125),
                         reads=[r_ps[pb]], writes=[r_q[hh][tt]])
            if SUB < 3.4:
                return
            un = kunit(w_in, 512)
            for tt, (t0, n) in enumerate(TTS):
                pb = cnt % 4
                cnt += 1
                for k in range(DT):
                    mm(psum[:, pb, 0:n], ru(un)[:, k, :], hT[:, k, t0:t0 + n], k == 0, k == DT - 1, [r_ring[un], r_h[k][tt]], [r_ps[pb]])
                S.op("dve", lambda e, pb=pb, t0=t0, n=n: e.tensor_copy(out=kT[:, t0:t0 + n], in_=psum[:, pb, 0:n]), reads=[r_ps[pb]], writes=[r_k[tt]])
            for (row, t0, n, tt) in ((0, 1920, 128, 3), (1, 2048, 64, 4)):
                for k in range(DT):
                    mm(psum[0:n, 4, row * 128:(row + 1) * 128], hT[:, k, t0:t0 + n], ru(un)[:, k, :], k == 0, k == DT - 1,
                       [r_ring[un], r_h[k][tt]], [r_ps[4]])
                S.op("act", lambda e, row=row, n=n: e.copy(out=tokst[0:n, row, :], in_=psum[0:n, 4, row * 128:(row + 1) * 128]),
                     reads=[r_ps[4]], writes=[r_tok[row]])
            s_tk = S.dsem("tokouts")
            ld("sp", o_kp[:, :], tokst[:, 0, :], s_tk, reads=[r_tok[0]])
            for b in range(16):
                ld("sp", o_ks[b, 124:128, :], tokst[4 * b:4 * b + 4, 1, :], s_tk, reads=[r_tok[1]])
            if SUB < 3.5:
                return
            un = kunit(w_in, 640)
            ld("pool", vbuf, din["state_win_v"].rearrange("b k c -> k b c"), s_vb, writes=[r_vb])
            for bi, (t0, n) in enumerate(BLKS):
                tt = min(t0 // 512, 4)
                pb = 5 + (bi % 2) * 2
                for k in range(DT):
                    mm(psum[0:n, pb, 0:128], hT[:, k, t0:t0 + n], ru(un)[:, k, :], k == 0, k == DT - 1, [r_ring[un], r_h[k][tt]], [r_ps[pb]])
                S.op("dve", lambda e, pb=pb, bi=bi, n=n: e.tensor_copy(out=vtok[0:n, bi, :], in_=psum[0:n, pb, 0:128]), reads=[r_ps[pb]], writes=[r_vt[bi]])
                if bi >= 15:
                    row = 2 + (bi - 15)
                    S.op("dve", lambda e, pb=pb, row=row, n=n: e.tensor_copy(out=tokst[0:n, row, :], in_=psum[0:n, pb, 0:128]), reads=[r_ps[pb]], writes=[r_tok[row]])
            if not os.environ.get("MK_NOVOUT"):
                ld("sp", o_vp[:, :], tokst[:, 2, :], s_tk, reads=[r_tok[2]])
                for b in range(16):
                    ld("sp", o_vs[b, 124:128, :], tokst[4 * b:4 * b + 4, 3, :], s_tk, reads=[r_tok[3]])
            S.op("dve", lambda e: e.memset(EBS[0:64, :, :], 0.0), writes=[r_ebs])
            s_ebs = S.dsem("ebs")
            for b in range(16):
                ld("sp", EBS[4 * b:4 * b + 4, :, 4 * b:4 * b + 4], EB[0:4, 0, :, 0:4], s_ebs, reads=[r_EB], writes=[r_ebs])
            attnT = hT
            etc = [0]

            PTB = tmpa[:, 6:8, :].bitcast(BF16).rearrange("p a (b f) -> p (a b) f", b=2)
            PTs = (PT, PTB)
            tok_all = list(r_tok)

            def pt_res(par, pb):
                return r_tmp[(4 if par == 0 else 6) + pb // 2]

            def att_scores(bi):
                t0 = bi * 128
                tt = min(t0 // 512, 4)
                par = bi % 2
                kbs = [(0, bi)] + ([(1, bi - 1)] if bi > 0 else [])
                for g in range(2):
                    gp = slice(g * 64, (g + 1) * 64)
                    for (kb, kblk) in kbs:
                        pb = g * 2 + kb
                        k0 = kblk * 128
                        ktt = min(k0 // 512, 4)
                        mm(psum[:, pb, :], kT[gp, k0:k0 + 128], q[gp, :, t0:t0 + 128], True, True,
                           [r_k[ktt]] + [r_q[hh][tt] for hh in range(4)], [r_ps[pb]])
                        ti = etc[0] % 4
                        etc[0] += 1
                        et = tmpa[:, ti, :] if ti < 2 else tmpb[:, ti - 2, :]
                        r_et = r_tmp[ti] if ti < 2 else r_tmpb[ti - 2]
                        S.op("act", lambda e, et=et, pb=pb: e.activation(out=et, in_=psum[:, pb, :], func=AF.Exp), reads=[r_ps[pb]], writes=[r_et])
                        S.op("dve", lambda e, et=et, pb=pb, kb=kb, g=g, par=par: e.tensor_tensor(
                            out=PTs[par][:, pb, :].rearrange("p (h t) -> p h t", h=4), in0=et.rearrange("p (h t) -> p h t", h=4),
                            in1=EB[:, kb, 4 * g:4 * g + 4, :], op=ALU.mult),
                            reads=[r_et, r_EB], writes=[pt_res(par, pb)] + (tok_all if par == 1 else []))

            def att_av(bi):
                t0 = bi * 128
                tt = min(t0 // 512, 4)
                par = bi % 2
                bo, bd = (4, 5) if par == 0 else (6, 7)
                kbs = [(0, bi)] + ([(1, bi - 1)] if bi > 0 else [])
                for (bank, is_den) in ((bo, False), (bd, True)):
                    for g in range(2):
                        gp = slice(g * 64, (g + 1) * 64)
                        for i, (kb, kblk) in enumerate(kbs):
                            pb = g * 2 + kb
                            lhs = ones_b[:, 0:64] if is_den else vtok[:, kblk, g * 64:(g + 1) * 64]
                            mm(psum[gp, bank, :], lhs, PTs[par][:, pb, :], i == 0, i == len(kbs) - 1,
                               [r_const if is_den else r_vt[kblk], pt_res(par, pb)], [r_ps[bank]], signal=(g == 1 and i == len(kbs) - 1))
                ds = 2 + par
                den = tmpa[:, ds, :]
                S.op("dve", lambda e, den=den, bd=bd: e.tensor_tensor(out=den, in0=psum[:, bd, :], in1=es_bc[:].rearrange("p h t -> p (h t)"), op=ALU.add),
                     reads=[r_ps[bd], r_es], writes=[r_tmp[ds]])
                S.op("act", lambda e, den=den: e.activation(out=den, in_=den, func=AF.Ln), reads=[r_tmp[ds]], writes=[r_tmp[ds]])
                S.op("act", lambda e, den=den: e.activation(out=den, in_=den, func=AF.Exp, scale=-1.0), reads=[r_tmp[ds]], writes=[r_tmp[ds]])
                S.op("dve", lambda e, t0=t0, den=den, bo=bo: e.tensor_tensor(out=attnT[:, 0:4, t0:t0 + 128], in0=psum[:, bo, :].rearrange("p (h t) -> p h t", h=4),
                                                                          in1=den.rearrange("p (h t) -> p h t", h=4), op=ALU.mult),
                     reads=[r_ps[bo], r_tmp[ds]], writes=[r_h[hh][tt] for hh in range(4)])

            if SUB < 4:
                return
            att_scores(0)
            for bi in range(16):
                if bi + 1 < 16:
                    att_scores(bi + 1)
                att_av(bi)
            if SUB < 5:
                return

            def sample_attn():
                t0, n, tt = 2048, 64, 4
                v4 = lambda ap: ap.rearrange("p (b h t) -> p b h t", b=16, h=4)
                for b in range(16):
                    for g in range(2):
                        gp = slice(g * 64, (g + 1) * 64)
                        mm(psum[:, g, b * 16:(b + 1) * 16], kbuf[gp, b, :], q[gp, :, t0 + 4 * b:t0 + 4 * b + 4], True, True,
                           [r_kb] + [r_q[hh][4] for hh in range(4)], [r_ps[g]], signal=(b == 15))
                for g in range(2):
                    gp = slice(g * 64, (g + 1) * 64)
                    mm(psum[0:64, 2 + g, 0:256], kT[gp, t0:t0 + 64], q[gp, :, t0:t0 + 64].rearrange("p h (b t) -> p b h t", t=4), True, True,
                       [r_k[4]] + [r_q[hh][4] for hh in range(4)], [r_ps[2 + g]], signal=True)
                et0, et1 = tmpa[:, 0, :], tmpa[:, 1, :]
                for g in range(2):
                    cs = slice(g * 256, (g + 1) * 256)
                    S.op("act", lambda e, g=g, cs=cs: e.activation(out=et0[:, cs], in_=psum[:, g, 0:256], func=AF.Exp),
                         reads=[r_ps[g]], writes=[r_tmp[0]])
                    S.op("dve", lambda e, g=g, cs=cs: e.tensor_tensor(
                        out=v4(PT[:, 0, cs]), in0=v4(et0[:, cs]),
                        in1=EB[:, 1, 4 * g:4 * g + 4, 0:4].unsqueeze(1).broadcast_to([128, 16, 4, 4]), op=ALU.mult),
                        reads=[r_tmp[0], r_EB], writes=[r_tmp[4]])
                for g in range(2):
                    cs = slice(g * 256, (g + 1) * 256)
                    S.op("act", lambda e, g=g, cs=cs: e.activation(out=et1[0:64, cs], in_=psum[0:64, 2 + g, 0:256], func=AF.Exp),
                         reads=[r_ps[2 + g]], writes=[r_tmp[1]])
                    S.op("dve", lambda e, g=g, cs=cs: e.tensor_tensor(
                        out=v4(PT[0:64, 2, cs]), in0=v4(et1[0:64, cs]),
                        in1=EBS[0:64, 4 * g:4 * g + 4, :].rearrange("p h (b t) -> p b h t", t=4), op=ALU.mult),
                        reads=[r_tmp[1], r_ebs], writes=[r_tmp[5]])
                for (bank, is_den) in ((4, False), (5, True)):
                    for g in range(2):
                        gp = slice(g * 64, (g + 1) * 64)
                        lhs_c = ones_b[0:64, 0:64] if is_den else vtok[0:64, 16, g * 64:(g + 1) * 64]
                        mm(psum[gp, bank, 0:256], lhs_c, PT[0:64, 2, g * 256:(g + 1) * 256], True, False,
                           [r_vt[16], r_tmp[5], r_const], [r_ps[bank]], signal=False)
                        for b in range(16):
                            lhs_p = ones_b[:, 0:64] if is_den else vbuf[:, b, g * 64:(g + 1) * 64]
                            mm(psum[gp, bank, b * 16:(b + 1) * 16], lhs_p, PT[:, 0, g * 256 + b * 16:g * 256 + (b + 1) * 16], False, b == 15,
                               [r_vb, r_tmp[4], r_const], [r_ps[bank]], signal=(b == 15 and g == 1))
                den = tmpa[:, 3, 0:256]
                S.op("dve", lambda e: e.tensor_tensor(out=v4(den), in0=v4(psum[:, 5, 0:256]),
                                                      in1=es_bc[:, :, 0:4].unsqueeze(1).broadcast_to([128, 16, 4, 4]), op=ALU.add),
                     reads=[r_ps[5], r_es], writes=[r_tmp[3]])
                S.op("dve", lambda e: e.reciprocal(out=den, in_=den), reads=[r_tmp[3]], writes=[r_tmp[3]])
                S.op("dve", lambda e: e.tensor_tensor(
                    out=attnT[:, 0:4, t0:t0 + 64].rearrange("p h (b t) -> p h b t", t=4),
                    in0=psum[:, 4, 0:256].rearrange("p (b h t) -> p h b t", b=16, h=4),
                    in1=den.rearrange("p (b h t) -> p h b t", b=16, h=4), op=ALU.mult),
                    reads=[r_ps[4], r_tmp[3]], writes=[r_h[hh][4] for hh in range(4)])

            sample_attn()
            if SUB < 6:
                return
            ous = []
            for d in range(DT):
                cs = slice(d * 128, (d + 1) * 128)
                ous.append(wunit([
                    (lambda r: r.rearrange("p (k f) -> p k f", k=8)[0:64, 0:4, :], w_out[0:256, cs].rearrange("(hh dd) c -> dd hh c", dd=64)),
                    (lambda r: r.rearrange("p (k f) -> p k f", k=8)[64:128, 0:4, :], w_out[256:512, cs].rearrange("(hh dd) c -> dd hh c", dd=64)),
                    (lambda r: r.rearrange("p (k f) -> p k f", k=8)[:, 4:8, :], w_out[512:1024, cs].rearrange("(g p) c -> p g c", p=128)),
                ]))
            ocnt = [0]

            def oproj_tile(tt):
                t0, n = TTS[tt]
                for d in range(DT):
                    pb = (4, 5, 0, 1)[ocnt[0] % 4]
                    ocnt[0] += 1
                    for k in range(8):
                        rhs = attnT[:, k, t0:t0 + n] if k < 4 else dT[:, k - 4, t0:t0 + n]
                        rr = r_h[k][tt] if k < 4 else r_dT[k - 4][tt]
                        mm(psum[:, pb, 0:n], ru(ous[d])[:, k, :], rhs, k == 0, k == 7, [r_ring[ous[d]], rr], [r_ps[pb]])
                    evac_add_x(pb, d, tt, t0, n)

            fused_tail(oproj_tile, nxt)
            S.barrier()

        fg = tmpa[:, 0:2, :].rearrange("p a f -> p (a f)")
        s_fg = S.dsem("fg")
        osb = arena[:, 21120:21120 + 4096].bitcast(F32).rearrange("p (a f) -> p a f", a=2)
        r_os = [Res("os0"), Res("os1")]
        s_os = [S.dsem("os0"), S.dsem("os1")]
        r_fs = [Res(), Res()]
        junkf = tmpa[:, 2:4, :]

        def final_setup():
            ld("sp", fg, din["final_gain"].partition_broadcast(128), s_fg, writes=[r_tmp[0], r_tmp[1]])

        def final_tile(tile):
            blks = [16] if tile == 4 else list(range(4 * tile, 4 * tile + 4))
            for bi in blks:
                t0, n = BLKS[bi]
                sl = bi % 2
                tt = min(t0 // 512, 4)
                pbs = (6, 7) if bi % 2 == 0 else (2, 3)
                ssq = tmpa[:, 6, 16 * sl:16 * sl + 16]
                for half in range(2):
                    pb = pbs[half]
                    for j in range(4):
                        d = half * 4 + j
                        S.op("pe", lambda e, pb=pb, j=j, d=d, n=n, t0=t0: e.transpose(
                            out=psum[0:n, pb, j * 128:(j + 1) * 128], in_=xT[:, d, t0:t0 + n], identity=ident[:]),
                            reads=[r_x[d][tt], r_const], writes=[r_ps[pb]], signal=(j == 3))
                S.op("act", lambda e, n=n, p0=pbs[0], ssq=ssq: e.activation(
                    out=junkf[0:n, :, :], in_=psum[0:n, p0:p0 + 2, :], func=AF.Square, accum_out=ssq[0:n, 0:1]),
                    reads=[r_ps[pbs[0]], r_ps[pbs[1]]], writes=[r_fs[sl]])
                S.op("act", lambda e, n=n, ssq=ssq: e.activation(out=ssq[0:n, 1:2], in_=ssq[0:n, 0:1], func=AF.Sqrt, bias=epsc[0:n, :], scale=1.0 / D),
                     reads=[r_fs[sl], r_const], writes=[r_fs[sl]])
                S.op("dve", lambda e, n=n, ssq=ssq: e.reciprocal(out=ssq[0:n, 1:2], in_=ssq[0:n, 1:2]), reads=[r_fs[sl]], writes=[r_fs[sl]])
                for half in range(2):
                    pb = pbs[half]
                    S.op("dve", lambda e, pb=pb, n=n, half=half, sl=sl, ssq=ssq: e.scalar_tensor_tensor(
                        out=osb[0:n, sl, half * 512:(half + 1) * 512], in0=psum[0:n, pb, :], scalar=ssq[0:n, 1:2],
                        in1=fg[0:n, half * 512:(half + 1) * 512], op0=ALU.mult, op1=ALU.mult),
                        reads=[r_ps[pb], r_fs[sl], r_tmp[0], r_tmp[1]], writes=[r_os[sl]])
                dst = o_yp[t0:t0 + n, :] if bi < 16 else o_ys[:, :]
                ld("sp", dst, osb[0:n, sl, :], s_os[sl], reads=[r_os[sl]])

        if STAGE >= 99:
            load_x(norm=(0, 0), pre_cb=prep_bias)
            prep_bias_b()
            ffn(0, 0, nxt=(0, 1))
            even_mixer(0, nxt=(0, 2))
            ffn(0, 1, nxt=(1, 0))
            ffn(1, 0, nxt=(1, 1))
            odd_mixer(1, nxt=(1, 2))
            ffn(1, 1, after_norm=final_setup, extra=final_tile)
        else:
            if STAGE == 4:
                prep_bias()
            load_x()
            if STAGE == 4:
                prep_bias_b()
            if STAGE == 2:
                ffn(0, 0)
            elif STAGE == 3:
                odd_mixer(1)
            elif STAGE == 4:
                even_mixer(0)
            elif STAGE == 6:
                ffn(0, 0, nxt=(0, 2))
                ffn(0, 1)
            elif STAGE == 5:
                ffn(0, 0); ffn(0, 1); ffn(1, 0); ffn(1, 1)
            S.barrier()
            final_setup()
            for tile in range(5):
                final_tile(tile)

        S.emit(nc, st)
    return nc


_PROG = None


def kernel(**inputs):
    global _PROG
    if _PROG is None:
        _PROG = build_program()
    nc = _PROG
    consts = host_consts()
    f = lambda a: np.ascontiguousarray(np.asarray(a, dtype=np.float32))
    wts = {n_: f(inputs[n_]) for n_, s_ in W_SPECS}
    in_maps = []
    for c in range(NCORES):
        m = {}
        m["x_prompt"] = f(inputs["x_prompt"][c])
        m["x_sample"] = f(inputs["x_sample"][c * 16:(c + 1) * 16]).reshape(NS, D)
        m["state_win_k"] = f(inputs["state_win_k"][0, c * 16:(c + 1) * 16]).reshape(16, 128, 128)
        m["state_win_v"] = f(inputs["state_win_v"][0, c * 16:(c + 1) * 16]).reshape(16, 128, 128)
        m["state_pool"] = f(inputs["state_pool"][0, c * 16:(c + 1) * 16])
        for n_, s_ in W_SPECS:
            m[n_] = wts[n_]
        for n_ in CONST_SHAPES:
            m["c_" + n_] = consts[n_]
        in_maps.append(m)
    res = run_bass_kernel_spmd(nc, in_maps, core_ids=list(range(NCORES)))
    R = res.results
    y_prompt = np.stack([R[c]["y_prompt"] for c in range(NCORES)], 0)
    y_sample = np.concatenate([R[c]["y_sample"].reshape(16, 4, D) for c in range(NCORES)], 0)
    nk_p = np.stack([R[c]["nk_p"].reshape(128, 2, 64) for c in range(NCORES)], 0)[None]
    nv_p = np.stack([R[c]["nv_p"].reshape(128, 2, 64) for c in range(NCORES)], 0)[None]
    np_p = np.stack([R[c]["np_p"] for c in range(NCORES)], 0)[None]
    nk_s = np.concatenate([R[c]["nk_s"].reshape(16, 128, 2, 64) for c in range(NCORES)], 0)[None]
    nv_s = np.concatenate([R[c]["nv_s"].reshape(16, 128, 2, 64) for c in range(NCORES)], 0)[None]
    np_s = np.concatenate([R[c]["np_s"] for c in range(NCORES)], 0)[None]
    sgu_v = np.concatenate([R[c]["sgu_v"].reshape(16, 4, D) for c in range(NCORES)], 0)[None]
    return (y_prompt, y_sample, nk_p, nv_p, np_p, nk_s, nv_s, np_s, sgu_v)
```

```python
import os
import numpy as np
from contextlib import ExitStack
import concourse.bass as bass
import concourse.mybir as mybir
from concourse.bass_utils import run_bass_kernel_spmd

F32 = mybir.dt.float32
BF16 = mybir.dt.bfloat16
AF = mybir.ActivationFunctionType
ALU = mybir.AluOpType

NCORES = 8
D = 1024
DT = 8
FF = 2816
FT = 22
SEQ = 2048
NS = 64
NB_S = 16
NTOK = SEQ + NS
TTS = [(0, 512), (512, 512), (1024, 512), (1536, 512), (2048, 64)]
BLKS = [(b * 128, 128) for b in range(16)] + [(2048, 64)]
EPS = 1e-6
STAGE = int(os.environ.get("MK_STAGE", "99"))
SUB = float(os.environ.get("MK_SUB", "99"))

ENGS = ("pe", "act", "dve", "pool", "sp")


class Res:
    __slots__ = ("name", "lw", "rd")

    def __init__(self, name=""):
        self.name = name
        self.lw = None
        self.rd = []


class DmaSem:
    __slots__ = ("name", "cnt", "h", "q")

    def __init__(self, name):
        self.name = name
        self.cnt = 0
        self.h = None
        self.q = None


class Sched:
    def __init__(self):
        self.q = {e: [] for e in ENGS}
        self.cnt = {e: 0 for e in ENGS}
        self.waited = {e: {} for e in ENGS}
        self.dsems = []

    def dsem(self, name):
        s = DmaSem(name)
        self.dsems.append(s)
        return s

    def _collect(self, eng, reads, writes, is_dma=False):
        need = {}
        for r in reads:
            if r.lw is not None:
                k, v = r.lw
                if v > need.get(k, 0):
                    need[k] = v
        for w in writes:
            if w.lw is not None:
                k, v = w.lw
                if v > need.get(k, 0):
                    need[k] = v
            for (k, v) in w.rd:
                if v > need.get(k, 0):
                    need[k] = v
        out = []
        wd = self.waited[eng]
        for k, v in need.items():
            if k == "pe" and eng == "pe" and not is_dma:
                continue
            if wd.get(k, 0) >= v:
                continue
            wd[k] = v
            out.append((k, v))
        return out

    def op(self, eng, fn, reads=(), writes=(), signal=True):
        waits = self._collect(eng, reads, writes)
        if signal:
            self.cnt[eng] += 1
            val = self.cnt[eng]
        else:
            val = self.cnt[eng] + 1
        e = (eng, val)
        for r in reads:
            r.rd.append(e)
        for w in writes:
            w.lw = e
            w.rd = []
        self.q[eng].append((waits, fn, (eng, 1) if signal else None))

    def dma(self, qeng, fn, sem, reads=(), writes=()):
        waits = self._collect(qeng, reads, writes, is_dma=True)
        assert sem.q in (None, qeng)
        sem.q = qeng
        sem.cnt += 16
        e = (sem, sem.cnt)
        for r in reads:
            r.rd.append(e)
        for w in writes:
            w.lw = e
            w.rd = []
        self.q[qeng].append((waits, fn, (sem, 16)))

    def barrier(self, engs=("pe", "act", "dve", "sp")):
        ev = [(e, self.cnt[e]) for e in engs if e != "sp" and self.cnt[e] > 0]
        ev += [(s, s.cnt) for s in self.dsems if s.cnt > 0 and s.q in engs]
        for e in engs:
            wd = self.waited[e]
            waits = []
            for k, v in ev:
                if k == e:
                    continue
                if wd.get(k, 0) >= v:
                    continue
                wd[k] = v
                waits.append((k, v))
            if waits:
                self.q[e].append((waits, None, None))

    def emit(self, nc, stack):
        esem = {}
        for e in ("pe", "act", "dve", "pool"):
            esem[e] = stack.enter_context(nc.semaphore("s_" + e))
        for s in self.dsems:
            s.h = stack.enter_context(nc.semaphore("d_" + s.name))
        fin = [(s, s.cnt) for s in self.dsems if s.cnt > 0]
        for e in ("pe", "act", "dve", "pool"):
            if self.cnt[e] > 0:
                fin.append((e, self.cnt[e]))

        def hof(k):
            return esem[k] if isinstance(k, str) else k.h

        def run(eng_name, eng):
            for waits, fn, sig in self.q[eng_name]:
                for k, v in waits:
                    eng.wait_ge(hof(k), v)
                if fn is None:
                    continue
                inst = fn(eng)
                if sig is not None:
                    inst.then_inc(hof(sig[0]), sig[1])
            if eng_name == "sp":
                for k, v in fin:
                    eng.wait_ge(hof(k), v)

        block = stack.enter_context(nc.Block())

        @block.tensor
        def _(e):
            run("pe", e)

        @block.scalar
        def _(e):
            run("act", e)

        @block.vector
        def _(e):
            run("dve", e)

        @block.gpsimd
        def _(e):
            run("pool", e)

        @block.sync
        def _(e):
            run("sp", e)


def t5_bucket_np(n):
    n = np.maximum(n, 0)
    max_exact = 16
    nf = np.maximum(n, 1).astype(np.float32)
    large = max_exact + (np.log(nf / np.float32(max_exact)) / np.float32(np.log(128 / max_exact))
                         * np.float32(32 - max_exact)).astype(np.int32)
    large = np.minimum(large, 31)
    return np.where(n < max_exact, n, large)


def host_consts():
    c = {}
    c["ident"] = np.eye(128, dtype=np.float32)
    oh = np.zeros((32, 384), np.float32)
    n = np.arange(384) - 128
    valid = (n >= 0) & (n < 128)
    bk = t5_bucket_np(n)
    oh[bk[valid], np.arange(384)[valid]] = 1.0
    c["onehot"] = oh
    c["trimask"] = np.triu(np.ones((128, 128), np.float32))
    m = np.zeros((64, 64), np.float32)
    for b in range(16):
        for s in range(4):
            for t in range(s, 4):
                m[4 * b + s, 4 * b + t] = 1.0
    c["bdmask"] = m
    ic = np.zeros((128, 4, 16), np.float32)
    for g, w in enumerate((2, 4, 8, 16)):
        ic[:, g, :] = 1.0 / np.minimum(np.arange(16) + 1, w)
    c["invc"] = ic
    sel = np.zeros((4, 64), np.float32)
    for b in range(16):
        for t in range(4):
            sel[t, 4 * b + t] = 1.0
    c["sel4"] = sel
    A = np.zeros((128, 4, 128), np.float32)
    B = np.zeros((128, 4, 128), np.float32)
    A0 = np.zeros((128, 4, 128), np.float32)
    for g, w in enumerate((2, 4, 8, 16)):
        for t in range(128):
            for sx in range(t - w + 1, t + 1):
                if sx >= 0:
                    A[sx, g, t] += 1.0 / w
                    A0[sx, g, t] += 1.0 / min(t + 1, w)
                else:
                    B[128 + sx, g, t] += 1.0 / w
            A[t, g, t] -= 1.0
            A0[t, g, t] -= 1.0
    c["poolA"], c["poolB"], c["poolA0"] = A, B, A0
    return c


CONST_SHAPES = {"ident": [128, 128], "onehot": [32, 384], "trimask": [128, 128],
                "bdmask": [64, 64], "invc": [128, 4, 16], "sel4": [4, 64],
                "poolA": [128, 4, 128], "poolB": [128, 4, 128], "poolA0": [128, 4, 128]}

W_SPECS = [("rel_bias", [32, 8]), ("norm_gains", [2, 3, 1024]), ("final_gain", [1024]),
           ("ffn_gate", [2, 2, 1024, 2816]), ("ffn_up", [2, 2, 1024, 2816]), ("ffn_down", [2, 2, 2816, 1024]),
           ("w_in_even", [1, 1024, 1280]), ("w_out_even", [1, 1024, 1024]), ("attn_sinks", [1, 8]),
           ("w_pool", [1, 4, 128, 128]), ("pool_scale", [1, 512]), ("w_in_odd", [1, 1024, 2048]),
           ("sgu_norm", [1, 1024]), ("w_spatial", [1, 4, 128, 128]), ("b_spatial", [1, 4, 128]),
           ("w_out_odd", [1, 1024, 1024])]


def build_program():
    nc = bass.Bass("TRN2", target_bir_lowering=False)
    din = {}

    def inp(name, shape):
        din[name] = nc.dram_tensor(name, list(shape), F32, kind="ExternalInput").ap()

    inp("x_prompt", [SEQ, D])
    inp("x_sample", [NS, D])
    inp("state_win_k", [NB_S, 128, 128])
    inp("state_win_v", [NB_S, 128, 128])
    inp("state_pool", [NB_S, 15, 512])
    for n_, s_ in W_SPECS:
        inp(n_, s_)
    for n_, s_ in CONST_SHAPES.items():
        inp("c_" + n_, s_)

    def outp(name, shape):
        return nc.dram_tensor(name, list(shape), F32, kind="ExternalOutput").ap()

    o_yp = outp("y_prompt", [SEQ, D])
    o_ys = outp("y_sample", [NS, D])
    o_kp = outp("nk_p", [128, 128])
    o_vp = outp("nv_p", [128, 128])
    o_pp = outp("np_p", [15, 512])
    o_ks = outp("nk_s", [NB_S, 128, 128])
    o_vs = outp("nv_s", [NB_S, 128, 128])
    o_ps = outp("np_s", [NB_S, 15, 512])
    o_sv = outp("sgu_v", [NS, D])
    scr = nc.dram_tensor("scr_bias", [8, 49664], F32, kind="Internal").ap()

    S = Sched()
    st = ExitStack()
    with st:
        sb = lambda name, shape, dt: st.enter_context(nc.sbuf_tensor(name, shape, dt))
        xT = sb("xT", [128, DT, NTOK], F32)
        hT = sb("hT", [128, DT, NTOK], BF16)
        ARENA_E = 26624
        arena = sb("arena", [128, ARENA_E], BF16)
        NRING = 12
        ring = sb("ring", [128, NRING, 1024], BF16)
        tmpa = sb("tmpa", [128, 8, 512], F32)
        ident = sb("ident", [128, 128], F32)
        ones_f = sb("ones_f", [128, 128], F32)
        ones_b = sb("ones_b", [128, 128], BF16)
        cols = sb("cols", [128, 64], F32)
        epsc = sb("epsc", [128, 1], F32)
        EB = sb("EB", [128, 2, 8, 128], F32)
        psum = st.enter_context(nc.psum_tensor("psum", [128, 8, 512], F32))

        r_ps = [Res(f"ps{i}") for i in range(8)]
        r_x = [[Res(f"x{d}_{t}") for t in range(5)] for d in range(DT)]
        r_h = [[Res(f"h{d}_{t}") for t in range(5)] for d in range(DT)]
        r_tmp = [Res(f"tmp{i}") for i in range(8)]
        r_const = Res("const")
        r_cols = Res("cols")
        r_ring = [Res(f"ring{i}") for i in range(NRING)]
        s_ring = [S.dsem(f"ring{i}") for i in range(NRING)]
        ring_pos = [0]

        def ps(b):
            return psum[:, b, :]

        def ld(qe, out, in_, sem, writes=(), reads=()):
            S.dma(qe, lambda e: e.dma_start(out=out, in_=in_), sem, reads=reads, writes=writes)

        def wunit(pairs):
            i = ring_pos[0] % NRING
            ring_pos[0] += 1
            for (dst_fn, src) in pairs:
                ld("pool", dst_fn(ring[:, i, :]), src, s_ring[i], writes=[r_ring[i]])
            return i

        def mm(out, lhsT, rhs, start, stop, reads, writes, signal=None):
            if signal is None:
                signal = stop
            S.op("pe", lambda e: e.matmul(out, lhsT=lhsT, rhs=rhs, start=start, stop=stop),
                 reads=reads, writes=writes, signal=signal)

        s_c = S.dsem("consts")
        ld("sp", ident[:], din["c_ident"], s_c, writes=[r_const])
        S.op("dve", lambda e: e.memset(ones_f[:], 1.0), writes=[r_const])
        S.op("dve", lambda e: e.memset(ones_b[:], 1.0), writes=[r_const])
        S.op("dve", lambda e: e.memset(epsc[:], EPS), writes=[r_const])
        prow = tmpa[:, 0, :].bitcast(F32)[0:52, 0:128]
        s_p = S.dsem("prow")
        ld("sp", prow[0:48, :], din["norm_gains"].rearrange("l i (dt p) -> (l i dt) p", p=128), s_p, writes=[r_tmp[0]])
        ld("sp", prow[48:52, :], din["pool_scale"].rearrange("o (g p) -> (o g) p", p=128), s_p, writes=[r_tmp[0]])
        S.op("pe", lambda e: e.transpose(out=psum[:, 7, 0:52], in_=prow, identity=ident[0:52, 0:52]),
             reads=[r_tmp[0], r_const], writes=[r_ps[7]])
        S.op("dve", lambda e: e.tensor_copy(out=cols[:, 0:52], in_=psum[:, 7, 0:52]), reads=[r_ps[7]], writes=[r_cols])

        def gain_col(l, i, d):
            j = (l * 3 + i) * 8 + d
            return cols[:, j:j + 1]

        NXS = 8
        xs = arena[:, 0:2048 * NXS].bitcast(F32).rearrange("p (a f) -> p a f", a=NXS)
        r_xs = [Res(f"xs{i}") for i in range(NXS)]
        s_xs = [S.dsem(f"xs{i}") for i in range(NXS)]

        def load_x(norm=None, pre_cb=None):
            for bi, (t0, n) in enumerate(BLKS):
                sl = bi % NXS
                src = din["x_prompt"][t0:t0 + n, :] if bi < 16 else din["x_sample"][:, :]
                ld("sp", xs[0:n, sl, :], src, s_xs[sl], writes=[r_xs[sl]])
                tt = min(t0 // 512, 4)
                for half in range(2):
                    pb = (4 if bi % 2 == 0 else 0) + half
                    for j in range(4):
                        d = half * 4 + j
                        S.op("pe", lambda e, pb=pb, j=j, d=d, n=n, sl=sl: e.transpose(
                            out=psum[:, pb, j * 128:j * 128 + n], in_=xs[0:n, sl, d * 128:(d + 1) * 128],
                            identity=ident[0:n, 0:n]),
                            reads=[r_xs[sl], r_const], writes=[r_ps[pb]], signal=(j == 3))
                    dst = xT[:, half * 4:(half + 1) * 4, t0:t0 + n]
                    srcp = psum[:, pb, :].rearrange("p (j t) -> p j t", j=4)[:, :, 0:n]
                    wr = [r_x[half * 4 + j][tt] for j in range(4)]
                    if half == 0:
                        S.op("dve", lambda e, dst=dst, srcp=srcp: e.tensor_copy(out=dst, in_=srcp), reads=[r_ps[pb]], writes=wr)
                    else:
                        S.op("act", lambda e, dst=dst, srcp=srcp: e.copy(out=dst, in_=srcp), reads=[r_ps[pb]], writes=wr)
                if bi == 3 and pre_cb is not None:
                    pre_cb()
                if norm is not None and (bi % 4 == 3 or bi == 16):
                    norm_stats(tt)
                    if tt >= 1:
                        norm_apply(norm[0], norm[1], tt - 1)
            if norm is not None:
                norm_apply(norm[0], norm[1], 4)
                normed[0] = tuple(norm)

        sq_bf = tmpa[:, 0:2, :].bitcast(BF16)

        def norm_stats(tt):
            t0, n = TTS[tt]
            pb = 6 + (tt % 2)
            for d in range(DT):
                ti = d % 2
                sq = sq_bf[:, ti, 0:n]
                S.op("act", lambda e, sq=sq, d=d: e.activation(out=sq, in_=xT[:, d, t0:t0 + n], func=AF.Square),
                     reads=[r_x[d][tt]], writes=[r_tmp[ti]])
                mm(psum[:, pb, 0:n], ones_b[:], sq, d == 0, d == DT - 1, [r_tmp[ti], r_const], [r_ps[pb]], signal=True)
            ri = 2 + (tt % 2)
            sd = tmpa[:, ri, 0:n]
            S.op("act", lambda e: e.activation(out=sd, in_=psum[:, pb, 0:n], func=AF.Ln, bias=epsc[:], scale=1.0 / D),
                 reads=[r_ps[pb], r_const], writes=[r_tmp[ri]])
            S.op("act", lambda e: e.activation(out=sd, in_=sd, func=AF.Exp, scale=-0.5), reads=[r_tmp[ri]], writes=[r_tmp[ri]])

        def norm_apply(l, i, tt):
            t0, n = TTS[tt]
            ri = 2 + (tt % 2)
            sd = tmpa[:, ri, 0:n]
            for d in range(DT):
                S.op("dve", lambda e, d=d: e.scalar_tensor_tensor(
                    out=hT[:, d, t0:t0 + n], in0=xT[:, d, t0:t0 + n], scalar=gain_col(l, i, d), in1=sd,
                    op0=ALU.mult, op1=ALU.mult),
                    reads=[r_x[d][tt], r_tmp[ri], r_cols], writes=[r_h[d][tt]])

        normed = [None]

        def norm_to_h(l, i):
            if normed[0] == (l, i):
                normed[0] = None
                return
            assert normed[0] is None
            for tt in range(5):
                norm_stats(tt)
                norm_apply(l, i, tt)

        def fused_tail(emit_tile, nxt, extra=None):
            for tt in range(5):
                emit_tile(tt)
                if nxt is not None:
                    if tt >= 1:
                        norm_stats(tt - 1)
                    if tt >= 2:
                        norm_apply(nxt[0], nxt[1], tt - 2)
                if extra is not None and tt >= 1:
                    extra(tt - 1)
            if nxt is not None:
                norm_stats(4)
                norm_apply(nxt[0], nxt[1], 3)
                norm_apply(nxt[0], nxt[1], 4)
                normed[0] = tuple(nxt)
            if extra is not None:
                extra(4)

        CH = 5
        hid = arena[:, 0:2 * CH * NTOK].rearrange("p (b c t) -> p b c t", b=2, c=CH)
        r_hid = [[[Res() for _ in range(5)] for _ in range(CH)] for _ in range(2)]

        def ffn(l, j, nxt=None, mid_cb=None, after_norm=None, extra=None):
            wg, wu, wd = din["ffn_gate"][l, j], din["ffn_up"][l, j], din["ffn_down"][l, j]
            norm_to_h(l, 0 if j == 0 else 2)
            if after_norm is not None:
                after_norm()
            chunks = [[0, 1, 2, 3], [4, 5, 6, 7], [8, 9, 10, 11], [12, 13, 14, 15, 16], [17, 18, 19, 20, 21]]

            def gu(ci):
                hb = ci % 2
                for fi, ft in enumerate(chunks[ci]):
                    ug = wunit([(lambda r: r.rearrange("p (k f) -> p k f", k=8),
                                 wg[:, ft * 128:(ft + 1) * 128].rearrange("(k p) f -> p k f", p=128))])
                    uu = wunit([(lambda r: r.rearrange("p (k f) -> p k f", k=8),
                                 wu[:, ft * 128:(ft + 1) * 128].rearrange("(k p) f -> p k f", p=128))])
                    for tt, (t0, n) in enumerate(TTS):
                        par = (fi * 5 + tt) % 2
                        pg, pu = 2 * par, 2 * par + 1
                        for (un, pb) in ((ug, pg), (uu, pu)):
                            for k in range(DT):
                                mm(psum[:, pb, 0:n], ring[:, un, k * 128:(k + 1) * 128], hT[:, k, t0:t0 + n],
                                   k == 0, k == DT - 1, [r_ring[un], r_h[k][tt]], [r_ps[pb]])
                        ti = 4 + par
                        sg = tmpa[:, ti, 0:n]
                        S.op("act", lambda e, sg=sg, pg=pg, n=n: e.activation(out=sg, in_=psum[:, pg, 0:n], func=AF.Silu),
                             reads=[r_ps[pg]], writes=[r_tmp[ti]])
                        S.op("dve", lambda e, sg=sg, pu=pu, n=n, hb=hb, fi=fi, t0=t0: e.tensor_tensor(
                            out=hid[:, hb, fi, t0:t0 + n], in0=sg, in1=psum[:, pu, 0:n], op=ALU.mult),
                            reads=[r_tmp[ti], r_ps[pu]], writes=[r_hid[hb][fi][tt]])

            def down(ci, last=False):
                hb = ci % 2
                fts = chunks[ci]
                us = [wunit([(lambda r: r, wd[ft * 128:(ft + 1) * 128, :])]) for ft in fts]
                cnt = [0]

                dbanks = (4, 5, 0, 1) if last else (4, 5, 6, 7)

                def group(d, tt):
                    t0, n = TTS[tt]
                    pb = dbanks[cnt[0] % 4]
                    cnt[0] += 1
                    for fi in range(len(fts)):
                        mm(psum[:, pb, 0:n], ring[:, us[fi], d * 128:(d + 1) * 128], hid[:, hb, fi, t0:t0 + n],
                           fi == 0, fi == len(fts) - 1, [r_ring[us[fi]], r_hid[hb][fi][tt]], [r_ps[pb]])
                    S.op("dve", lambda e: e.scalar_tensor_tensor(
                        out=xT[:, d, t0:t0 + n], in0=psum[:, pb, 0:n], scalar=0.5, in1=xT[:, d, t0:t0 + n],
                        op0=ALU.mult, op1=ALU.add),
                        reads=[r_ps[pb], r_x[d][tt]], writes=[r_x[d][tt]])

                if not last:
                    for d in range(DT):
                        for tt in range(5):
                            group(d, tt)
                else:
                    fused_tail(lambda tt: [group(d, tt) for d in range(DT)], nxt, extra)

            for ci in range(len(chunks)):
                gu(ci)
                if ci > 0:
                    down(ci - 1)
                if ci == 2 and mid_cb is not None:
                    mid_cb()
            down(len(chunks) - 1, last=True)

        def A_bf(off, n):
            return arena[:, off:off + n]

        def A_f32(off, n_f32):
            return arena[:, off:off + 2 * n_f32].bitcast(F32)

        tmp_bf = tmpa[:, 4:6, :].bitcast(BF16).rearrange("p a (b f) -> p (a b) f", b=2)

        def evac_add_x(pb, d, tt, t0, n):
            S.op("dve", lambda e: e.tensor_tensor(out=xT[:, d, t0:t0 + n], in0=psum[:, pb, 0:n], in1=xT[:, d, t0:t0 + n], op=ALU.add),
                 reads=[r_ps[pb], r_x[d][tt]], writes=[r_x[d][tt]])

        def kunit(w2d, c0, ncol=128):
            return wunit([(lambda r: r.rearrange("p (k f) -> p k f", k=8)[:, :, 0:ncol],
                           w2d[:, c0:c0 + ncol].rearrange("(k p) f -> p k f", p=128))])

        def ru(i):
            return ring[:, i, :].rearrange("p (k f) -> p k f", k=8)

        es_bc = sb("es_bc", [128, 4, 128], F32)
        tmpb = sb("tmpb", [128, 2, 512], F32)
        r_tmpb = [Res("tmpb0"), Res("tmpb1")]
        r_EB = Res("EB")
        r_es = Res("es")

        prep_state = {}

        def prep_bias():
            rb = tmpa[0:32, 4, 0:8]
            oh = tmpa[0:32, 5, 0:384]
            Lh = tmpa[0:32, 6:8, :].rearrange("p a f -> p (a f)").rearrange("p (h m) -> p h m", h=8)
            ers = [arena[:, 21120:21888].bitcast(F32), arena[:, 21888:22656].bitcast(F32)]
            r_er = [Res("er0"), Res("er1")]
            s_b = S.dsem("biasld")
            ld("sp", rb, din["rel_bias"], s_b, writes=[r_tmp[4]])
            ld("sp", oh, din["c_onehot"], S.dsem("ohld"), writes=[r_tmp[5]])
            S.op("act", lambda e: e.activation(out=rb, in_=rb, func=AF.Exp), reads=[r_tmp[4]], writes=[r_tmp[4]])
            S.op("dve", lambda e: e.tensor_copy(out=Lh, in_=rb.unsqueeze(2).broadcast_to([32, 8, 128])),
                 reads=[r_tmp[4]], writes=[r_tmp[6], r_tmp[7]])
            s_scr = S.dsem("scrw")
            r_scr = Res("scr")
            for h in range(8):
                pb = 2 + (h % 2)
                ti = h % 2
                mm(psum[:, pb, 0:384], Lh[:, h, :], oh, True, True, [r_tmp[5], r_tmp[6], r_tmp[7]], [r_ps[pb]])
                er = ers[ti]
                S.op("dve", lambda e, er=er, pb=pb: e.tensor_copy(out=er, in_=psum[:, pb, 0:384]), reads=[r_ps[pb]], writes=[r_er[ti]])
                ld("pool", scr[h, 0:128 * 384].rearrange("(k i) -> k i", i=384), er, s_scr, reads=[r_er[ti]], writes=[r_scr])
            prep_state["r_scr"] = r_scr

        def prep_bias_b():
            r_scr = prep_state["r_scr"]
            s_eb = S.dsem("ebld")
            for kb, off in ((0, 128), (1, 256)):
                src = scr[:, off:off + 128 * 383].rearrange("h (k i) -> k h i", i=383)[:, :, 0:128]
                ld("pool", EB[:, kb, :, :], src, s_eb, reads=[r_scr], writes=[r_EB])
            es8 = cols[:, 56:60]
            s_es = S.dsem("esld")
            for g in range(2):
                ld("sp", es8[g * 64:(g + 1) * 64, :], din["attn_sinks"][0, 4 * g:4 * g + 4].partition_broadcast(64), s_es, writes=[r_es])
            S.op("act", lambda e: e.activation(out=es8, in_=es8, func=AF.Exp), reads=[r_es], writes=[r_es])
            S.op("dve", lambda e: e.tensor_copy(out=es_bc[:], in_=es8.unsqueeze(2).broadcast_to([128, 4, 128])),
                 reads=[r_es], writes=[r_es])

        def odd_mixer(l, nxt=None):
            o = l // 2
            S.barrier()
            norm_to_h(l, 1)
            w_in, w_out = din["w_in_odd"][o], din["w_out_odd"][o]
            uT = A_bf(0, 8 * NTOK).rearrange("p (c t) -> p c t", c=8)
            vt32 = A_f32(16896, 2048).rearrange("p (a f) -> p a f", a=2)
            vn = A_bf(20992, 2048).rearrange("p (a f) -> p a f", a=2)
            wsT = A_bf(23040, 512).rearrange("p (g t) -> p g t", g=4)
            wsbd = A_bf(23552, 256).rearrange("p (g t) -> p g t", g=4)
            sgn = A_f32(23808, 1024)
            trim = A_f32(25856, 128)
            bdm = A_f32(26112, 64)
            sel4f = A_f32(26240, 64)
            sel4b = A_bf(26368, 64)
            bsp = tmpa[:, 6, :].rearrange("p (g t) -> p g t", g=4)
            wst = tmpa[:, 7, :].rearrange("p (g t) -> p g t", g=4)
            r_u = [[Res() for _ in range(5)] for _ in range(8)]
            r_vt = [Res(), Res()]
            r_vn = [Res(), Res()]
            r_c = Res("oddc")
            r_ws = Res("wsT")
            s_c2 = S.dsem("oddc")
            ld("sp", wst, w_sp_ap(o), S.dsem("wst"), writes=[r_tmp[7]])
            ld("sp", bsp, din["b_spatial"][o].partition_broadcast(128), S.dsem("bsp"), writes=[r_tmp[6]])
            ld("sp", sgn, din["sgu_norm"][o].partition_broadcast(128), s_c2, writes=[r_c])
            ld("sp", trim, din["c_trimask"], s_c2, writes=[r_c])
            ld("sp", bdm[0:64, :], din["c_bdmask"], s_c2, writes=[r_c])
            ld("sp", sel4f[0:4, :], din["c_sel4"], s_c2, writes=[r_c])
            cnt = 0
            for ft in range(8):
                un = kunit(w_in, ft * 128)
                for tt, (t0, n) in enumerate(TTS):
                    pb = cnt % 4
                    cnt += 1
                    for k in range(DT):
                        mm(psum[:, pb, 0:n], ru(un)[:, k, :], hT[:, k, t0:t0 + n], k == 0, k == DT - 1,
                           [r_ring[un], r_h[k][tt]], [r_ps[pb]])
                    S.op("act", lambda e, pb=pb, ft=ft, t0=t0, n=n: e.activation(out=uT[:, ft, t0:t0 + n], in_=psum[:, pb, 0:n],
                                                                                func=AF.Gelu_apprx_tanh),
                         reads=[r_ps[pb]], writes=[r_u[ft][tt]])
            S.op("dve", lambda e: e.tensor_copy(out=sel4b[0:4, :], in_=sel4f[0:4, :]), reads=[r_c], writes=[r_c])
            for g in range(4):
                S.op("pe", lambda e, g=g: e.transpose(out=psum[:, 6, g * 128:(g + 1) * 128], in_=wst[:, g, :], identity=ident[:]),
                     reads=[r_tmp[7], r_const], writes=[r_ps[6]], signal=(g == 3))
            S.op("dve", lambda e: e.tensor_tensor(out=wsT, in0=psum[:, 6, :].rearrange("p (g t) -> p g t", g=4),
                                                  in1=trim.unsqueeze(1).broadcast_to([128, 4, 128]), op=ALU.mult),
                 reads=[r_ps[6], r_c], writes=[r_ws])
            for g in range(4):
                mm(psum[0:64, 7, g * 64:(g + 1) * 64], sel4b[0:4, :],
                   wsT[0:4, g, 0:4].unsqueeze(1).broadcast_to([4, 16, 4]), True, True, [r_ws, r_c], [r_ps[7]], signal=(g == 3))
            S.op("dve", lambda e: e.tensor_tensor(out=wsbd[0:64, :, :], in0=psum[0:64, 7, 0:256].rearrange("p (g t) -> p g t", g=4),
                                                  in1=bdm[0:64, :].unsqueeze(1).broadcast_to([64, 4, 64]), op=ALU.mult),
                 reads=[r_ps[7], r_c], writes=[r_ws])
            uv = [wunit([(lambda r: r, w_in[k * 128:(k + 1) * 128, 1024:2048])]) for k in range(8)]
            junk = tmpa[:, 4, :].rearrange("p (a f) -> p a f", a=1)
            junk2 = tmpa[:, 4:6, :]
            r_ssq = [Res(), Res()]
            r_mix = [[Res(), Res()], [Res(), Res()]]
            s_sv = S.dsem("sv")
            def v_stage1(bi):
                t0, n = BLKS[bi]
                sl = bi % 2
                tt = min(t0 // 512, 4)
                vb = (0, 1) if sl == 0 else (4, 5)
                sbk = (2, 3) if sl == 0 else (6, 7)
                ssq = cols[:, 60 + 2 * sl:62 + 2 * sl]
                for half in range(2):
                    pb = vb[half]
                    for k in range(8):
                        mm(psum[0:n, pb, :], hT[:, k, t0:t0 + n], ring[:, uv[k], half * 512:(half + 1) * 512],
                           k == 0, k == 7, [r_ring[uv[k]], r_h[k][tt]], [r_ps[pb]])
                    S.op("act", lambda e, pb=pb, n=n, sl=sl, half=half: e.activation(
                        out=vt32[0:n, sl, half * 512:(half + 1) * 512], in_=psum[0:n, pb, :], func=AF.Gelu_apprx_tanh),
                        reads=[r_ps[pb]], writes=[r_vt[sl]])
                S.op("act", lambda e, n=n, sl=sl, ssq=ssq: e.activation(
                    out=junk2[0:n, :, :], in_=vt32[0:n, sl, :].rearrange("p (a f) -> p a f", a=2), func=AF.Square,
                    accum_out=ssq[0:n, 0:1]),
                    reads=[r_vt[sl]], writes=[r_ssq[sl]])
                S.op("act", lambda e, n=n, ssq=ssq: e.activation(out=ssq[0:n, 1:2], in_=ssq[0:n, 0:1], func=AF.Sqrt, bias=epsc[0:n, :], scale=1.0 / 1024),
                     reads=[r_ssq[sl], r_const], writes=[r_ssq[sl]])
                S.op("dve", lambda e, n=n, ssq=ssq: e.reciprocal(out=ssq[0:n, 1:2], in_=ssq[0:n, 1:2]), reads=[r_ssq[sl]], writes=[r_ssq[sl]])
                if bi < 16:
                    S.op("dve", lambda e, n=n, sl=sl, ssq=ssq: e.scalar_tensor_tensor(
                        out=vn[0:n, sl, :], in0=vt32[0:n, sl, :], scalar=ssq[0:n, 1:2], in1=sgn[0:n, :], op0=ALU.mult, op1=ALU.mult),
                        reads=[r_vt[sl], r_ssq[sl], r_c], writes=[r_vn[sl]])
                else:
                    S.op("dve", lambda e, n=n, sl=sl, ssq=ssq: e.scalar_tensor_tensor(
                        out=vt32[0:n, sl, :], in0=vt32[0:n, sl, :], scalar=ssq[0:n, 1:2], in1=sgn[0:n, :], op0=ALU.mult, op1=ALU.mult),
                        reads=[r_vt[sl], r_ssq[sl], r_c], writes=[r_vt[sl]])
                    S.op("dve", lambda e, n=n, sl=sl: e.tensor_copy(out=vn[0:n, sl, :], in_=vt32[0:n, sl, :]),
                         reads=[r_vt[sl]], writes=[r_vn[sl]])
                    ld("sp", o_sv[:, :], vt32[0:n, sl, :], s_sv, reads=[r_vt[sl]])

            def v_stage2(bi):
                t0, n = BLKS[bi]
                sl = bi % 2
                tt = min(t0 // 512, 4)
                vb = (0, 1) if sl == 0 else (4, 5)
                sbk = (2, 3) if sl == 0 else (6, 7)
                for half in range(2):
                    pb = sbk[half]
                    for c4 in range(4):
                        ct = half * 4 + c4
                        g = ct // 2
                        rhs = wsT[0:n, g, 0:n] if bi < 16 else wsbd[0:n, g, 0:n]
                        mm(psum[:, pb, c4 * 128:c4 * 128 + n], vn[0:n, sl, ct * 128:(ct + 1) * 128], rhs, True, True,
                           [r_vn[sl], r_ws], [r_ps[pb]], signal=(c4 == 3))
                    ti = 2 * sl + half
                    rm = r_tmp[ti]
                    mix = tmpa[:, ti, :].rearrange("p (c t) -> p c t", c=4)
                    pv = psum[:, pb, :].rearrange("p (c t) -> p c t", c=4)
                    for gg in range(2):
                        g = half * 2 + gg
                        if bi < 16:
                            in1 = bsp[:, g:g + 1, 0:n].broadcast_to([128, 2, n])
                            S.op("dve", lambda e, mix=mix, pv=pv, gg=gg, n=n, in1=in1: e.tensor_tensor(
                                out=mix[:, 2 * gg:2 * gg + 2, 0:n], in0=pv[:, 2 * gg:2 * gg + 2, 0:n], in1=in1, op=ALU.add),
                                reads=[r_ps[pb], r_tmp[6]], writes=[rm])
                        else:
                            in1 = bsp[:, g, 0:4].unsqueeze(1).unsqueeze(1).broadcast_to([128, 2, 16, 4])
                            S.op("dve", lambda e, mix=mix, pv=pv, gg=gg, n=n, in1=in1: e.tensor_tensor(
                                out=mix[:, 2 * gg:2 * gg + 2, 0:n].rearrange("p c (b t) -> p c b t", t=4),
                                in0=pv[:, 2 * gg:2 * gg + 2, 0:n].rearrange("p c (b t) -> p c b t", t=4), in1=in1, op=ALU.add),
                                reads=[r_ps[pb], r_tmp[6]], writes=[rm])
                    S.op("dve", lambda e, mix=mix, half=half, t0=t0, n=n: e.tensor_tensor(
                        out=uT[:, half * 4:(half + 1) * 4, t0:t0 + n], in0=mix[:, :, 0:n], in1=uT[:, half * 4:(half + 1) * 4, t0:t0 + n],
                        op=ALU.mult),
                        reads=[rm] + [r_u[half * 4 + c][tt] for c in range(4)],
                        writes=[r_u[half * 4 + c][tt] for c in range(4)])

            v_stage1(0)
            for bi in range(len(BLKS)):
                if bi + 1 < len(BLKS):
                    v_stage1(bi + 1)
                v_stage2(bi)
            ous = [kunit(w_out, d * 128) for d in range(DT)]
            ocnt = [0]

            def oproj_tile(tt):
                t0, n = TTS[tt]
                for d in range(DT):
                    pb = (4, 5, 0, 1)[ocnt[0] % 4]
                    ocnt[0] += 1
                    for k in range(8):
                        mm(psum[:, pb, 0:n], ru(ous[d])[:, k, :], uT[:, k, t0:t0 + n], k == 0, k == 7, [r_ring[ous[d]], r_u[k][tt]], [r_ps[pb]])
                    evac_add_x(pb, d, tt, t0, n)

            fused_tail(oproj_tile, nxt)
            S.barrier()

        def w_sp_ap(o):
            return din["w_spatial"][o].rearrange("g t s -> t g s")

        def even_mixer(l, nxt=None):
            ev = l // 2
            S.barrier()
            norm_to_h(l, 1)
            w_in, w_out = din["w_in_even"][ev], din["w_out_even"][ev]
            if SUB < 2:
                return
            dT = A_bf(0, 4 * NTOK).rearrange("p (g t) -> p g t", g=4)
            utb = A_bf(8448, 3 * 512).rearrange("p (a c) -> p a c", a=3)
            ut32 = A_f32(9984, 512)
            poolA = A_bf(11008, 512).rearrange("p (g t) -> p g t", g=4)
            poolB = A_bf(11520, 512).rearrange("p (g t) -> p g t", g=4)
            poolA0 = A_f32(12032, 512).rearrange("p (g t) -> p g t", g=4)
            Us = A_f32(14784, 4 * 16 * 19).rearrange("p (g b t) -> p g b t", g=4, b=16)
            Ws1 = A_f32(17216, 304).rearrange("p (b t) -> p b t", b=16)
            Ws2 = A_f32(17824, 304).rearrange("p (b t) -> p b t", b=16)
            invc = A_f32(18432, 64).rearrange("p (g t) -> p g t", g=4)
            wpool = A_bf(25280, 512).rearrange("p (g d) -> p g d", g=4)
            ctxs = A_f32(19072, 1024).rearrange("p (a f) -> p a f", a=2)
            uptok = A_f32(19072, 1024).rearrange("p (a f) -> p a f", a=2)
            t16 = A_f32(18560, 16)
            r_dT = [[Res() for _ in range(5)] for _ in range(4)]
            r_utb = [Res() for _ in range(3)]
            r_ut32 = Res("ut32")
            r_pm = Res("poolmats")
            r_Us = [Res() for _ in range(4)]
            r_Ws = [Res(), Res()]
            r_pc = Res("poolc")
            r_ctx = [Res(), Res()]
            r_upt = Res("uptok")
            s_pc = S.dsem("poolc")
            s_ctx = [S.dsem("ctx0"), S.dsem("ctx1")]
            ld("sp", poolA0, din["c_poolA0"], s_pc, writes=[r_pc])
            s_wp = S.dsem("wpool")
            r_wp = Res("wpool")
            ld("pool", wpool, din["w_pool"][ev].rearrange("g c d -> c g d"), s_wp, writes=[r_wp])
            s_o = S.dsem("outs")
            ld("sp", o_ps[:, 0:11, :], din["state_pool"][:, 4:15, :], s_o)
            kbuf = A_bf(21184, 2048).rearrange("p (b c) -> p b c", b=16)
            skst = tmpa[:, 6, :].rearrange("p (a c) -> p a c", a=4)
            r_kb = Res("kbuf")
            def ctx_prep():
                for hb in range(2):
                    for g in range(4):
                        S.op("pe", lambda e, g=g, hb=hb: e.transpose(out=psum[:, 6, g * 128:g * 128 + 120], in_=ctxs[0:120, hb, g * 128:(g + 1) * 128],
                                                                    identity=ident[0:120, 0:120]),
                             reads=[r_ctx[hb], r_const], writes=[r_ps[6]], signal=(g == 3))
                    S.op("dve", lambda e, hb=hb: e.tensor_copy(
                        out=Us[:, :, hb * 8:(hb + 1) * 8, 0:15],
                        in_=psum[:, 6, :].rearrange("p (g t) -> p g t", g=4)[:, :, 0:120].rearrange("p g (b r) -> p g b r", r=15)),
                        reads=[r_ps[6]], writes=r_Us)

            for hb in range(2):
                ld("sp", ctxs[0:120, hb, :], din["state_pool"][hb * 8:(hb + 1) * 8].rearrange("b r c -> (b r) c"), s_ctx[hb], writes=[r_ctx[hb]])
            s_sk = [S.dsem(f"sk{i}") for i in range(4)]
            r_sk = [Res() for _ in range(4)]

            def kbuf_prep(b4):
                for j in range(4):
                    b = b4 * 4 + j
                    ld("sp", skst[:, j, :], din["state_win_k"][b], s_sk[j], writes=[r_sk[j]])
                    S.op("pe", lambda e, j=j: e.transpose(out=psum[:, 7, j * 128:(j + 1) * 128], in_=skst[:, j, :], identity=ident[:]),
                         reads=[r_sk[j], r_const], writes=[r_ps[7]], signal=True)
                S.op("dve", lambda e: e.tensor_copy(out=kbuf[:, b4 * 4:(b4 + 1) * 4, :], in_=psum[:, 7, :].rearrange("p (a c) -> p a c", a=4)),
                     reads=[r_ps[7]], writes=[r_kb])

            if SUB < 2.2:
                return
            uns = [kunit(w_in, 768 + g * 128) for g in range(4)]
            urs = [wunit([(lambda r: r.rearrange("p (a c) -> p a c", a=2),
                           w_in[2 * i * 128:(2 * i + 2) * 128, 768:1280].rearrange("(a p) c -> p a c", p=128))]) for i in range(4)]
            s_pm = S.dsem("poolmats")
            allhid = [r for a_ in r_hid for b_ in a_ for r in b_]
            ld("pool", poolA, din["c_poolA"], s_pm, writes=[r_pm] + allhid)
            ld("pool", poolB, din["c_poolB"], s_pm, writes=[r_pm])
            pcnt = [0]

            def pool_proj(g, tt):
                t0, n = TTS[tt]
                pb = 6 + (pcnt[0] % 2)
                pcnt[0] += 1
                mm(psum[:, pb, 0:n], wpool[:, g, :], dT[:, g, t0:t0 + n], True, True, [r_wp, r_dT[g][tt]], [r_ps[pb]])
                S.op("act", lambda e: e.activation(out=dT[:, g, t0:t0 + n], in_=psum[:, pb, 0:n], func=AF.Identity,
                                                   scale=cols[:, 48 + g:49 + g]),
                     reads=[r_ps[pb], r_cols], writes=[r_dT[g][tt]])

            def pool_block_a(b):
                t0 = b * 128
                tt = b // 4
                bx = b % 2
                ui = b % 3
                for k in range(DT):
                    un = urs[k // 2]
                    mm(psum[:, bx, :], hT[:, k, t0:t0 + 128], ring[:, un, (k % 2) * 512:(k % 2 + 1) * 512], k == 0, k == DT - 1,
                       [r_ring[un], r_h[k][tt]], [r_ps[bx]])
                S.op("act", lambda e: e.copy(out=utb[:, ui, :], in_=psum[:, bx, :]), reads=[r_ps[bx]], writes=[r_utb[ui]])
                if b == 0:
                    S.op("act", lambda e: e.copy(out=ut32, in_=psum[:, bx, :]), reads=[r_ps[bx]], writes=[r_ut32])
                if b == 15:
                    S.op("act", lambda e: e.copy(out=uptok[:, 0, :], in_=psum[:, bx, :]), reads=[r_ps[bx]], writes=[r_upt])

            def pool_block_b(b):
                t0 = b * 128
                tt = b // 4
                by = 2 + (b % 2)
                ui = b % 3
                for g in range(4):
                    gc = slice(g * 128, (g + 1) * 128)
                    if b == 0:
                        mm(psum[:, by, gc], ut32[:, gc], poolA0[:, g, :], True, True, [r_ut32, r_pc], [r_ps[by]], signal=(g == 3))
                    else:
                        mm(psum[:, by, gc], utb[:, ui, gc], poolA[:, g, :], True, False, [r_utb[ui], r_pm], [r_ps[by]], signal=False)
                        mm(psum[:, by, gc], utb[:, (b - 1) % 3, gc], poolB[:, g, :], False, True, [r_utb[(b - 1) % 3], r_pm], [r_ps[by]],
                           signal=(g == 3))
                S.op("dve", lambda e: e.tensor_copy(out=dT[:, :, t0:t0 + 128], in_=psum[:, by, :].rearrange("p (g t) -> p g t", g=4)),
                     reads=[r_ps[by]], writes=[r_dT[g][tt] for g in range(4)])

            def pool_step(g, tt):
                w = 2 ** (g + 1)
                un = uns[g]
                t0, n = TTS[tt]
                pb = g
                for k in range(DT):
                    mm(psum[:, pb, 0:n], ru(un)[:, k, :], hT[:, k, t0:t0 + n], k == 0, k == DT - 1, [r_ring[un], r_h[k][tt]], [r_ps[pb]])
                Ug = Us[:, g, :, :]
                S.op("act", lambda e: e.copy(out=Ug[:, :, 15:19], in_=psum[:, pb, 0:64].rearrange("p (b t) -> p b t", t=4)),
                     reads=[r_ps[pb]], writes=[r_Us[g]])
                bufs = [Ws1, Ws2]
                src, rs = Ug, r_Us[g]
                sh = 1
                lo = 1
                for step in range(g + 1):
                    dst, rd = bufs[step % 2], r_Ws[step % 2]
                    S.op("dve", lambda e, dst=dst, src=src, lo=lo, sh=sh: e.tensor_tensor(
                        out=dst[:, :, lo:19], in0=src[:, :, lo:19], in1=src[:, :, lo - sh:19 - sh], op=ALU.add),
                        reads=[rs], writes=[rd])
                    src, rs = dst, rd
                    sh *= 2
                    lo += sh
                S.op("dve", lambda e, src=src: e.scalar_tensor_tensor(
                    out=dT[:, g, 2048:2112].rearrange("p (b t) -> p b t", t=4), in0=src[:, :, 15:19], scalar=1.0 / w,
                    in1=Ug[:, :, 15:19], op0=ALU.mult, op1=ALU.subtract),
                    reads=[rs, r_Us[g]], writes=[r_dT[g][4]])

            pool_block_a(0)
            for b in range(16):
                if b + 1 < 16:
                    pool_block_a(b + 1)
                pool_block_b(b)
                if b % 4 == 1 and b >= 5:
                    for g in range(4):
                        pool_proj(g, b // 4 - 1)
                if b % 4 == 2:
                    kbuf_prep(b // 4)
                if b == 11:
                    ctx_prep()
            for g in range(4):
                pool_step(g, 4)
            for g in range(4):
                pool_proj(g, 3)
            for g in range(4):
                for (row, t0, n, tt) in ((1, 2048, 64, 4),):
                    pb = 4 + row
                    for k in range(DT):
                        mm(psum[0:n, pb, g * 128:(g + 1) * 128], hT[:, k, t0:t0 + n], ru(uns[g])[:, k, :], k == 0, k == DT - 1,
                           [r_ring[uns[g]], r_h[k][tt]], [r_ps[pb]])
            for g in range(4):
                pool_proj(g, 4)
            for row, n in ((1, 64),):
                S.op("act", lambda e, row=row, n=n: e.copy(out=uptok[0:n, row, :], in_=psum[0:n, 4 + row, :]), reads=[r_ps[4 + row]], writes=[r_upt])
            ld("sp", o_pp[:, :], uptok[113:128, 0, :], s_o, reads=[r_upt])
            for b in range(16):
                ld("sp", o_ps[b, 11:15, :], uptok[4 * b:4 * b + 4, 1, :], s_o, reads=[r_upt])
            if SUB < 3:
                return
            q = A_bf(8448, 4 * NTOK).rearrange("p (h t) -> p h t", h=4)
            kT = A_bf(16896, NTOK)
            vtok = A_bf(19008, 17 * 128).rearrange("p (b c) -> p b c", b=17)
            vbuf = A_bf(23232, 2048).rearrange("p (b c) -> p b c", b=16)
            EBS = A_f32(25280, 512).rearrange("p (h t) -> p h t", h=8)
            tokst = tmpa[:, 7, :].rearrange("p (a c) -> p a c", a=4)
            PT = tmp_bf
            r_q = [[Res() for _ in range(5)] for _ in range(4)]
            r_k = [Res() for _ in range(5)]
            r_vt = [Res() for _ in range(17)]
            r_vb, r_ebs, r_tok = Res(), Res(), [Res() for _ in range(4)]
            S.barrier()
            s_vb = S.dsem("vbuf")
            ld("sp", o_ks[:, 0:124, :], din["state_win_k"][:, 4:128, :], s_o)
            ld("sp", o_vs[:, 0:124, :], din["state_win_v"][:, 4:128, :], s_o)
            if SUB < 3.3:
                return
            cnt = 0
            for hh in range(4):
                un = wunit([(lambda r: r.rearrange("p (k f) -> p k f", k=8)[:, :, 0:64],
                             w_in[:, hh * 64:(hh + 1) * 64].rearrange("(k p) f -> p k f", p=128)),
                            (lambda r: r.rearrange("p (k f) -> p k f", k=8)[:, :, 64:128],
                             w_in[:, (4 + hh) * 64:(5 + hh) * 64].rearrange("(k p) f -> p k f", p=128))])
                for tt, (t0, n) in enumerate(TTS):
                    pb = cnt % 4
                    cnt += 1
                    for k in range(DT):
                        mm(psum[:, pb, 0:n], ru(un)[:, k, :], hT[:, k, t0:t0 + n], k == 0, k == DT - 1, [r_ring[un], r_h[k][tt]], [r_ps[pb]])
                    S.op("act", lambda e, pb=pb, hh=hh, t0=t0, n=n: e.mul(out=q[:, hh, t0:t0 + n], in_=psum[:, pb, 0:n], mul=0.125),
                         reads=[r_ps[pb]], writes=[r_q[hh][tt]])
            if SUB < 3.4:
                return
            un = kunit(w_in, 512)
            for tt, (t0, n) in enumerate(TTS):
                pb = cnt % 4
                cnt += 1
                for k in range(DT):
                    mm(psum[:, pb, 0:n], ru(un)[:, k, :], hT[:, k, t0:t0 + n], k == 0, k == DT - 1, [r_ring[un], r_h[k][tt]], [r_ps[pb]])
                S.op("dve", lambda e, pb=pb, t0=t0, n=n: e.tensor_copy(out=kT[:, t0:t0 + n], in_=psum[:, pb, 0:n]), reads=[r_ps[pb]], writes=[r_k[tt]])
            for (row, t0, n, tt) in ((0, 1920, 128, 3), (1, 2048, 64, 4)):
                for k in range(DT):
                    mm(psum[0:n, 4, row * 128:(row + 1) * 128], hT[:, k, t0:t0 + n], ru(un)[:, k, :], k == 0, k == DT - 1,
                       [r_ring[un], r_h[k][tt]], [r_ps[4]])
                S.op("act", lambda e, row=row, n=n: e.copy(out=tokst[0:n, row, :], in_=psum[0:n, 4, row * 128:(row + 1) * 128]),
                     reads=[r_ps[4]], writes=[r_tok[row]])
            s_tk = S.dsem("tokouts")
            ld("sp", o_kp[:, :], tokst[:, 0, :], s_tk, reads=[r_tok[0]])
            for b in range(16):
                ld("sp", o_ks[b, 124:128, :], tokst[4 * b:4 * b + 4, 1, :], s_tk, reads=[r_tok[1]])
            if SUB < 3.5:
                return
            un = kunit(w_in, 640)
            ld("pool", vbuf, din["state_win_v"].rearrange("b k c -> k b c"), s_vb, writes=[r_vb])
            for bi, (t0, n) in enumerate(BLKS):
                tt = min(t0 // 512, 4)
                pb = 5 + (bi % 2) * 2
                for k in range(DT):
                    mm(psum[0:n, pb, 0:128], hT[:, k, t0:t0 + n], ru(un)[:, k, :], k == 0, k == DT - 1, [r_ring[un], r_h[k][tt]], [r_ps[pb]])
                S.op("dve", lambda e, pb=pb, bi=bi, n=n: e.tensor_copy(out=vtok[0:n, bi, :], in_=psum[0:n, pb, 0:128]), reads=[r_ps[pb]], writes=[r_vt[bi]])
                if bi >= 15:
                    row = 2 + (bi - 15)
                    S.op("dve", lambda e, pb=pb, row=row, n=n: e.tensor_copy(out=tokst[0:n, row, :], in_=psum[0:n, pb, 0:128]), reads=[r_ps[pb]], writes=[r_tok[row]])
            if not os.environ.get("MK_NOVOUT"):
                ld("sp", o_vp[:, :], tokst[:, 2, :], s_tk, reads=[r_tok[2]])
                for b in range(16):
                    ld("sp", o_vs[b, 124:128, :], tokst[4 * b:4 * b + 4, 3, :], s_tk, reads=[r_tok[3]])
            S.op("dve", lambda e: e.memset(EBS[0:64, :, :], 0.0), writes=[r_ebs])
            s_ebs = S.dsem("ebs")
            for b in range(16):
                ld("sp", EBS[4 * b:4 * b + 4, :, 4 * b:4 * b + 4], EB[0:4, 0, :, 0:4], s_ebs, reads=[r_EB], writes=[r_ebs])
            attnT = hT
            etc = [0]

            PTB = tmpa[:, 6:8, :].bitcast(BF16).rearrange("p a (b f) -> p (a b) f", b=2)
            PTs = (PT, PTB)
            tok_all = list(r_tok)

            def pt_res(par, pb):
                return r_tmp[(4 if par == 0 else 6) + pb // 2]

            def att_scores(bi):
                t0 = bi * 128
                tt = min(t0 // 512, 4)
                par = bi % 2
                kbs = [(0, bi)] + ([(1, bi - 1)] if bi > 0 else [])
                for g in range(2):
                    gp = slice(g * 64, (g + 1) * 64)
                    for (kb, kblk) in kbs:
                        pb = g * 2 + kb
                        k0 = kblk * 128
                        ktt = min(k0 // 512, 4)
                        mm(psum[:, pb, :], kT[gp, k0:k0 + 128], q[gp, :, t0:t0 + 128], True, True,
                           [r_k[ktt]] + [r_q[hh][tt] for hh in range(4)], [r_ps[pb]])
                        ti = etc[0] % 4
                        etc[0] += 1
                        et = tmpa[:, ti, :] if ti < 2 else tmpb[:, ti - 2, :]
                        r_et = r_tmp[ti] if ti < 2 else r_tmpb[ti - 2]
                        S.op("act", lambda e, et=et, pb=pb: e.activation(out=et, in_=psum[:, pb, :], func=AF.Exp), reads=[r_ps[pb]], writes=[r_et])
                        S.op("dve", lambda e, et=et, pb=pb, kb=kb, g=g, par=par: e.tensor_tensor(
                            out=PTs[par][:, pb, :].rearrange("p (h t) -> p h t", h=4), in0=et.rearrange("p (h t) -> p h t", h=4),
                            in1=EB[:, kb, 4 * g:4 * g + 4, :], op=ALU.mult),
                            reads=[r_et, r_EB], writes=[pt_res(par, pb)] + (tok_all if par == 1 else []))

            def att_av(bi):
                t0 = bi * 128
                tt = min(t0 // 512, 4)
                par = bi % 2
                bo, bd = (4, 5) if par == 0 else (6, 7)
                kbs = [(0, bi)] + ([(1, bi - 1)] if bi > 0 else [])
                for (bank, is_den) in ((bo, False), (bd, True)):
                    for g in range(2):
                        gp = slice(g * 64, (g + 1) * 64)
                        for i, (kb, kblk) in enumerate(kbs):
                            pb = g * 2 + kb
                            lhs = ones_b[:, 0:64] if is_den else vtok[:, kblk, g * 64:(g + 1) * 64]
                            mm(psum[gp, bank, :], lhs, PTs[par][:, pb, :], i == 0, i == len(kbs) - 1,
                               [r_const if is_den else r_vt[kblk], pt_res(par, pb)], [r_ps[bank]], signal=(g == 1 and i == len(kbs) - 1))
                ds = 2 + par
                den = tmpa[:, ds, :]
                S.op("dve", lambda e, den=den, bd=bd: e.tensor_tensor(out=den, in0=psum[:, bd, :], in1=es_bc[:].rearrange("p h t -> p (h t)"), op=ALU.add),
                     reads=[r_ps[bd], r_es], writes=[r_tmp[ds]])
                S.op("act", lambda e, den=den: e.activation(out=den, in_=den, func=AF.Ln), reads=[r_tmp[ds]], writes=[r_tmp[ds]])
                S.op("act", lambda e, den=den: e.activation(out=den, in_=den, func=AF.Exp, scale=-1.0), reads=[r_tmp[ds]], writes=[r_tmp[ds]])
                S.op("dve", lambda e, t0=t0, den=den, bo=bo: e.tensor_tensor(out=attnT[:, 0:4, t0:t0 + 128], in0=psum[:, bo, :].rearrange("p (h t) -> p h t", h=4),
                                                                          in1=den.rearrange("p (h t) -> p h t", h=4), op=ALU.mult),
                     reads=[r_ps[bo], r_tmp[ds]], writes=[r_h[hh][tt] for hh in range(4)])

            if SUB < 4:
                return
            att_scores(0)
            for bi in range(16):
                if bi + 1 < 16:
                    att_scores(bi + 1)
                att_av(bi)
            if SUB < 5:
                return

            def sample_attn():
                t0, n, tt = 2048, 64, 4
                v4 = lambda ap: ap.rearrange("p (b h t) -> p b h t", b=16, h=4)
                for b in range(16):
                    for g in range(2):
                        gp = slice(g * 64, (g + 1) * 64)
                        mm(psum[:, g, b * 16:(b + 1) * 16], kbuf[gp, b, :], q[gp, :, t0 + 4 * b:t0 + 4 * b + 4], True, True,
                           [r_kb] + [r_q[hh][4] for hh in range(4)], [r_ps[g]], signal=(b == 15))
                for g in range(2):
                    gp = slice(g * 64, (g + 1) * 64)
                    mm(psum[0:64, 2 + g, 0:256], kT[gp, t0:t0 + 64], q[gp, :, t0:t0 + 64].rearrange("p h (b t) -> p b h t", t=4), True, True,
                       [r_k[4]] + [r_q[hh][4] for hh in range(4)], [r_ps[2 + g]], signal=True)
                et0, et1 = tmpa[:, 0, :], tmpa[:, 1, :]
                for g in range(2):
                    cs = slice(g * 256, (g + 1) * 256)
                    S.op("act", lambda e, g=g, cs=cs: e.activation(out=et0[:, cs], in_=psum[:, g, 0:256], func=AF.Exp),
                         reads=[r_ps[g]], writes=[r_tmp[0]])
                    S.op("dve", lambda e, g=g, cs=cs: e.tensor_tensor(
                        out=v4(PT[:, 0, cs]), in0=v4(et0[:, cs]),
                        in1=EB[:, 1, 4 * g:4 * g + 4, 0:4].unsqueeze(1).broadcast_to([128, 16, 4, 4]), op=ALU.mult),
                        reads=[r_tmp[0], r_EB], writes=[r_tmp[4]])
                for g in range(2):
                    cs = slice(g * 256, (g + 1) * 256)
                    S.op("act", lambda e, g=g, cs=cs: e.activation(out=et1[0:64, cs], in_=psum[0:64, 2 + g, 0:256], func=AF.Exp),
                         reads=[r_ps[2 + g]], writes=[r_tmp[1]])
                    S.op("dve", lambda e, g=g, cs=cs: e.tensor_tensor(
                        out=v4(PT[0:64, 2, cs]), in0=v4(et1[0:64, cs]),
                        in1=EBS[0:64, 4 * g:4 * g + 4, :].rearrange("p h (b t) -> p b h t", t=4), op=ALU.mult),
                        reads=[r_tmp[1], r_ebs], writes=[r_tmp[5]])
                for (bank, is_den) in ((4, False), (5, True)):
                    for g in range(2):
                        gp = slice(g * 64, (g + 1) * 64)
                        lhs_c = ones_b[0:64, 0:64] if is_den else vtok[0:64, 16, g * 64:(g + 1) * 64]
                        mm(psum[gp, bank, 0:256], lhs_c, PT[0:64, 2, g * 256:(g + 1) * 256], True, False,
                           [r_vt[16], r_tmp[5], r_const], [r_ps[bank]], signal=False)
                        for b in range(16):
                            lhs_p = ones_b[:, 0:64] if is_den else vbuf[:, b, g * 64:(g + 1) * 64]
                            mm(psum[gp, bank, b * 16:(b + 1) * 16], lhs_p, PT[:, 0, g * 256 + b * 16:g * 256 + (b + 1) * 16], False, b == 15,
                               [r_vb, r_tmp[4], r_const], [r_ps[bank]], signal=(b == 15 and g == 1))
                den = tmpa[:, 3, 0:256]
                S.op("dve", lambda e: e.tensor_tensor(out=v4(den), in0=v4(psum[:, 5, 0:256]),
                                                      in1=es_bc[:, :, 0:4].unsqueeze(1).broadcast_to([128, 16, 4, 4]), op=ALU.add),
                     reads=[r_ps[5], r_es], writes=[r_tmp[3]])
                S.op("dve", lambda e: e.reciprocal(out=den, in_=den), reads=[r_tmp[3]], writes=[r_tmp[3]])
                S.op("dve", lambda e: e.tensor_tensor(
                    out=attnT[:, 0:4, t0:t0 + 64].rearrange("p h (b t) -> p h b t", t=4),
                    in0=psum[:, 4, 0:256].rearrange("p (b h t) -> p h b t", b=16, h=4),
                    in1=den.rearrange("p (b h t) -> p h b t", b=16, h=4), op=ALU.mult),
                    reads=[r_ps[4], r_tmp[3]], writes=[r_h[hh][4] for hh in range(4)])

            sample_attn()
            if SUB < 6:
                return
            ous = []
            for d in range(DT):
                cs = slice(d * 128, (d + 1) * 128)
                ous.append(wunit([
                    (lambda r: r.rearrange("p (k f) -> p k f", k=8)[0:64, 0:4, :], w_out[0:256, cs].rearrange("(hh dd) c -> dd hh c", dd=64)),
                    (lambda r: r.rearrange("p (k f) -> p k f", k=8)[64:128, 0:4, :], w_out[256:512, cs].rearrange("(hh dd) c -> dd hh c", dd=64)),
                    (lambda r: r.rearrange("p (k f) -> p k f", k=8)[:, 4:8, :], w_out[512:1024, cs].rearrange("(g p) c -> p g c", p=128)),
                ]))
            ocnt = [0]

            def oproj_tile(tt):
                t0, n = TTS[tt]
                for d in range(DT):
                    pb = (4, 5, 0, 1)[ocnt[0] % 4]
                    ocnt[0] += 1
                    for k in range(8):
                        rhs = attnT[:, k, t0:t0 + n] if k < 4 else dT[:, k - 4, t0:t0 + n]
                        rr = r_h[k][tt] if k < 4 else r_dT[k - 4][tt]
                        mm(psum[:, pb, 0:n], ru(ous[d])[:, k, :], rhs, k == 0, k == 7, [r_ring[ous[d]], rr], [r_ps[pb]])
                    evac_add_x(pb, d, tt, t0, n)

            fused_tail(oproj_tile, nxt)
            S.barrier()

        fg = tmpa[:, 0:2, :].rearrange("p a f -> p (a f)")
        s_fg = S.dsem("fg")
        osb = arena[:, 21120:21120 + 4096].bitcast(F32).rearrange("p (a f) -> p a f", a=2)
        r_os = [Res("os0"), Res("os1")]
        s_os = [S.dsem("os0"), S.dsem("os1")]
        r_fs = [Res(), Res()]
        junkf = tmpa[:, 2:4, :]

        def final_setup():
            ld("sp", fg, din["final_gain"].partition_broadcast(128), s_fg, writes=[r_tmp[0], r_tmp[1]])

        def final_tile(tile):
            blks = [16] if tile == 4 else list(range(4 * tile, 4 * tile + 4))
            for bi in blks:
                t0, n = BLKS[bi]
                sl = bi % 2
                tt = min(t0 // 512, 4)
                pbs = (6, 7) if bi % 2 == 0 else (2, 3)
                ssq = tmpa[:, 6, 16 * sl:16 * sl + 16]
                for half in range(2):
                    pb = pbs[half]
                    for j in range(4):
                        d = half * 4 + j
                        S.op("pe", lambda e, pb=pb, j=j, d=d, n=n, t0=t0: e.transpose(
                            out=psum[0:n, pb, j * 128:(j + 1) * 128], in_=xT[:, d, t0:t0 + n], identity=ident[:]),
                            reads=[r_x[d][tt], r_const], writes=[r_ps[pb]], signal=(j == 3))
                S.op("act", lambda e, n=n, p0=pbs[0], ssq=ssq: e.activation(
                    out=junkf[0:n, :, :], in_=psum[0:n, p0:p0 + 2, :], func=AF.Square, accum_out=ssq[0:n, 0:1]),
                    reads=[r_ps[pbs[0]], r_ps[pbs[1]]], writes=[r_fs[sl]])
                S.op("act", lambda e, n=n, ssq=ssq: e.activation(out=ssq[0:n, 1:2], in_=ssq[0:n, 0:1], func=AF.Sqrt, bias=epsc[0:n, :], scale=1.0 / D),
                     reads=[r_fs[sl], r_const], writes=[r_fs[sl]])
                S.op("dve", lambda e, n=n, ssq=ssq: e.reciprocal(out=ssq[0:n, 1:2], in_=ssq[0:n, 1:2]), reads=[r_fs[sl]], writes=[r_fs[sl]])
                for half in range(2):
                    pb = pbs[half]
                    S.op("dve", lambda e, pb=pb, n=n, half=half, sl=sl, ssq=ssq: e.scalar_tensor_tensor(
                        out=osb[0:n, sl, half * 512:(half + 1) * 512], in0=psum[0:n, pb, :], scalar=ssq[0:n, 1:2],
                        in1=fg[0:n, half * 512:(half + 1) * 512], op0=ALU.mult, op1=ALU.mult),
                        reads=[r_ps[pb], r_fs[sl], r_tmp[0], r_tmp[1]], writes=[r_os[sl]])
                dst = o_yp[t0:t0 + n, :] if bi < 16 else o_ys[:, :]
                ld("sp", dst, osb[0:n, sl, :], s_os[sl], reads=[r_os[sl]])

        if STAGE >= 99:
            load_x(norm=(0, 0), pre_cb=prep_bias)
            prep_bias_b()
            ffn(0, 0, nxt=(0, 1))
            even_mixer(0, nxt=(0, 2))
            ffn(0, 1, nxt=(1, 0))
            ffn(1, 0, nxt=(1, 1))
            odd_mixer(1, nxt=(1, 2))
            ffn(1, 1, after_norm=final_setup, extra=final_tile)
        else:
            if STAGE == 4:
                prep_bias()
            load_x()
            if STAGE == 4:
                prep_bias_b()
            if STAGE == 2:
                ffn(0, 0)
            elif STAGE == 3:
                odd_mixer(1)
            elif STAGE == 4:
                even_mixer(0)
            elif STAGE == 6:
                ffn(0, 0, nxt=(0, 2))
                ffn(0, 1)
            elif STAGE == 5:
                ffn(0, 0); ffn(0, 1); ffn(1, 0); ffn(1, 1)
            S.barrier()
            final_setup()
            for tile in range(5):
                final_tile(tile)

        S.emit(nc, st)
    return nc


_PROG = None


def kernel(**inputs):
    global _PROG
    if _PROG is None:
        _PROG = build_program()
    nc = _PROG
    consts = host_consts()
    f = lambda a: np.ascontiguousarray(np.asarray(a, dtype=np.float32))
    wts = {n_: f(inputs[n_]) for n_, s_ in W_SPECS}
    in_maps = []
    for c in range(NCORES):
        m = {}
        m["x_prompt"] = f(inputs["x_prompt"][c])
        m["x_sample"] = f(inputs["x_sample"][c * 16:(c + 1) * 16]).reshape(NS, D)
        m["state_win_k"] = f(inputs["state_win_k"][0, c * 16:(c + 1) * 16]).reshape(16, 128, 128)
        m["state_win_v"] = f(inputs["state_win_v"][0, c * 16:(c + 1) * 16]).reshape(16, 128, 128)
        m["state_pool"] = f(inputs["state_pool"][0, c * 16:(c + 1) * 16])
        for n_, s_ in W_SPECS:
            m[n_] = wts[n_]
        for n_ in CONST_SHAPES:
            m["c_" + n_] = consts[n_]
        in_maps.append(m)
    res = run_bass_kernel_spmd(nc, in_maps, core_ids=list(range(NCORES)))
    R = res.results
    y_prompt = np.stack([R[c]["y_prompt"] for c in range(NCORES)], 0)
    y_sample = np.concatenate([R[c]["y_sample"].reshape(16, 4, D) for c in range(NCORES)], 0)
    nk_p = np.stack([R[c]["nk_p"].reshape(128, 2, 64) for c in range(NCORES)], 0)[None]
    nv_p = np.stack([R[c]["nv_p"].reshape(128, 2, 64) for c in range(NCORES)], 0)[None]
    np_p = np.stack([R[c]["np_p"] for c in range(NCORES)], 0)[None]
    nk_s = np.concatenate([R[c]["nk_s"].reshape(16, 128, 2, 64) for c in range(NCORES)], 0)[None]
    nv_s = np.concatenate([R[c]["nv_s"].reshape(16, 128, 2, 64) for c in range(NCORES)], 0)[None]
    np_s = np.concatenate([R[c]["np_s"] for c in range(NCORES)], 0)[None]
    sgu_v = np.concatenate([R[c]["sgu_v"].reshape(16, 4, D) for c in range(NCORES)], 0)[None]
    return (y_prompt, y_sample, nk_p, nv_p, np_p, nk_s, nv_s, np_s, sgu_v)
```

```python
import os
import numpy as np
from contextlib import ExitStack
import concourse.bass as bass
import concourse.mybir as mybir
from concourse.bass_utils import run_bass_kernel_spmd

F32 = mybir.dt.float32
BF16 = mybir.dt.bfloat16
AF = mybir.ActivationFunctionType
ALU = mybir.AluOpType

NCORES = 8
D = 1024
DT = 8
FF = 2816
FT = 22
SEQ = 2048
NS = 64
NB_S = 16
NTOK = SEQ + NS
TTS = [(0, 512), (512, 512), (1024, 512), (1536, 512), (2048, 64)]
BLKS = [(b * 128, 128) for b in range(16)] + [(2048, 64)]
EPS = 1e-6
STAGE = int(os.environ.get("MK_STAGE", "99"))
SUB = float(os.environ.get("MK_SUB", "99"))

ENGS = ("pe", "act", "dve", "pool", "sp")


class Res:
    __slots__ = ("name", "lw", "rd")

    def __init__(self, name=""):
        self.name = name
        self.lw = None
        self.rd = []


class DmaSem:
    __slots__ = ("name", "cnt", "h", "q")

    def __init__(self, name):
        self.name = name
        self.cnt = 0
        self.h = None
        self.q = None


class Sched:
    def __init__(self):
        self.q = {e: [] for e in ENGS}
        self.cnt = {e: 0 for e in ENGS}
        self.waited = {e: {} for e in ENGS}
        self.dsems = []

    def dsem(self, name):
        s = DmaSem(name)
        self.dsems.append(s)
        return s

    def _collect(self, eng, reads, writes, is_dma=False):
        need = {}
        for r in reads:
            if r.lw is not None:
                k, v = r.lw
                if v > need.get(k, 0):
                    need[k] = v
        for w in writes:
            if w.lw is not None:
                k, v = w.lw
                if v > need.get(k, 0):
                    need[k] = v
            for (k, v) in w.rd:
                if v > need.get(k, 0):
                    need[k] = v
        out = []
        wd = self.waited[eng]
        for k, v in need.items():
            if k == "pe" and eng == "pe" and not is_dma:
                continue
            if wd.get(k, 0) >= v:
                continue
            wd[k] = v
            out.append((k, v))
        return out

    def op(self, eng, fn, reads=(), writes=(), signal=True):
        waits = self._collect(eng, reads, writes)
        if signal:
            self.cnt[eng] += 1
            val = self.cnt[eng]
        else:
            val = self.cnt[eng] + 1
        e = (eng, val)
        for r in reads:
            r.rd.append(e)
        for w in writes:
            w.lw = e
            w.rd = []
        self.q[eng].append((waits, fn, (eng, 1) if signal else None))

    def dma(self, qeng, fn, sem, reads=(), writes=()):
        waits = self._collect(qeng, reads, writes, is_dma=True)
        assert sem.q in (None, qeng)
        sem.q = qeng
        sem.cnt += 16
        e = (sem, sem.cnt)
        for r in reads:
            r.rd.append(e)
        for w in writes:
            w.lw = e
            w.rd = []
        self.q[qeng].append((waits, fn, (sem, 16)))

    def barrier(self, engs=("pe", "act", "dve", "sp"), skip=()):
        ev = [(e, self.cnt[e]) for e in engs if e != "sp" and self.cnt[e] > 0]
        ev += [(s, s.cnt) for s in self.dsems if s.cnt > 0 and s.q in engs and s not in skip]
        for e in engs:
            wd = self.waited[e]
            waits = []
            for k, v in ev:
                if k == e:
                    continue
                if wd.get(k, 0) >= v:
                    continue
                wd[k] = v
                waits.append((k, v))
            if waits:
                self.q[e].append((waits, None, None))

    def emit(self, nc, stack):
        esem = {}
        for e in ("pe", "act", "dve", "pool"):
            esem[e] = stack.enter_context(nc.semaphore("s_" + e))
        for s in self.dsems:
            s.h = stack.enter_context(nc.semaphore("d_" + s.name))
        fin = [(s, s.cnt) for s in self.dsems if s.cnt > 0]
        for e in ("pe", "act", "dve", "pool"):
            if self.cnt[e] > 0:
                fin.append((e, self.cnt[e]))

        def hof(k):
            return esem[k] if isinstance(k, str) else k.h

        def run(eng_name, eng):
            for waits, fn, sig in self.q[eng_name]:
                for k, v in waits:
                    eng.wait_ge(hof(k), v)
                if fn is None:
                    continue
                inst = fn(eng)
                if sig is not None:
                    inst.then_inc(hof(sig[0]), sig[1])
            if eng_name == "sp":
                for k, v in fin:
                    eng.wait_ge(hof(k), v)

        block = stack.enter_context(nc.Block())

        @block.tensor
        def _(e):
            run("pe", e)

        @block.scalar
        def _(e):
            run("act", e)

        @block.vector
        def _(e):
            run("dve", e)

        @block.gpsimd
        def _(e):
            run("pool", e)

        @block.sync
        def _(e):
            run("sp", e)


def t5_bucket_np(n):
    n = np.maximum(n, 0)
    max_exact = 16
    nf = np.maximum(n, 1).astype(np.float32)
    large = max_exact + (np.log(nf / np.float32(max_exact)) / np.float32(np.log(128 / max_exact))
                         * np.float32(32 - max_exact)).astype(np.int32)
    large = np.minimum(large, 31)
    return np.where(n < max_exact, n, large)


def host_consts():
    c = {}
    c["ident"] = np.eye(128, dtype=np.float32)
    oh = np.zeros((32, 384), np.float32)
    n = np.arange(384) - 128
    valid = (n >= 0) & (n < 128)
    bk = t5_bucket_np(n)
    oh[bk[valid], np.arange(384)[valid]] = 1.0
    c["onehot"] = oh
    c["trimask"] = np.triu(np.ones((128, 128), np.float32))
    m = np.zeros((64, 64), np.float32)
    for b in range(16):
        for s in range(4):
            for t in range(s, 4):
                m[4 * b + s, 4 * b + t] = 1.0
    c["bdmask"] = m
    ic = np.zeros((128, 4, 16), np.float32)
    for g, w in enumerate((2, 4, 8, 16)):
        ic[:, g, :] = 1.0 / np.minimum(np.arange(16) + 1, w)
    c["invc"] = ic
    sel = np.zeros((4, 64), np.float32)
    for b in range(16):
        for t in range(4):
            sel[t, 4 * b + t] = 1.0
    c["sel4"] = sel
    A = np.zeros((128, 4, 128), np.float32)
    B = np.zeros((128, 4, 128), np.float32)
    A0 = np.zeros((128, 4, 128), np.float32)
    for g, w in enumerate((2, 4, 8, 16)):
        for t in range(128):
            for sx in range(t - w + 1, t + 1):
                if sx >= 0:
                    A[sx, g, t] += 1.0 / w
                    A0[sx, g, t] += 1.0 / min(t + 1, w)
                else:
                    B[128 + sx, g, t] += 1.0 / w
            A[t, g, t] -= 1.0
            A0[t, g, t] -= 1.0
    c["poolA"], c["poolB"], c["poolA0"] = A, B, A0
    return c


CONST_SHAPES = {"ident": [128, 128], "onehot": [32, 384], "trimask": [128, 128],
                "bdmask": [64, 64], "invc": [128, 4, 16], "sel4": [4, 64],
                "poolA": [128, 4, 128], "poolB": [128, 4, 128], "poolA0": [128, 4, 128]}

W_SPECS = [("rel_bias", [32, 8]), ("norm_gains", [2, 3, 1024]), ("final_gain", [1024]),
           ("ffn_gate", [2, 2, 1024, 2816]), ("ffn_up", [2, 2, 1024, 2816]), ("ffn_down", [2, 2, 2816, 1024]),
           ("w_in_even", [1, 1024, 1280]), ("w_out_even", [1, 1024, 1024]), ("attn_sinks", [1, 8]),
           ("w_pool", [1, 4, 128, 128]), ("pool_scale", [1, 512]), ("w_in_odd", [1, 1024, 2048]),
           ("sgu_norm", [1, 1024]), ("w_spatial", [1, 4, 128, 128]), ("b_spatial", [1, 4, 128]),
           ("w_out_odd", [1, 1024, 1024])]


def build_program():
    nc = bass.Bass("TRN2", target_bir_lowering=False)
    din = {}

    def inp(name, shape):
        din[name] = nc.dram_tensor(name, list(shape), F32, kind="ExternalInput").ap()

    inp("x_prompt", [SEQ, D])
    inp("x_sample", [NS, D])
    inp("state_win_k", [NB_S, 128, 128])
    inp("state_win_v", [NB_S, 128, 128])
    inp("state_pool", [NB_S, 15, 512])
    for n_, s_ in W_SPECS:
        inp(n_, s_)
    for n_, s_ in CONST_SHAPES.items():
        inp("c_" + n_, s_)

    def outp(name, shape):
        return nc.dram_tensor(name, list(shape), F32, kind="ExternalOutput").ap()

    o_yp = outp("y_prompt", [SEQ, D])
    o_ys = outp("y_sample", [NS, D])
    o_kp = outp("nk_p", [128, 128])
    o_vp = outp("nv_p", [128, 128])
    o_pp = outp("np_p", [15, 512])
    o_ks = outp("nk_s", [NB_S, 128, 128])
    o_vs = outp("nv_s", [NB_S, 128, 128])
    o_ps = outp("np_s", [NB_S, 15, 512])
    o_sv = outp("sgu_v", [NS, D])
    scr = nc.dram_tensor("scr_bias", [8, 49664], F32, kind="Internal").ap()

    S = Sched()
    st = ExitStack()
    with st:
        sb = lambda name, shape, dt: st.enter_context(nc.sbuf_tensor(name, shape, dt))
        xT = sb("xT", [128, DT, NTOK], F32)
        hT = sb("hT", [128, DT, NTOK], BF16)
        ARENA_E = 26624
        arena = sb("arena", [128, ARENA_E], BF16)
        NRING = 12
        ring = sb("ring", [128, NRING, 1024], BF16)
        tmpa = sb("tmpa", [128, 8, 512], F32)
        ident = sb("ident", [128, 128], F32)
        ones_f = sb("ones_f", [128, 128], F32)
        ones_b = sb("ones_b", [128, 128], BF16)
        cols = sb("cols", [128, 64], F32)
        epsc = sb("epsc", [128, 1], F32)
        EB = sb("EB", [128, 2, 8, 128], F32)
        psum = st.enter_context(nc.psum_tensor("psum", [128, 8, 512], F32))

        r_ps = [Res(f"ps{i}") for i in range(8)]
        r_x = [[Res(f"x{d}_{t}") for t in range(5)] for d in range(DT)]
        r_h = [[Res(f"h{d}_{t}") for t in range(5)] for d in range(DT)]
        r_tmp = [Res(f"tmp{i}") for i in range(8)]
        r_const = Res("const")
        r_cols = Res("cols")
        r_ring = [Res(f"ring{i}") for i in range(NRING)]
        s_ring = [S.dsem(f"ring{i}") for i in range(NRING)]
        ring_pos = [0]

        def ps(b):
            return psum[:, b, :]

        def ld(qe, out, in_, sem, writes=(), reads=()):
            S.dma(qe, lambda e: e.dma_start(out=out, in_=in_), sem, reads=reads, writes=writes)

        def wunit(pairs):
            i = ring_pos[0] % NRING
            ring_pos[0] += 1
            for (dst_fn, src) in pairs:
                ld("pool", dst_fn(ring[:, i, :]), src, s_ring[i], writes=[r_ring[i]])
            return i

        def mm(out, lhsT, rhs, start, stop, reads, writes, signal=None):
            if signal is None:
                signal = stop
            S.op("pe", lambda e: e.matmul(out, lhsT=lhsT, rhs=rhs, start=start, stop=stop),
                 reads=reads, writes=writes, signal=signal)

        s_c = S.dsem("consts")
        ld("sp", ident[:], din["c_ident"], s_c, writes=[r_const])
        S.op("dve", lambda e: e.memset(ones_f[:], 1.0), writes=[r_const])
        S.op("dve", lambda e: e.memset(ones_b[:], 1.0), writes=[r_const])
        S.op("dve", lambda e: e.memset(epsc[:], EPS), writes=[r_const])
        prow = tmpa[:, 0, :].bitcast(F32)[0:52, 0:128]
        s_p = S.dsem("prow")
        ld("sp", prow[0:48, :], din["norm_gains"].rearrange("l i (dt p) -> (l i dt) p", p=128), s_p, writes=[r_tmp[0]])
        ld("sp", prow[48:52, :], din["pool_scale"].rearrange("o (g p) -> (o g) p", p=128), s_p, writes=[r_tmp[0]])
        S.op("pe", lambda e: e.transpose(out=psum[:, 7, 0:52], in_=prow, identity=ident[0:52, 0:52]),
             reads=[r_tmp[0], r_const], writes=[r_ps[7]])
        S.op("dve", lambda e: e.tensor_copy(out=cols[:, 0:52], in_=psum[:, 7, 0:52]), reads=[r_ps[7]], writes=[r_cols])

        def gain_col(l, i, d):
            j = (l * 3 + i) * 8 + d
            return cols[:, j:j + 1]

        NXS = 8
        xs = arena[:, 0:2048 * NXS].bitcast(F32).rearrange("p (a f) -> p a f", a=NXS)
        r_xs = [Res(f"xs{i}") for i in range(NXS)]
        s_xs = [S.dsem(f"xs{i}") for i in range(NXS)]

        def load_x(norm=None, pre_cb=None):
            for bi, (t0, n) in enumerate(BLKS):
                sl = bi % NXS
                src = din["x_prompt"][t0:t0 + n, :] if bi < 16 else din["x_sample"][:, :]
                ld("sp", xs[0:n, sl, :], src, s_xs[sl], writes=[r_xs[sl]])
                tt = min(t0 // 512, 4)
                for half in range(2):
                    pb = 4 + half
                    for j in range(4):
                        d = half * 4 + j
                        S.op("pe", lambda e, pb=pb, j=j, d=d, n=n, sl=sl: e.transpose(
                            out=psum[:, pb, j * 128:j * 128 + n], in_=xs[0:n, sl, d * 128:(d + 1) * 128],
                            identity=ident[0:n, 0:n]),
                            reads=[r_xs[sl], r_const], writes=[r_ps[pb]], signal=(j == 3))
                    dst = xT[:, half * 4:(half + 1) * 4, t0:t0 + n]
                    srcp = psum[:, pb, :].rearrange("p (j t) -> p j t", j=4)[:, :, 0:n]
                    wr = [r_x[half * 4 + j][tt] for j in range(4)]
                    if half == 0:
                        S.op("dve", lambda e, dst=dst, srcp=srcp: e.tensor_copy(out=dst, in_=srcp), reads=[r_ps[pb]], writes=wr)
                    else:
                        S.op("act", lambda e, dst=dst, srcp=srcp: e.copy(out=dst, in_=srcp), reads=[r_ps[pb]], writes=wr)
                if bi == 3 and pre_cb is not None:
                    pre_cb()
                if norm is not None and (bi % 4 == 3 or bi == 16):
                    norm_stats(tt)
                    if tt >= 1:
                        norm_apply(norm[0], norm[1], tt - 1)
            if norm is not None:
                norm_apply(norm[0], norm[1], 4)
                normed[0] = tuple(norm)

        sq_bf = tmpa[:, 0:2, :].bitcast(BF16)

        def norm_stats(tt):
            t0, n = TTS[tt]
            pb = 6 + (tt % 2)
            for d in range(DT):
                ti = d % 2
                sq = sq_bf[:, ti, 0:n]
                S.op("act", lambda e, sq=sq, d=d: e.activation(out=sq, in_=xT[:, d, t0:t0 + n], func=AF.Square),
                     reads=[r_x[d][tt]], writes=[r_tmp[ti]])
                mm(psum[:, pb, 0:n], ones_b[:], sq, d == 0, d == DT - 1, [r_tmp[ti], r_const], [r_ps[pb]], signal=True)
            ri = 2 + (tt % 2)
            sd = tmpa[:, ri, 0:n]
            S.op("act", lambda e: e.activation(out=sd, in_=psum[:, pb, 0:n], func=AF.Ln, bias=epsc[:], scale=1.0 / D),
                 reads=[r_ps[pb], r_const], writes=[r_tmp[ri]])
            S.op("act", lambda e: e.activation(out=sd, in_=sd, func=AF.Exp, scale=-0.5), reads=[r_tmp[ri]], writes=[r_tmp[ri]])

        def norm_apply(l, i, tt):
            t0, n = TTS[tt]
            ri = 2 + (tt % 2)
            sd = tmpa[:, ri, 0:n]
            for d in range(DT):
                S.op("dve", lambda e, d=d: e.scalar_tensor_tensor(
                    out=hT[:, d, t0:t0 + n], in0=xT[:, d, t0:t0 + n], scalar=gain_col(l, i, d), in1=sd,
                    op0=ALU.mult, op1=ALU.mult),
                    reads=[r_x[d][tt], r_tmp[ri], r_cols], writes=[r_h[d][tt]])

        normed = [None]

        def norm_to_h(l, i):
            if normed[0] == (l, i):
                normed[0] = None
                return
            assert normed[0] is None
            for tt in range(5):
                norm_stats(tt)
                norm_apply(l, i, tt)

        def fused_tail(emit_tile, nxt, extra=None):
            for tt in range(5):
                emit_tile(tt)
                if nxt is not None:
                    if tt >= 1:
                        norm_stats(tt - 1)
                    if tt >= 2:
                        norm_apply(nxt[0], nxt[1], tt - 2)
                if extra is not None and tt >= 1:
                    extra(tt - 1)
            if nxt is not None:
                norm_stats(4)
                norm_apply(nxt[0], nxt[1], 3)
                norm_apply(nxt[0], nxt[1], 4)
                normed[0] = tuple(nxt)
            if extra is not None:
                extra(4)

        CH = 5
        hid = arena[:, 0:2 * CH * NTOK].rearrange("p (b c t) -> p b c t", b=2, c=CH)
        r_hid = [[[Res() for _ in range(5)] for _ in range(CH)] for _ in range(2)]

        def ffn(l, j, nxt=None, mid_cb=None, after_norm=None, extra=None):
            wg, wu, wd = din["ffn_gate"][l, j], din["ffn_up"][l, j], din["ffn_down"][l, j]
            norm_to_h(l, 0 if j == 0 else 2)
            if after_norm is not None:
                after_norm()
            chunks = [[0, 1, 2, 3], [4, 5, 6, 7], [8, 9, 10, 11], [12, 13, 14, 15, 16], [17, 18, 19, 20, 21]]

            def gu(ci):
                hb = ci % 2
                for fi, ft in enumerate(chunks[ci]):
                    ug = wunit([(lambda r: r.rearrange("p (k f) -> p k f", k=8),
                                 wg[:, ft * 128:(ft + 1) * 128].rearrange("(k p) f -> p k f", p=128))])
                    uu = wunit([(lambda r: r.rearrange("p (k f) -> p k f", k=8),
                                 wu[:, ft * 128:(ft + 1) * 128].rearrange("(k p) f -> p k f", p=128))])
                    for tt, (t0, n) in enumerate(TTS):
                        par = (fi * 5 + tt) % 2
                        pg, pu = 2 * par, 2 * par + 1
                        for (un, pb) in ((ug, pg), (uu, pu)):
                            for k in range(DT):
                                mm(psum[:, pb, 0:n], ring[:, un, k * 128:(k + 1) * 128], hT[:, k, t0:t0 + n],
                                   k == 0, k == DT - 1, [r_ring[un], r_h[k][tt]], [r_ps[pb]])
                        ti = 4 + par
                        sg = tmpa[:, ti, 0:n]
                        S.op("act", lambda e, sg=sg, pg=pg, n=n: e.activation(out=sg, in_=psum[:, pg, 0:n], func=AF.Silu),
                             reads=[r_ps[pg]], writes=[r_tmp[ti]])
                        S.op("dve", lambda e, sg=sg, pu=pu, n=n, hb=hb, fi=fi, t0=t0: e.tensor_tensor(
                            out=hid[:, hb, fi, t0:t0 + n], in0=sg, in1=psum[:, pu, 0:n], op=ALU.mult),
                            reads=[r_tmp[ti], r_ps[pu]], writes=[r_hid[hb][fi][tt]])

            def down(ci, last=False):
                hb = ci % 2
                fts = chunks[ci]
                us = [wunit([(lambda r: r, wd[ft * 128:(ft + 1) * 128, :])]) for ft in fts]
                cnt = [0]

                dbanks = (4, 5, 0, 1) if last else (4, 5, 6, 7)

                def group(d, tt):
                    t0, n = TTS[tt]
                    pb = dbanks[cnt[0] % 4]
                    cnt[0] += 1
                    for fi in range(len(fts)):
                        mm(psum[:, pb, 0:n], ring[:, us[fi], d * 128:(d + 1) * 128], hid[:, hb, fi, t0:t0 + n],
                           fi == 0, fi == len(fts) - 1, [r_ring[us[fi]], r_hid[hb][fi][tt]], [r_ps[pb]])
                    S.op("dve", lambda e: e.scalar_tensor_tensor(
                        out=xT[:, d, t0:t0 + n], in0=psum[:, pb, 0:n], scalar=0.5, in1=xT[:, d, t0:t0 + n],
                        op0=ALU.mult, op1=ALU.add),
                        reads=[r_ps[pb], r_x[d][tt]], writes=[r_x[d][tt]])

                if not last:
                    for d in range(DT):
                        for tt in range(5):
                            group(d, tt)
                else:
                    fused_tail(lambda tt: [group(d, tt) for d in range(DT)], nxt, extra)

            for ci in range(len(chunks)):
                gu(ci)
                if ci > 0:
                    down(ci - 1)
                if ci == 2 and mid_cb is not None:
                    mid_cb()
            down(len(chunks) - 1, last=True)

        def A_bf(off, n):
            return arena[:, off:off + n]

        def A_f32(off, n_f32):
            return arena[:, off:off + 2 * n_f32].bitcast(F32)

        tmp_bf = tmpa[:, 4:6, :].bitcast(BF16).rearrange("p a (b f) -> p (a b) f", b=2)

        def evac_add_x(pb, d, tt, t0, n):
            S.op("dve", lambda e: e.tensor_tensor(out=xT[:, d, t0:t0 + n], in0=psum[:, pb, 0:n], in1=xT[:, d, t0:t0 + n], op=ALU.add),
                 reads=[r_ps[pb], r_x[d][tt]], writes=[r_x[d][tt]])

        def kunit(w2d, c0, ncol=128):
            return wunit([(lambda r: r.rearrange("p (k f) -> p k f", k=8)[:, :, 0:ncol],
                           w2d[:, c0:c0 + ncol].rearrange("(k p) f -> p k f", p=128))])

        def ru(i):
            return ring[:, i, :].rearrange("p (k f) -> p k f", k=8)

        es_bc = sb("es_bc", [128, 4, 128], F32)
        tmpb = sb("tmpb", [128, 2, 512], F32)
        r_tmpb = [Res("tmpb0"), Res("tmpb1")]
        r_EB = Res("EB")
        r_es = Res("es")

        prep_state = {}

        def prep_bias():
            rb = tmpa[0:32, 4, 0:8]
            oh = tmpa[0:32, 5, 0:384]
            Lh = tmpa[0:32, 6:8, :].rearrange("p a f -> p (a f)").rearrange("p (h m) -> p h m", h=8)
            ers = [arena[:, 21120:21888].bitcast(F32), arena[:, 21888:22656].bitcast(F32)]
            r_er = [Res("er0"), Res("er1")]
            s_b = S.dsem("biasld")
            ld("sp", rb, din["rel_bias"], s_b, writes=[r_tmp[4]])
            ld("sp", oh, din["c_onehot"], S.dsem("ohld"), writes=[r_tmp[5]])
            S.op("act", lambda e: e.activation(out=rb, in_=rb, func=AF.Exp), reads=[r_tmp[4]], writes=[r_tmp[4]])
            S.op("dve", lambda e: e.tensor_copy(out=Lh, in_=rb.unsqueeze(2).broadcast_to([32, 8, 128])),
                 reads=[r_tmp[4]], writes=[r_tmp[6], r_tmp[7]])
            s_scr = S.dsem("scrw")
            r_scr = Res("scr")
            for h in range(8):
                pb = 2 + (h % 2)
                ti = h % 2
                mm(psum[:, pb, 0:384], Lh[:, h, :], oh, True, True, [r_tmp[5], r_tmp[6], r_tmp[7]], [r_ps[pb]])
                er = ers[ti]
                S.op("dve", lambda e, er=er, pb=pb: e.tensor_copy(out=er, in_=psum[:, pb, 0:384]), reads=[r_ps[pb]], writes=[r_er[ti]])
                ld("pool", scr[h, 0:128 * 384].rearrange("(k i) -> k i", i=384), er, s_scr, reads=[r_er[ti]], writes=[r_scr])
            prep_state["r_scr"] = r_scr

        def prep_bias_b():
            r_scr = prep_state["r_scr"]
            s_eb = S.dsem("ebld")
            for kb, off in ((0, 128), (1, 256)):
                src = scr[:, off:off + 128 * 383].rearrange("h (k i) -> k h i", i=383)[:, :, 0:128]
                ld("pool", EB[:, kb, :, :], src, s_eb, reads=[r_scr], writes=[r_EB])
            es8 = cols[:, 56:60]
            s_es = S.dsem("esld")
            for g in range(2):
                ld("sp", es8[g * 64:(g + 1) * 64, :], din["attn_sinks"][0, 4 * g:4 * g + 4].partition_broadcast(64), s_es, writes=[r_es])
            S.op("act", lambda e: e.activation(out=es8, in_=es8, func=AF.Exp), reads=[r_es], writes=[r_es])
            S.op("dve", lambda e: e.tensor_copy(out=es_bc[:], in_=es8.unsqueeze(2).broadcast_to([128, 4, 128])),
                 reads=[r_es], writes=[r_es])

        def odd_mixer(l, nxt=None):
            o = l // 2
            S.barrier()
            norm_to_h(l, 1)
            w_in, w_out = din["w_in_odd"][o], din["w_out_odd"][o]
            uT = A_bf(0, 8 * NTOK).rearrange("p (c t) -> p c t", c=8)
            vt32 = A_f32(16896, 2048).rearrange("p (a f) -> p a f", a=2)
            vn = A_bf(20992, 2048).rearrange("p (a f) -> p a f", a=2)
            wsT = A_bf(23040, 512).rearrange("p (g t) -> p g t", g=4)
            wsbd = A_bf(23552, 256).rearrange("p (g t) -> p g t", g=4)
            sgn = A_f32(23808, 1024)
            trim = A_f32(25856, 128)
            bdm = A_f32(26112, 64)
            sel4f = A_f32(26240, 64)
            sel4b = A_bf(26368, 64)
            bsp = tmpa[:, 6, :].rearrange("p (g t) -> p g t", g=4)
            wst = tmpa[:, 7, :].rearrange("p (g t) -> p g t", g=4)
            r_u = [[Res() for _ in range(5)] for _ in range(8)]
            r_vt = [Res(), Res()]
            r_vn = [Res(), Res()]
            r_c = Res("oddc")
            r_ws = Res("wsT")
            s_c2 = S.dsem("oddc")
            ld("sp", wst, w_sp_ap(o), S.dsem("wst"), writes=[r_tmp[7]])
            ld("sp", bsp, din["b_spatial"][o].partition_broadcast(128), S.dsem("bsp"), writes=[r_tmp[6]])
            ld("sp", sgn, din["sgu_norm"][o].partition_broadcast(128), s_c2, writes=[r_c])
            ld("sp", trim, din["c_trimask"], s_c2, writes=[r_c])
            ld("sp", bdm[0:64, :], din["c_bdmask"], s_c2, writes=[r_c])
            ld("sp", sel4f[0:4, :], din["c_sel4"], s_c2, writes=[r_c])
            cnt = 0
            for ft in range(8):
                un = kunit(w_in, ft * 128)
                for tt, (t0, n) in enumerate(TTS):
                    pb = cnt % 4
                    cnt += 1
                    for k in range(DT):
                        mm(psum[:, pb, 0:n], ru(un)[:, k, :], hT[:, k, t0:t0 + n], k == 0, k == DT - 1,
                           [r_ring[un], r_h[k][tt]], [r_ps[pb]])
                    S.op("act", lambda e, pb=pb, ft=ft, t0=t0, n=n: e.activation(out=uT[:, ft, t0:t0 + n], in_=psum[:, pb, 0:n],
                                                                                func=AF.Gelu_apprx_tanh),
                         reads=[r_ps[pb]], writes=[r_u[ft][tt]])
            S.op("dve", lambda e: e.tensor_copy(out=sel4b[0:4, :], in_=sel4f[0:4, :]), reads=[r_c], writes=[r_c])
            for g in range(4):
                S.op("pe", lambda e, g=g: e.transpose(out=psum[:, 6, g * 128:(g + 1) * 128], in_=wst[:, g, :], identity=ident[:]),
                     reads=[r_tmp[7], r_const], writes=[r_ps[6]], signal=(g == 3))
            S.op("dve", lambda e: e.tensor_tensor(out=wsT, in0=psum[:, 6, :].rearrange("p (g t) -> p g t", g=4),
                                                  in1=trim.unsqueeze(1).broadcast_to([128, 4, 128]), op=ALU.mult),
                 reads=[r_ps[6], r_c], writes=[r_ws])
            for g in range(4):
                mm(psum[0:64, 7, g * 64:(g + 1) * 64], sel4b[0:4, :],
                   wsT[0:4, g, 0:4].unsqueeze(1).broadcast_to([4, 16, 4]), True, True, [r_ws, r_c], [r_ps[7]], signal=(g == 3))
            S.op("dve", lambda e: e.tensor_tensor(out=wsbd[0:64, :, :], in0=psum[0:64, 7, 0:256].rearrange("p (g t) -> p g t", g=4),
                                                  in1=bdm[0:64, :].unsqueeze(1).broadcast_to([64, 4, 64]), op=ALU.mult),
                 reads=[r_ps[7], r_c], writes=[r_ws])
            uv = [wunit([(lambda r: r, w_in[k * 128:(k + 1) * 128, 1024:2048])]) for k in range(8)]
            junk = tmpa[:, 4, :].rearrange("p (a f) -> p a f", a=1)
            junk2 = tmpa[:, 4:6, :]
            r_ssq = [Res(), Res()]
            r_mix = [[Res(), Res()], [Res(), Res()]]
            s_sv = S.dsem("sv")
            def v_stage1(bi):
                t0, n = BLKS[bi]
                sl = bi % 2
                tt = min(t0 // 512, 4)
                vb = (0, 1) if sl == 0 else (4, 5)
                sbk = (2, 3) if sl == 0 else (6, 7)
                ssq = cols[:, 60 + 2 * sl:62 + 2 * sl]
                for half in range(2):
                    pb = vb[half]
                    for k in range(8):
                        mm(psum[0:n, pb, :], hT[:, k, t0:t0 + n], ring[:, uv[k], half * 512:(half + 1) * 512],
                           k == 0, k == 7, [r_ring[uv[k]], r_h[k][tt]], [r_ps[pb]])
                    S.op("act", lambda e, pb=pb, n=n, sl=sl, half=half: e.activation(
                        out=vt32[0:n, sl, half * 512:(half + 1) * 512], in_=psum[0:n, pb, :], func=AF.Gelu_apprx_tanh),
                        reads=[r_ps[pb]], writes=[r_vt[sl]])
                S.op("act", lambda e, n=n, sl=sl, ssq=ssq: e.activation(
                    out=junk2[0:n, :, :], in_=vt32[0:n, sl, :].rearrange("p (a f) -> p a f", a=2), func=AF.Square,
                    accum_out=ssq[0:n, 0:1]),
                    reads=[r_vt[sl]], writes=[r_ssq[sl]])
                S.op("act", lambda e, n=n, ssq=ssq: e.activation(out=ssq[0:n, 1:2], in_=ssq[0:n, 0:1], func=AF.Sqrt, bias=epsc[0:n, :], scale=1.0 / 1024),
                     reads=[r_ssq[sl], r_const], writes=[r_ssq[sl]])
                S.op("dve", lambda e, n=n, ssq=ssq: e.reciprocal(out=ssq[0:n, 1:2], in_=ssq[0:n, 1:2]), reads=[r_ssq[sl]], writes=[r_ssq[sl]])
                if bi < 16:
                    S.op("dve", lambda e, n=n, sl=sl, ssq=ssq: e.scalar_tensor_tensor(
                        out=vn[0:n, sl, :], in0=vt32[0:n, sl, :], scalar=ssq[0:n, 1:2], in1=sgn[0:n, :], op0=ALU.mult, op1=ALU.mult),
                        reads=[r_vt[sl], r_ssq[sl], r_c], writes=[r_vn[sl]])
                else:
                    S.op("dve", lambda e, n=n, sl=sl, ssq=ssq: e.scalar_tensor_tensor(
                        out=vt32[0:n, sl, :], in0=vt32[0:n, sl, :], scalar=ssq[0:n, 1:2], in1=sgn[0:n, :], op0=ALU.mult, op1=ALU.mult),
                        reads=[r_vt[sl], r_ssq[sl], r_c], writes=[r_vt[sl]])
                    S.op("dve", lambda e, n=n, sl=sl: e.tensor_copy(out=vn[0:n, sl, :], in_=vt32[0:n, sl, :]),
                         reads=[r_vt[sl]], writes=[r_vn[sl]])
                    ld("sp", o_sv[:, :], vt32[0:n, sl, :], s_sv, reads=[r_vt[sl]])

            def v_stage2(bi):
                t0, n = BLKS[bi]
                sl = bi % 2
                tt = min(t0 // 512, 4)
                vb = (0, 1) if sl == 0 else (4, 5)
                sbk = (2, 3) if sl == 0 else (6, 7)
                for half in range(2):
                    pb = sbk[half]
                    for c4 in range(4):
                        ct = half * 4 + c4
                        g = ct // 2
                        rhs = wsT[0:n, g, 0:n] if bi < 16 else wsbd[0:n, g, 0:n]
                        mm(psum[:, pb, c4 * 128:c4 * 128 + n], vn[0:n, sl, ct * 128:(ct + 1) * 128], rhs, True, True,
                           [r_vn[sl], r_ws], [r_ps[pb]], signal=(c4 == 3))
                    ti = 2 * sl + half
                    rm = r_tmp[ti]
                    mix = tmpa[:, ti, :].rearrange("p (c t) -> p c t", c=4)
                    pv = psum[:, pb, :].rearrange("p (c t) -> p c t", c=4)
                    for gg in range(2):
                        g = half * 2 + gg
                        if bi < 16:
                            in1 = bsp[:, g:g + 1, 0:n].broadcast_to([128, 2, n])
                            S.op("dve", lambda e, mix=mix, pv=pv, gg=gg, n=n, in1=in1: e.tensor_tensor(
                                out=mix[:, 2 * gg:2 * gg + 2, 0:n], in0=pv[:, 2 * gg:2 * gg + 2, 0:n], in1=in1, op=ALU.add),
                                reads=[r_ps[pb], r_tmp[6]], writes=[rm])
                        else:
                            in1 = bsp[:, g, 0:4].unsqueeze(1).unsqueeze(1).broadcast_to([128, 2, 16, 4])
                            S.op("dve", lambda e, mix=mix, pv=pv, gg=gg, n=n, in1=in1: e.tensor_tensor(
                                out=mix[:, 2 * gg:2 * gg + 2, 0:n].rearrange("p c (b t) -> p c b t", t=4),
                                in0=pv[:, 2 * gg:2 * gg + 2, 0:n].rearrange("p c (b t) -> p c b t", t=4), in1=in1, op=ALU.add),
                                reads=[r_ps[pb], r_tmp[6]], writes=[rm])
                    S.op("dve", lambda e, mix=mix, half=half, t0=t0, n=n: e.tensor_tensor(
                        out=uT[:, half * 4:(half + 1) * 4, t0:t0 + n], in0=mix[:, :, 0:n], in1=uT[:, half * 4:(half + 1) * 4, t0:t0 + n],
                        op=ALU.mult),
                        reads=[rm] + [r_u[half * 4 + c][tt] for c in range(4)],
                        writes=[r_u[half * 4 + c][tt] for c in range(4)])

            v_stage1(0)
            for bi in range(len(BLKS)):
                if bi + 1 < len(BLKS):
                    v_stage1(bi + 1)
                v_stage2(bi)
            ous = [kunit(w_out, d * 128) for d in range(DT)]
            ocnt = [0]

            def oproj_tile(tt):
                t0, n = TTS[tt]
                for d in range(DT):
                    pb = (4, 5, 0, 1)[ocnt[0] % 4]
                    ocnt[0] += 1
                    for k in range(8):
                        mm(psum[:, pb, 0:n], ru(ous[d])[:, k, :], uT[:, k, t0:t0 + n], k == 0, k == 7, [r_ring[ous[d]], r_u[k][tt]], [r_ps[pb]])
                    evac_add_x(pb, d, tt, t0, n)

            fused_tail(oproj_tile, nxt)
            S.barrier()

        def w_sp_ap(o):
            return din["w_spatial"][o].rearrange("g t s -> t g s")

        def even_mixer(l, nxt=None):
            ev = l // 2
            S.barrier()
            norm_to_h(l, 1)
            w_in, w_out = din["w_in_even"][ev], din["w_out_even"][ev]
            if SUB < 2:
                return
            dT = A_bf(0, 4 * NTOK).rearrange("p (g t) -> p g t", g=4)
            utb = A_bf(8448, 3 * 512).rearrange("p (a c) -> p a c", a=3)
            ut32 = A_f32(9984, 512)
            poolA = A_bf(11008, 512).rearrange("p (g t) -> p g t", g=4)
            poolB = A_bf(11520, 512).rearrange("p (g t) -> p g t", g=4)
            poolA0 = A_f32(12032, 512).rearrange("p (g t) -> p g t", g=4)
            Us = A_f32(14784, 4 * 16 * 19).rearrange("p (g b t) -> p g b t", g=4, b=16)
            Ws1 = A_f32(17216, 304).rearrange("p (b t) -> p b t", b=16)
            Ws2 = A_f32(17824, 304).rearrange("p (b t) -> p b t", b=16)
            invc = A_f32(18432, 64).rearrange("p (g t) -> p g t", g=4)
            wpool = A_bf(25280, 512).rearrange("p (g d) -> p g d", g=4)
            ctxs = A_f32(19072, 1024).rearrange("p (a f) -> p a f", a=2)
            uptok = A_f32(19072, 1024).rearrange("p (a f) -> p a f", a=2)
            t16 = A_f32(18560, 16)
            r_dT = [[Res() for _ in range(5)] for _ in range(4)]
            r_utb = [Res() for _ in range(3)]
            r_ut32 = Res("ut32")
            r_pm = Res("poolmats")
            r_Us = [Res() for _ in range(4)]
            r_Ws = [Res(), Res()]
            r_pc = Res("poolc")
            r_ctx = [Res(), Res()]
            r_upt = Res("uptok")
            s_pc = S.dsem("poolc")
            s_ctx = [S.dsem("ctx0"), S.dsem("ctx1")]
            ld("sp", poolA0, din["c_poolA0"], s_pc, writes=[r_pc])
            s_wp = S.dsem("wpool")
            r_wp = Res("wpool")
            ld("pool", wpool, din["w_pool"][ev].rearrange("g c d -> c g d"), s_wp, writes=[r_wp])
            s_o = S.dsem("outs")
            ld("sp", o_ps[:, 0:11, :], din["state_pool"][:, 4:15, :], s_o)
            kbuf = A_bf(21184, 2048).rearrange("p (b c) -> p b c", b=16)
            skst = tmpa[:, 6, :].rearrange("p (a c) -> p a c", a=4)
            r_kb = Res("kbuf")
            def ctx_prep():
                for hb in range(2):
                    for g in range(4):
                        S.op("pe", lambda e, g=g, hb=hb: e.transpose(out=psum[:, 6, g * 128:g * 128 + 120], in_=ctxs[0:120, hb, g * 128:(g + 1) * 128],
                                                                    identity=ident[0:120, 0:120]),
                             reads=[r_ctx[hb], r_const], writes=[r_ps[6]], signal=(g == 3))
                    S.op("dve", lambda e, hb=hb: e.tensor_copy(
                        out=Us[:, :, hb * 8:(hb + 1) * 8, 0:15],
                        in_=psum[:, 6, :].rearrange("p (g t) -> p g t", g=4)[:, :, 0:120].rearrange("p g (b r) -> p g b r", r=15)),
                        reads=[r_ps[6]], writes=r_Us)

            for hb in range(2):
                ld("sp", ctxs[0:120, hb, :], din["state_pool"][hb * 8:(hb + 1) * 8].rearrange("b r c -> (b r) c"), s_ctx[hb], writes=[r_ctx[hb]])
            s_sk = [S.dsem(f"sk{i}") for i in range(4)]
            r_sk = [Res() for _ in range(4)]

            def kbuf_prep(b4):
                for j in range(4):
                    b = b4 * 4 + j
                    ld("sp", skst[:, j, :], din["state_win_k"][b], s_sk[j], writes=[r_sk[j]])
                    S.op("pe", lambda e, j=j: e.transpose(out=psum[:, 7, j * 128:(j + 1) * 128], in_=skst[:, j, :], identity=ident[:]),
                         reads=[r_sk[j], r_const], writes=[r_ps[7]], signal=True)
                S.op("dve", lambda e: e.tensor_copy(out=kbuf[:, b4 * 4:(b4 + 1) * 4, :], in_=psum[:, 7, :].rearrange("p (a c) -> p a c", a=4)),
                     reads=[r_ps[7]], writes=[r_kb])

            if SUB < 2.2:
                return
            uns = [kunit(w_in, 768 + g * 128) for g in range(4)]
            urs = [wunit([(lambda r: r.rearrange("p (a c) -> p a c", a=2),
                           w_in[2 * i * 128:(2 * i + 2) * 128, 768:1280].rearrange("(a p) c -> p a c", p=128))]) for i in range(4)]
            s_pm = S.dsem("poolmats")
            allhid = [r for a_ in r_hid for b_ in a_ for r in b_]
            ld("pool", poolA, din["c_poolA"], s_pm, writes=[r_pm] + allhid)
            ld("pool", poolB, din["c_poolB"], s_pm, writes=[r_pm])
            pcnt = [0]

            def pool_proj(g, tt):
                t0, n = TTS[tt]
                pb = 6 + (pcnt[0] % 2)
                pcnt[0] += 1
                mm(psum[:, pb, 0:n], wpool[:, g, :], dT[:, g, t0:t0 + n], True, True, [r_wp, r_dT[g][tt]], [r_ps[pb]])
                S.op("act", lambda e: e.activation(out=dT[:, g, t0:t0 + n], in_=psum[:, pb, 0:n], func=AF.Identity,
                                                   scale=cols[:, 48 + g:49 + g]),
                     reads=[r_ps[pb], r_cols], writes=[r_dT[g][tt]])

            def pool_block_a(b):
                t0 = b * 128
                tt = b // 4
                bx = b % 2
                ui = b % 3
                for k in range(DT):
                    un = urs[k // 2]
                    mm(psum[:, bx, :], hT[:, k, t0:t0 + 128], ring[:, un, (k % 2) * 512:(k % 2 + 1) * 512], k == 0, k == DT - 1,
                       [r_ring[un], r_h[k][tt]], [r_ps[bx]])
                S.op("act", lambda e: e.copy(out=utb[:, ui, :], in_=psum[:, bx, :]), reads=[r_ps[bx]], writes=[r_utb[ui]])
                if b == 0:
                    S.op("act", lambda e: e.copy(out=ut32, in_=psum[:, bx, :]), reads=[r_ps[bx]], writes=[r_ut32])
                if b == 15:
                    S.op("act", lambda e: e.copy(out=uptok[:, 0, :], in_=psum[:, bx, :]), reads=[r_ps[bx]], writes=[r_upt])

            def pool_block_b(b):
                t0 = b * 128
                tt = b // 4
                by = 2 + (b % 2)
                ui = b % 3
                for g in range(4):
                    gc = slice(g * 128, (g + 1) * 128)
                    if b == 0:
                        mm(psum[:, by, gc], ut32[:, gc], poolA0[:, g, :], True, True, [r_ut32, r_pc], [r_ps[by]], signal=(g == 3))
                    else:
                        mm(psum[:, by, gc], utb[:, ui, gc], poolA[:, g, :], True, False, [r_utb[ui], r_pm], [r_ps[by]], signal=False)
                        mm(psum[:, by, gc], utb[:, (b - 1) % 3, gc], poolB[:, g, :], False, True, [r_utb[(b - 1) % 3], r_pm], [r_ps[by]],
                           signal=(g == 3))
                S.op("dve", lambda e: e.tensor_copy(out=dT[:, :, t0:t0 + 128], in_=psum[:, by, :].rearrange("p (g t) -> p g t", g=4)),
                     reads=[r_ps[by]], writes=[r_dT[g][tt] for g in range(4)])

            def pool_step(g, tt):
                w = 2 ** (g + 1)
                un = uns[g]
                t0, n = TTS[tt]
                pb = g
                for k in range(DT):
                    mm(psum[:, pb, 0:n], ru(un)[:, k, :], hT[:, k, t0:t0 + n], k == 0, k == DT - 1, [r_ring[un], r_h[k][tt]], [r_ps[pb]])
                Ug = Us[:, g, :, :]
                S.op("act", lambda e: e.copy(out=Ug[:, :, 15:19], in_=psum[:, pb, 0:64].rearrange("p (b t) -> p b t", t=4)),
                     reads=[r_ps[pb]], writes=[r_Us[g]])
                bufs = [Ws1, Ws2]
                src, rs = Ug, r_Us[g]
                sh = 1
                lo = 1
                for step in range(g + 1):
                    dst, rd = bufs[step % 2], r_Ws[step % 2]
                    S.op("dve", lambda e, dst=dst, src=src, lo=lo, sh=sh: e.tensor_tensor(
                        out=dst[:, :, lo:19], in0=src[:, :, lo:19], in1=src[:, :, lo - sh:19 - sh], op=ALU.add),
                        reads=[rs], writes=[rd])
                    src, rs = dst, rd
                    sh *= 2
                    lo += sh
                S.op("dve", lambda e, src=src: e.scalar_tensor_tensor(
                    out=dT[:, g, 2048:2112].rearrange("p (b t) -> p b t", t=4), in0=src[:, :, 15:19], scalar=1.0 / w,
                    in1=Ug[:, :, 15:19], op0=ALU.mult, op1=ALU.subtract),
                    reads=[rs, r_Us[g]], writes=[r_dT[g][4]])

            pool_block_a(0)
            for b in range(16):
                if b + 1 < 16:
                    pool_block_a(b + 1)
                pool_block_b(b)
                if b % 4 == 1 and b >= 5:
                    for g in range(4):
                        pool_proj(g, b // 4 - 1)
                if b % 4 == 2:
                    kbuf_prep(b // 4)
                if b == 11:
                    ctx_prep()
            for g in range(4):
                pool_step(g, 4)
            for g in range(4):
                pool_proj(g, 3)
            for g in range(4):
                for (row, t0, n, tt) in ((1, 2048, 64, 4),):
                    pb = 4 + row
                    for k in range(DT):
                        mm(psum[0:n, pb, g * 128:(g + 1) * 128], hT[:, k, t0:t0 + n], ru(uns[g])[:, k, :], k == 0, k == DT - 1,
                           [r_ring[uns[g]], r_h[k][tt]], [r_ps[pb]])
            for g in range(4):
                pool_proj(g, 4)
            for row, n in ((1, 64),):
                S.op("act", lambda e, row=row, n=n: e.copy(out=uptok[0:n, row, :], in_=psum[0:n, 4 + row, :]), reads=[r_ps[4 + row]], writes=[r_upt])
            s_up = S.dsem("uptok_out")
            ld("sp", o_pp[:, :], uptok[113:128, 0, :], s_up, reads=[r_upt])
            for b in range(16):
                ld("sp", o_ps[b, 11:15, :], uptok[4 * b:4 * b + 4, 1, :], s_up, reads=[r_upt])
            if SUB < 3:
                return
            q = A_bf(8448, 4 * NTOK).rearrange("p (h t) -> p h t", h=4)
            kT = A_bf(16896, NTOK)
            vtok = A_bf(19008, 17 * 128).rearrange("p (b c) -> p b c", b=17)
            vbuf = A_bf(23232, 2048).rearrange("p (b c) -> p b c", b=16)
            EBS = A_f32(25280, 512).rearrange("p (h t) -> p h t", h=8)
            tokst = tmpa[:, 7, :].rearrange("p (a c) -> p a c", a=4)
            PT = tmp_bf
            r_q = [[Res() for _ in range(5)] for _ in range(4)]
            r_k = [Res() for _ in range(5)]
            r_vt = [Res() for _ in range(17)]
            r_vb, r_ebs, r_tok = Res(), Res(), [Res() for _ in range(4)]
            S.barrier(skip=(s_up,))
            s_vb = S.dsem("vbuf")
            ld("sp", o_ks[:, 0:124, :], din["state_win_k"][:, 4:128, :], s_o)
            ld("sp", o_vs[:, 0:124, :], din["state_win_v"][:, 4:128, :], s_o)
            if SUB < 3.3:
                return
            cnt = 0
            for hh in range(4):
                un = wunit([(lambda r: r.rearrange("p (k f) -> p k f", k=8)[:, :, 0:64],
                             w_in[:, hh * 64:(hh + 1) * 64].rearrange("(k p) f -> p k f", p=128)),
                            (lambda r: r.rearrange("p (k f) -> p k f", k=8)[:, :, 64:128],
                             w_in[:, (4 + hh) * 64:(5 + hh) * 64].rearrange("(k p) f -> p k f", p=128))])
                for tt, (t0, n) in enumerate(TTS):
                    pb = cnt % 4
                    cnt += 1
                    for k in range(DT):
                        mm(psum[:, pb, 0:n], ru(un)[:, k, :], hT[:, k, t0:t0 + n], k == 0, k == DT - 1, [r_ring[un], r_h[k][tt]], [r_ps[pb]])
                    S.op("act", lambda e, pb=pb, hh=hh, t0=t0, n=n: e.mul(out=q[:, hh, t0:t0 + n], in_=psum[:, pb, 0:n], mul=0.125),
                         reads=[r_ps[pb]], writes=[r_q[hh][tt]])
            if SUB < 3.4:
                return
            un = kunit(w_in, 512)
            for tt, (t0, n) in enumerate(TTS):
                pb = cnt % 4
                cnt += 1
                for k in range(DT):
                    mm(psum[:, pb, 0:n], ru(un)[:, k, :], hT[:, k, t0:t0 + n], k == 0, k == DT - 1, [r_ring[un], r_h[k][tt]], [r_ps[pb]])
                S.op("dve", lambda e, pb=pb, t0=t0, n=n: e.tensor_copy(out=kT[:, t0:t0 + n], in_=psum[:, pb, 0:n]), reads=[r_ps[pb]], writes=[r_k[tt]])
            for (row, t0, n, tt) in ((0, 1920, 128, 3), (1, 2048, 64, 4)):
                for k in range(DT):
                    mm(psum[0:n, 4, row * 128:(row + 1) * 128], hT[:, k, t0:t0 + n], ru(un)[:, k, :], k == 0, k == DT - 1,
                       [r_ring[un], r_h[k][tt]], [r_ps[4]])
                S.op("act", lambda e, row=row, n=n: e.copy(out=tokst[0:n, row, :], in_=psum[0:n, 4, row * 128:(row + 1) * 128]),
                     reads=[r_ps[4]], writes=[r_tok[row]])
            s_tk = S.dsem("tokouts")
            ld("sp", o_kp[:, :], tokst[:, 0, :], s_tk, reads=[r_tok[0]])
            for b in range(16):
                ld("sp", o_ks[b, 124:128, :], tokst[4 * b:4 * b + 4, 1, :], s_tk, reads=[r_tok[1]])
            if SUB < 3.5:
                return
            un = kunit(w_in, 640)
            ld("pool", vbuf, din["state_win_v"].rearrange("b k c -> k b c"), s_vb, writes=[r_vb])
            for bi, (t0, n) in enumerate(BLKS):
                tt = min(t0 // 512, 4)
                pb = 5 + (bi % 2) * 2
                for k in range(DT):
                    mm(psum[0:n, pb, 0:128], hT[:, k, t0:t0 + n], ru(un)[:, k, :], k == 0, k == DT - 1, [r_ring[un], r_h[k][tt]], [r_ps[pb]])
                S.op("dve", lambda e, pb=pb, bi=bi, n=n: e.tensor_copy(out=vtok[0:n, bi, :], in_=psum[0:n, pb, 0:128]), reads=[r_ps[pb]], writes=[r_vt[bi], r_upt])
                if bi >= 15:
                    row = 2 + (bi - 15)
                    S.op("dve", lambda e, pb=pb, row=row, n=n: e.tensor_copy(out=tokst[0:n, row, :], in_=psum[0:n, pb, 0:128]), reads=[r_ps[pb]], writes=[r_tok[row]])
            if not os.environ.get("MK_NOVOUT"):
                ld("sp", o_vp[:, :], tokst[:, 2, :], s_tk, reads=[r_tok[2]])
                for b in range(16):
                    ld("sp", o_vs[b, 124:128, :], tokst[4 * b:4 * b + 4, 3, :], s_tk, reads=[r_tok[3]])
            S.op("dve", lambda e: e.memset(EBS[0:64, :, :], 0.0), writes=[r_ebs])
            s_ebs = S.dsem("ebs")
            for b in range(16):
                ld("sp", EBS[4 * b:4 * b + 4, :, 4 * b:4 * b + 4], EB[0:4, 0, :, 0:4], s_ebs, reads=[r_EB], writes=[r_ebs])
            attnT = hT
            etc = [0]

            PTB = tmpa[:, 6:8, :].bitcast(BF16).rearrange("p a (b f) -> p (a b) f", b=2)
            PTs = (PT, PTB)
            tok_all = list(r_tok)

            def pt_res(par, pb):
                return r_tmp[(4 if par == 0 else 6) + pb // 2]

            def att_scores(bi):
                t0 = bi * 128
                tt = min(t0 // 512, 4)
                par = bi % 2
                kbs = [(0, bi)] + ([(1, bi - 1)] if bi > 0 else [])
                for g in range(2):
                    gp = slice(g * 64, (g + 1) * 64)
                    for (kb, kblk) in kbs:
                        pb = g * 2 + kb
                        k0 = kblk * 128
                        ktt = min(k0 // 512, 4)
                        mm(psum[:, pb, :], kT[gp, k0:k0 + 128], q[gp, :, t0:t0 + 128], True, True,
                           [r_k[ktt]] + [r_q[hh][tt] for hh in range(4)], [r_ps[pb]])
                        ti = etc[0] % 4
                        etc[0] += 1
                        et = tmpa[:, ti, :] if ti < 2 else tmpb[:, ti - 2, :]
                        r_et = r_tmp[ti] if ti < 2 else r_tmpb[ti - 2]
                        S.op("act", lambda e, et=et, pb=pb: e.activation(out=et, in_=psum[:, pb, :], func=AF.Exp), reads=[r_ps[pb]], writes=[r_et])
                        S.op("dve", lambda e, et=et, pb=pb, kb=kb, g=g, par=par: e.tensor_tensor(
                            out=PTs[par][:, pb, :].rearrange("p (h t) -> p h t", h=4), in0=et.rearrange("p (h t) -> p h t", h=4),
                            in1=EB[:, kb, 4 * g:4 * g + 4, :], op=ALU.mult),
                            reads=[r_et, r_EB], writes=[pt_res(par, pb)] + (tok_all if par == 1 else []))

            def att_av(bi):
                t0 = bi * 128
                tt = min(t0 // 512, 4)
                par = bi % 2
                bo, bd = (4, 5) if par == 0 else (6, 7)
                kbs = [(0, bi)] + ([(1, bi - 1)] if bi > 0 else [])
                for (bank, is_den) in ((bo, False), (bd, True)):
                    for g in range(2):
                        gp = slice(g * 64, (g + 1) * 64)
                        for i, (kb, kblk) in enumerate(kbs):
                            pb = g * 2 + kb
                            lhs = ones_b[:, 0:64] if is_den else vtok[:, kblk, g * 64:(g + 1) * 64]
                            mm(psum[gp, bank, :], lhs, PTs[par][:, pb, :], i == 0, i == len(kbs) - 1,
                               [r_const if is_den else r_vt[kblk], pt_res(par, pb)], [r_ps[bank]], signal=(g == 1 and i == len(kbs) - 1))
                ds = 2 + par
                den = tmpa[:, ds, :]
                S.op("dve", lambda e, den=den, bd=bd: e.tensor_tensor(out=den, in0=psum[:, bd, :], in1=es_bc[:].rearrange("p h t -> p (h t)"), op=ALU.add),
                     reads=[r_ps[bd], r_es], writes=[r_tmp[ds]])
                S.op("act", lambda e, den=den: e.activation(out=den, in_=den, func=AF.Ln), reads=[r_tmp[ds]], writes=[r_tmp[ds]])
                S.op("act", lambda e, den=den: e.activation(out=den, in_=den, func=AF.Exp, scale=-1.0), reads=[r_tmp[ds]], writes=[r_tmp[ds]])
                S.op("dve", lambda e, t0=t0, den=den, bo=bo: e.tensor_tensor(out=attnT[:, 0:4, t0:t0 + 128], in0=psum[:, bo, :].rearrange("p (h t) -> p h t", h=4),
                                                                          in1=den.rearrange("p (h t) -> p h t", h=4), op=ALU.mult),
                     reads=[r_ps[bo], r_tmp[ds]], writes=[r_h[hh][tt] for hh in range(4)])

            if SUB < 4:
                return
            att_scores(0)
            for bi in range(16):
                if bi + 1 < 16:
                    att_scores(bi + 1)
                att_av(bi)
            if SUB < 5:
                return

            def sample_attn():
                t0, n, tt = 2048, 64, 4
                v4 = lambda ap: ap.rearrange("p (b h t) -> p b h t", b=16, h=4)
                for b in range(16):
                    for g in range(2):
                        gp = slice(g * 64, (g + 1) * 64)
                        mm(psum[:, g, b * 16:(b + 1) * 16], kbuf[gp, b, :], q[gp, :, t0 + 4 * b:t0 + 4 * b + 4], True, True,
                           [r_kb] + [r_q[hh][4] for hh in range(4)], [r_ps[g]], signal=(b == 15))
                for g in range(2):
                    gp = slice(g * 64, (g + 1) * 64)
                    mm(psum[0:64, 2 + g, 0:256], kT[gp, t0:t0 + 64], q[gp, :, t0:t0 + 64].rearrange("p h (b t) -> p b h t", t=4), True, True,
                       [r_k[4]] + [r_q[hh][4] for hh in range(4)], [r_ps[2 + g]], signal=True)
                et0, et1 = tmpa[:, 0, :], tmpa[:, 1, :]
                for g in range(2):
                    cs = slice(g * 256, (g + 1) * 256)
                    S.op("act", lambda e, g=g, cs=cs: e.activation(out=et0[:, cs], in_=psum[:, g, 0:256], func=AF.Exp),
                         reads=[r_ps[g]], writes=[r_tmp[0]])
                    S.op("dve", lambda e, g=g, cs=cs: e.tensor_tensor(
                        out=v4(PT[:, 0, cs]), in0=v4(et0[:, cs]),
                        in1=EB[:, 1, 4 * g:4 * g + 4, 0:4].unsqueeze(1).broadcast_to([128, 16, 4, 4]), op=ALU.mult),
                        reads=[r_tmp[0], r_EB], writes=[r_tmp[4]])
                for g in range(2):
                    cs = slice(g * 256, (g + 1) * 256)
                    S.op("act", lambda e, g=g, cs=cs: e.activation(out=et1[0:64, cs], in_=psum[0:64, 2 + g, 0:256], func=AF.Exp),
                         reads=[r_ps[2 + g]], writes=[r_tmp[1]])
                    S.op("dve", lambda e, g=g, cs=cs: e.tensor_tensor(
                        out=v4(PT[0:64, 2, cs]), in0=v4(et1[0:64, cs]),
                        in1=EBS[0:64, 4 * g:4 * g + 4, :].rearrange("p h (b t) -> p b h t", t=4), op=ALU.mult),
                        reads=[r_tmp[1], r_ebs], writes=[r_tmp[5]])
                for (bank, is_den) in ((4, False), (5, True)):
                    for g in range(2):
                        gp = slice(g * 64, (g + 1) * 64)
                        lhs_c = ones_b[0:64, 0:64] if is_den else vtok[0:64, 16, g * 64:(g + 1) * 64]
                        mm(psum[gp, bank, 0:256], lhs_c, PT[0:64, 2, g * 256:(g + 1) * 256], True, False,
                           [r_vt[16], r_tmp[5], r_const], [r_ps[bank]], signal=False)
                        for b in range(16):
                            lhs_p = ones_b[:, 0:64] if is_den else vbuf[:, b, g * 64:(g + 1) * 64]
                            mm(psum[gp, bank, b * 16:(b + 1) * 16], lhs_p, PT[:, 0, g * 256 + b * 16:g * 256 + (b + 1) * 16], False, b == 15,
                               [r_vb, r_tmp[4], r_const], [r_ps[bank]], signal=(b == 15 and g == 1))
                den = tmpa[:, 3, 0:256]
                S.op("dve", lambda e: e.tensor_tensor(out=v4(den), in0=v4(psum[:, 5, 0:256]),
                                                      in1=es_bc[:, :, 0:4].unsqueeze(1).broadcast_to([128, 16, 4, 4]), op=ALU.add),
                     reads=[r_ps[5], r_es], writes=[r_tmp[3]])
                S.op("dve", lambda e: e.reciprocal(out=den, in_=den), reads=[r_tmp[3]], writes=[r_tmp[3]])
                S.op("dve", lambda e: e.tensor_tensor(
                    out=attnT[:, 0:4, t0:t0 + 64].rearrange("p h (b t) -> p h b t", t=4),
                    in0=psum[:, 4, 0:256].rearrange("p (b h t) -> p h b t", b=16, h=4),
                    in1=den.rearrange("p (b h t) -> p h b t", b=16, h=4), op=ALU.mult),
                    reads=[r_ps[4], r_tmp[3]], writes=[r_h[hh][4] for hh in range(4)])

            sample_attn()
            if SUB < 6:
                return
            ous = []
            for d in range(DT):
                cs = slice(d * 128, (d + 1) * 128)
                ous.append(wunit([
                    (lambda r: r.rearrange("p (k f) -> p k f", k=8)[0:64, 0:4, :], w_out[0:256, cs].rearrange("(hh dd) c -> dd hh c", dd=64)),
                    (lambda r: r.rearrange("p (k f) -> p k f", k=8)[64:128, 0:4, :], w_out[256:512, cs].rearrange("(hh dd) c -> dd hh c", dd=64)),
                    (lambda r: r.rearrange("p (k f) -> p k f", k=8)[:, 4:8, :], w_out[512:1024, cs].rearrange("(g p) c -> p g c", p=128)),
                ]))
            ocnt = [0]

            def oproj_tile(tt):
                t0, n = TTS[tt]
                for d in range(DT):
                    pb = (4, 5, 0, 1)[ocnt[0] % 4]
                    ocnt[0] += 1
                    for k in range(8):
                        rhs = attnT[:, k, t0:t0 + n] if k < 4 else dT[:, k - 4, t0:t0 + n]
                        rr = r_h[k][tt] if k < 4 else r_dT[k - 4][tt]
                        mm(psum[:, pb, 0:n], ru(ous[d])[:, k, :], rhs, k == 0, k == 7, [r_ring[ous[d]], rr], [r_ps[pb]])
                    evac_add_x(pb, d, tt, t0, n)

            fused_tail(oproj_tile, nxt)
            S.barrier()

        fg = tmpa[:, 0:2, :].rearrange("p a f -> p (a f)")
        s_fg = S.dsem("fg")
        osb = arena[:, 21120:21120 + 4096].bitcast(F32).rearrange("p (a f) -> p a f", a=2)
        r_os = [Res("os0"), Res("os1")]
        s_os = [S.dsem("os0"), S.dsem("os1")]
        r_fs = [Res(), Res()]
        junkf = tmpa[:, 2:4, :]

        def final_setup():
            ld("sp", fg, din["final_gain"].partition_broadcast(128), s_fg, writes=[r_tmp[0], r_tmp[1]])

        def final_tile(tile):
            blks = [16] if tile == 4 else list(range(4 * tile, 4 * tile + 4))
            for bi in blks:
                t0, n = BLKS[bi]
                sl = bi % 2
                tt = min(t0 // 512, 4)
                pbs = (6, 7) if bi % 2 == 0 else (2, 3)
                ssq = tmpa[:, 6, 16 * sl:16 * sl + 16]
                for half in range(2):
                    pb = pbs[half]
                    for j in range(4):
                        d = half * 4 + j
                        S.op("pe", lambda e, pb=pb, j=j, d=d, n=n, t0=t0: e.transpose(
                            out=psum[0:n, pb, j * 128:(j + 1) * 128], in_=xT[:, d, t0:t0 + n], identity=ident[:]),
                            reads=[r_x[d][tt], r_const], writes=[r_ps[pb]], signal=(j == 3))
                S.op("act", lambda e, n=n, p0=pbs[0], ssq=ssq: e.activation(
                    out=junkf[0:n, :, :], in_=psum[0:n, p0:p0 + 2, :], func=AF.Square, accum_out=ssq[0:n, 0:1]),
                    reads=[r_ps[pbs[0]], r_ps[pbs[1]]], writes=[r_fs[sl]])
                S.op("act", lambda e, n=n, ssq=ssq: e.activation(out=ssq[0:n, 1:2], in_=ssq[0:n, 0:1], func=AF.Sqrt, bias=epsc[0:n, :], scale=1.0 / D),
                     reads=[r_fs[sl], r_const], writes=[r_fs[sl]])
                S.op("dve", lambda e, n=n, ssq=ssq: e.reciprocal(out=ssq[0:n, 1:2], in_=ssq[0:n, 1:2]), reads=[r_fs[sl]], writes=[r_fs[sl]])
                for half in range(2):
                    pb = pbs[half]
                    S.op("dve", lambda e, pb=pb, n=n, half=half, sl=sl, ssq=ssq: e.scalar_tensor_tensor(
                        out=osb[0:n, sl, half * 512:(half + 1) * 512], in0=psum[0:n, pb, :], scalar=ssq[0:n, 1:2],
                        in1=fg[0:n, half * 512:(half + 1) * 512], op0=ALU.mult, op1=ALU.mult),
                        reads=[r_ps[pb], r_fs[sl], r_tmp[0], r_tmp[1]], writes=[r_os[sl]])
                dst = o_yp[t0:t0 + n, :] if bi < 16 else o_ys[:, :]
                ld("sp", dst, osb[0:n, sl, :], s_os[sl], reads=[r_os[sl]])

        if STAGE >= 99:
            load_x(norm=(0, 0), pre_cb=prep_bias)
            prep_bias_b()
            ffn(0, 0, nxt=(0, 1))
            even_mixer(0, nxt=(0, 2))
            ffn(0, 1, nxt=(1, 0))
            ffn(1, 0, nxt=(1, 1))
            odd_mixer(1, nxt=(1, 2))
            ffn(1, 1, after_norm=final_setup, extra=final_tile)
        else:
            if STAGE == 4:
                prep_bias()
            load_x()
            if STAGE == 4:
                prep_bias_b()
            if STAGE == 2:
                ffn(0, 0)
            elif STAGE == 3:
                odd_mixer(1)
            elif STAGE == 4:
                even_mixer(0)
            elif STAGE == 6:
                ffn(0, 0, nxt=(0, 2))
                ffn(0, 1)
            elif STAGE == 5:
                ffn(0, 0); ffn(0, 1); ffn(1, 0); ffn(1, 1)
            S.barrier()
            final_setup()
            for tile in range(5):
                final_tile(tile)

        S.emit(nc, st)
    return nc


_PROG = None


def kernel(**inputs):
    global _PROG
    if _PROG is None:
        _PROG = build_program()
    nc = _PROG
    consts = host_consts()
    f = lambda a: np.ascontiguousarray(np.asarray(a, dtype=np.float32))
    wts = {n_: f(inputs[n_]) for n_, s_ in W_SPECS}
    in_maps = []
    for c in range(NCORES):
        m = {}
        m["x_prompt"] = f(inputs["x_prompt"][c])
        m["x_sample"] = f(inputs["x_sample"][c * 16:(c + 1) * 16]).reshape(NS, D)
        m["state_win_k"] = f(inputs["state_win_k"][0, c * 16:(c + 1) * 16]).reshape(16, 128, 128)
        m["state_win_v"] = f(inputs["state_win_v"][0, c * 16:(c + 1) * 16]).reshape(16, 128, 128)
        m["state_pool"] = f(inputs["state_pool"][0, c * 16:(c + 1) * 16])
        for n_, s_ in W_SPECS:
            m[n_] = wts[n_]
        for n_ in CONST_SHAPES:
            m["c_" + n_] = consts[n_]
        in_maps.append(m)
    res = run_bass_kernel_spmd(nc, in_maps, core_ids=list(range(NCORES)))
    R = res.results
    y_prompt = np.stack([R[c]["y_prompt"] for c in range(NCORES)], 0)
    y_sample = np.concatenate([R[c]["y_sample"].reshape(16, 4, D) for c in range(NCORES)], 0)
    nk_p = np.stack([R[c]["nk_p"].reshape(128, 2, 64) for c in range(NCORES)], 0)[None]
    nv_p = np.stack([R[c]["nv_p"].reshape(128, 2, 64) for c in range(NCORES)], 0)[None]
    np_p = np.stack([R[c]["np_p"] for c in range(NCORES)], 0)[None]
    nk_s = np.concatenate([R[c]["nk_s"].reshape(16, 128, 2, 64) for c in range(NCORES)], 0)[None]
    nv_s = np.concatenate([R[c]["nv_s"].reshape(16, 128, 2, 64) for c in range(NCORES)], 0)[None]
    np_s = np.concatenate([R[c]["np_s"] for c in range(NCORES)], 0)[None]
    sgu_v = np.concatenate([R[c]["sgu_v"].reshape(16, 4, D) for c in range(NCORES)], 0)[None]
    return (y_prompt, y_sample, nk_p, nv_p, np_p, nk_s, nv_s, np_s, sgu_v)
```

```python
import os
import numpy as np
from contextlib import ExitStack
import concourse.bass as bass
import concourse.mybir as mybir
from concourse.bass_utils import run_bass_kernel_spmd

F32 = mybir.dt.float32
BF16 = mybir.dt.bfloat16
AF = mybir.ActivationFunctionType
ALU = mybir.AluOpType

NCORES = 8
D = 1024
DT = 8
FF = 2816
FT = 22
SEQ = 2048
NS = 64
NB_S = 16
NTOK = SEQ + NS
TTS = [(0, 512), (512, 512), (1024, 512), (1536, 512), (2048, 64)]
BLKS = [(b * 128, 128) for b in range(16)] + [(2048, 64)]
EPS = 1e-6
STAGE = int(os.environ.get("MK_STAGE", "99"))
SUB = float(os.environ.get("MK_SUB", "99"))

ENGS = ("pe", "act", "dve", "pool", "sp")


class Res:
    __slots__ = ("name", "lw", "rd")

    def __init__(self, name=""):
        self.name = name
        self.lw = None
        self.rd = []


class DmaSem:
    __slots__ = ("name", "cnt", "h", "q")

    def __init__(self, name):
        self.name = name
        self.cnt = 0
        self.h = None
        self.q = None


class Sched:
    def __init__(self):
        self.q = {e: [] for e in ENGS}
        self.cnt = {e: 0 for e in ENGS}
        self.waited = {e: {} for e in ENGS}
        self.dsems = []

    def dsem(self, name):
        s = DmaSem(name)
        self.dsems.append(s)
        return s

    def _collect(self, eng, reads, writes, is_dma=False):
        need = {}
        for r in reads:
            if r.lw is not None:
                k, v = r.lw
                if v > need.get(k, 0):
                    need[k] = v
        for w in writes:
            if w.lw is not None:
                k, v = w.lw
                if v > need.get(k, 0):
                    need[k] = v
            for (k, v) in w.rd:
                if v > need.get(k, 0):
                    need[k] = v
        out = []
        wd = self.waited[eng]
        for k, v in need.items():
            if k == "pe" and eng == "pe" and not is_dma:
                continue
            if wd.get(k, 0) >= v:
                continue
            wd[k] = v
            out.append((k, v))
        return out

    def op(self, eng, fn, reads=(), writes=(), signal=True):
        waits = self._collect(eng, reads, writes)
        if signal:
            self.cnt[eng] += 1
            val = self.cnt[eng]
        else:
            val = self.cnt[eng] + 1
        e = (eng, val)
        for r in reads:
            r.rd.append(e)
        for w in writes:
            w.lw = e
            w.rd = []
        self.q[eng].append((waits, fn, (eng, 1) if signal else None))

    def dma(self, qeng, fn, sem, reads=(), writes=()):
        waits = self._collect(qeng, reads, writes, is_dma=True)
        assert sem.q in (None, qeng)
        sem.q = qeng
        sem.cnt += 16
        e = (sem, sem.cnt)
        for r in reads:
            r.rd.append(e)
        for w in writes:
            w.lw = e
            w.rd = []
        self.q[qeng].append((waits, fn, (sem, 16)))

    def barrier(self, engs=("pe", "act", "dve", "sp"), skip=(), dma_only=False):
        ev = [] if dma_only else [(e, self.cnt[e]) for e in engs if e != "sp" and self.cnt[e] > 0]
        ev += [(s, s.cnt) for s in self.dsems if s.cnt > 0 and s.q in engs and s not in skip]
        for e in engs:
            wd = self.waited[e]
            waits = []
            for k, v in ev:
                if k == e:
                    continue
                if wd.get(k, 0) >= v:
                    continue
                wd[k] = v
                waits.append((k, v))
            if waits:
                self.q[e].append((waits, None, None))

    def emit(self, nc, stack):
        esem = {}
        for e in ("pe", "act", "dve", "pool"):
            esem[e] = stack.enter_context(nc.semaphore("s_" + e))
        for s in self.dsems:
            s.h = stack.enter_context(nc.semaphore("d_" + s.name))
        fin = [(s, s.cnt) for s in self.dsems if s.cnt > 0]
        for e in ("pe", "act", "dve", "pool"):
            if self.cnt[e] > 0:
                fin.append((e, self.cnt[e]))

        def hof(k):
            return esem[k] if isinstance(k, str) else k.h

        def run(eng_name, eng):
            for waits, fn, sig in self.q[eng_name]:
                for k, v in waits:
                    eng.wait_ge(hof(k), v)
                if fn is None:
                    continue
                inst = fn(eng)
                if sig is not None:
                    inst.then_inc(hof(sig[0]), sig[1])
            if eng_name == "sp":
                for k, v in fin:
                    eng.wait_ge(hof(k), v)

        block = stack.enter_context(nc.Block())

        @block.tensor
        def _(e):
            run("pe", e)

        @block.scalar
        def _(e):
            run("act", e)

        @block.vector
        def _(e):
            run("dve", e)

        @block.gpsimd
        def _(e):
            run("pool", e)

        @block.sync
        def _(e):
            run("sp", e)


def t5_bucket_np(n):
    n = np.maximum(n, 0)
    max_exact = 16
    nf = np.maximum(n, 1).astype(np.float32)
    large = max_exact + (np.log(nf / np.float32(max_exact)) / np.float32(np.log(128 / max_exact))
                         * np.float32(32 - max_exact)).astype(np.int32)
    large = np.minimum(large, 31)
    return np.where(n < max_exact, n, large)


def host_consts():
    c = {}
    c["ident"] = np.eye(128, dtype=np.float32)
    oh = np.zeros((32, 384), np.float32)
    n = np.arange(384) - 128
    valid = (n >= 0) & (n < 128)
    bk = t5_bucket_np(n)
    oh[bk[valid], np.arange(384)[valid]] = 1.0
    c["onehot"] = oh
    c["trimask"] = np.triu(np.ones((128, 128), np.float32))
    m = np.zeros((64, 64), np.float32)
    for b in range(16):
        for s in range(4):
            for t in range(s, 4):
                m[4 * b + s, 4 * b + t] = 1.0
    c["bdmask"] = m
    ic = np.zeros((128, 4, 16), np.float32)
    for g, w in enumerate((2, 4, 8, 16)):
        ic[:, g, :] = 1.0 / np.minimum(np.arange(16) + 1, w)
    c["invc"] = ic
    sel = np.zeros((4, 64), np.float32)
    for b in range(16):
        for t in range(4):
            sel[t, 4 * b + t] = 1.0
    c["sel4"] = sel
    A = np.zeros((128, 4, 128), np.float32)
    B = np.zeros((128, 4, 128), np.float32)
    A0 = np.zeros((128, 4, 128), np.float32)
    for g, w in enumerate((2, 4, 8, 16)):
        for t in range(128):
            for sx in range(t - w + 1, t + 1):
                if sx >= 0:
                    A[sx, g, t] += 1.0 / w
                    A0[sx, g, t] += 1.0 / min(t + 1, w)
                else:
                    B[128 + sx, g, t] += 1.0 / w
            A[t, g, t] -= 1.0
            A0[t, g, t] -= 1.0
    c["poolA"], c["poolB"], c["poolA0"] = A, B, A0
    return c


CONST_SHAPES = {"ident": [128, 128], "onehot": [32, 384], "trimask": [128, 128],
                "bdmask": [64, 64], "invc": [128, 4, 16], "sel4": [4, 64],
                "poolA": [128, 4, 128], "poolB": [128, 4, 128], "poolA0": [128, 4, 128]}

W_SPECS = [("rel_bias", [32, 8]), ("norm_gains", [2, 3, 1024]), ("final_gain", [1024]),
           ("ffn_gate", [2, 2, 1024, 2816]), ("ffn_up", [2, 2, 1024, 2816]), ("ffn_down", [2, 2, 2816, 1024]),
           ("w_in_even", [1, 1024, 1280]), ("w_out_even", [1, 1024, 1024]), ("attn_sinks", [1, 8]),
           ("w_pool", [1, 4, 128, 128]), ("pool_scale", [1, 512]), ("w_in_odd", [1, 1024, 2048]),
           ("sgu_norm", [1, 1024]), ("w_spatial", [1, 4, 128, 128]), ("b_spatial", [1, 4, 128]),
           ("w_out_odd", [1, 1024, 1024])]


def build_program():
    nc = bass.Bass("TRN2", target_bir_lowering=False)
    din = {}

    def inp(name, shape):
        din[name] = nc.dram_tensor(name, list(shape), F32, kind="ExternalInput").ap()

    inp("x_prompt", [SEQ, D])
    inp("x_sample", [NS, D])
    inp("state_win_k", [NB_S, 128, 128])
    inp("state_win_v", [NB_S, 128, 128])
    inp("state_pool", [NB_S, 15, 512])
    for n_, s_ in W_SPECS:
        inp(n_, s_)
    for n_, s_ in CONST_SHAPES.items():
        inp("c_" + n_, s_)

    def outp(name, shape):
        return nc.dram_tensor(name, list(shape), F32, kind="ExternalOutput").ap()

    o_yp = outp("y_prompt", [SEQ, D])
    o_ys = outp("y_sample", [NS, D])
    o_kp = outp("nk_p", [128, 128])
    o_vp = outp("nv_p", [128, 128])
    o_pp = outp("np_p", [15, 512])
    o_ks = outp("nk_s", [NB_S, 128, 128])
    o_vs = outp("nv_s", [NB_S, 128, 128])
    o_ps = outp("np_s", [NB_S, 15, 512])
    o_sv = outp("sgu_v", [NS, D])
    scr = nc.dram_tensor("scr_bias", [8, 49664], F32, kind="Internal").ap()

    S = Sched()
    st = ExitStack()
    with st:
        sb = lambda name, shape, dt: st.enter_context(nc.sbuf_tensor(name, shape, dt))
        xT = sb("xT", [128, DT, NTOK], F32)
        hT = sb("hT", [128, DT, NTOK], BF16)
        ARENA_E = 26624
        arena = sb("arena", [128, ARENA_E], BF16)
        NRING = 12
        ring = sb("ring", [128, NRING, 1024], BF16)
        tmpa = sb("tmpa", [128, 8, 512], F32)
        ident = sb("ident", [128, 128], F32)
        ones_f = sb("ones_f", [128, 128], F32)
        ones_b = sb("ones_b", [128, 128], BF16)
        cols = sb("cols", [128, 64], F32)
        epsc = sb("epsc", [128, 1], F32)
        EB = sb("EB", [128, 2, 8, 128], F32)
        psum = st.enter_context(nc.psum_tensor("psum", [128, 8, 512], F32))

        r_ps = [Res(f"ps{i}") for i in range(8)]
        r_x = [[Res(f"x{d}_{t}") for t in range(5)] for d in range(DT)]
        r_h = [[Res(f"h{d}_{t}") for t in range(5)] for d in range(DT)]
        r_tmp = [Res(f"tmp{i}") for i in range(8)]
        r_const = Res("const")
        r_cols = Res("cols")
        r_ring = [Res(f"ring{i}") for i in range(NRING)]
        s_ring = [S.dsem(f"ring{i}") for i in range(NRING)]
        ring_pos = [0]

        def ps(b):
            return psum[:, b, :]

        def ld(qe, out, in_, sem, writes=(), reads=()):
            S.dma(qe, lambda e: e.dma_start(out=out, in_=in_), sem, reads=reads, writes=writes)

        def wunit(pairs):
            i = ring_pos[0] % NRING
            ring_pos[0] += 1
            for (dst_fn, src) in pairs:
                ld("pool", dst_fn(ring[:, i, :]), src, s_ring[i], writes=[r_ring[i]])
            return i

        def mm(out, lhsT, rhs, start, stop, reads, writes, signal=None):
            if signal is None:
                signal = stop
            S.op("pe", lambda e: e.matmul(out, lhsT=lhsT, rhs=rhs, start=start, stop=stop),
                 reads=reads, writes=writes, signal=signal)

        s_c = S.dsem("consts")
        ld("sp", ident[:], din["c_ident"], s_c, writes=[r_const])
        S.op("dve", lambda e: e.memset(ones_f[:], 1.0), writes=[r_const])
        S.op("dve", lambda e: e.memset(ones_b[:], 1.0), writes=[r_const])
        S.op("dve", lambda e: e.memset(epsc[:], EPS), writes=[r_const])
        prow = tmpa[:, 0, :].bitcast(F32)[0:52, 0:128]
        s_p = S.dsem("prow")
        ld("sp", prow[0:48, :], din["norm_gains"].rearrange("l i (dt p) -> (l i dt) p", p=128), s_p, writes=[r_tmp[0]])
        ld("sp", prow[48:52, :], din["pool_scale"].rearrange("o (g p) -> (o g) p", p=128), s_p, writes=[r_tmp[0]])
        S.op("pe", lambda e: e.transpose(out=psum[:, 7, 0:52], in_=prow, identity=ident[0:52, 0:52]),
             reads=[r_tmp[0], r_const], writes=[r_ps[7]])
        S.op("dve", lambda e: e.tensor_copy(out=cols[:, 0:52], in_=psum[:, 7, 0:52]), reads=[r_ps[7]], writes=[r_cols])

        def gain_col(l, i, d):
            j = (l * 3 + i) * 8 + d
            return cols[:, j:j + 1]

        NXS = 8
        xs = arena[:, 0:2048 * NXS].bitcast(F32).rearrange("p (a f) -> p a f", a=NXS)
        r_xs = [Res(f"xs{i}") for i in range(NXS)]
        s_xs = [S.dsem(f"xs{i}") for i in range(NXS)]

        def load_x(norm=None, pre_cb=None):
            for bi, (t0, n) in enumerate(BLKS):
                sl = bi % NXS
                src = din["x_prompt"][t0:t0 + n, :] if bi < 16 else din["x_sample"][:, :]
                ld("sp", xs[0:n, sl, :], src, s_xs[sl], writes=[r_xs[sl]])
                tt = min(t0 // 512, 4)
                for half in range(2):
                    pb = 4 + half
                    for j in range(4):
                        d = half * 4 + j
                        S.op("pe", lambda e, pb=pb, j=j, d=d, n=n, sl=sl: e.transpose(
                            out=psum[:, pb, j * 128:j * 128 + n], in_=xs[0:n, sl, d * 128:(d + 1) * 128],
                            identity=ident[0:n, 0:n]),
                            reads=[r_xs[sl], r_const], writes=[r_ps[pb]], signal=(j == 3))
                    dst = xT[:, half * 4:(half + 1) * 4, t0:t0 + n]
                    srcp = psum[:, pb, :].rearrange("p (j t) -> p j t", j=4)[:, :, 0:n]
                    wr = [r_x[half * 4 + j][tt] for j in range(4)]
                    if half == 0:
                        S.op("dve", lambda e, dst=dst, srcp=srcp: e.tensor_copy(out=dst, in_=srcp), reads=[r_ps[pb]], writes=wr)
                    else:
                        S.op("act", lambda e, dst=dst, srcp=srcp: e.copy(out=dst, in_=srcp), reads=[r_ps[pb]], writes=wr)
                if bi == 3 and pre_cb is not None:
                    pre_cb()
                if norm is not None and (bi % 4 == 3 or bi == 16):
                    norm_stats(tt)
                    if tt >= 1:
                        norm_apply(norm[0], norm[1], tt - 1)
            if norm is not None:
                norm_apply(norm[0], norm[1], 4)
                normed[0] = tuple(norm)

        sq_bf = tmpa[:, 0:2, :].bitcast(BF16)

        def norm_stats(tt):
            t0, n = TTS[tt]
            pb = 6 + (tt % 2)
            for d in range(DT):
                ti = d % 2
                sq = sq_bf[:, ti, 0:n]
                S.op("act", lambda e, sq=sq, d=d: e.activation(out=sq, in_=xT[:, d, t0:t0 + n], func=AF.Square),
                     reads=[r_x[d][tt]], writes=[r_tmp[ti]])
                mm(psum[:, pb, 0:n], ones_b[:], sq, d == 0, d == DT - 1, [r_tmp[ti], r_const], [r_ps[pb]], signal=True)
            ri = 2 + (tt % 2)
            sd = tmpa[:, ri, 0:n]
            S.op("act", lambda e: e.activation(out=sd, in_=psum[:, pb, 0:n], func=AF.Ln, bias=epsc[:], scale=1.0 / D),
                 reads=[r_ps[pb], r_const], writes=[r_tmp[ri]])
            S.op("act", lambda e: e.activation(out=sd, in_=sd, func=AF.Exp, scale=-0.5), reads=[r_tmp[ri]], writes=[r_tmp[ri]])

        def norm_apply(l, i, tt):
            t0, n = TTS[tt]
            ri = 2 + (tt % 2)
            sd = tmpa[:, ri, 0:n]
            for d in range(DT):
                S.op("dve", lambda e, d=d: e.scalar_tensor_tensor(
                    out=hT[:, d, t0:t0 + n], in0=xT[:, d, t0:t0 + n], scalar=gain_col(l, i, d), in1=sd,
                    op0=ALU.mult, op1=ALU.mult),
                    reads=[r_x[d][tt], r_tmp[ri], r_cols], writes=[r_h[d][tt]])

        normed = [None]

        def norm_to_h(l, i):
            if normed[0] == (l, i):
                normed[0] = None
                return
            assert normed[0] is None
            for tt in range(5):
                norm_stats(tt)
                norm_apply(l, i, tt)

        def fused_tail(emit_tile, nxt, extra=None):
            for tt in range(5):
                emit_tile(tt)
                if nxt is not None:
                    if tt >= 1:
                        norm_stats(tt - 1)
                    if tt >= 2:
                        norm_apply(nxt[0], nxt[1], tt - 2)
                if extra is not None and tt >= 1:
                    extra(tt - 1)
            if nxt is not None:
                norm_stats(4)
                norm_apply(nxt[0], nxt[1], 3)
                norm_apply(nxt[0], nxt[1], 4)
                normed[0] = tuple(nxt)
            if extra is not None:
                extra(4)

        CH = 5
        hid = arena[:, 0:2 * CH * NTOK].rearrange("p (b c t) -> p b c t", b=2, c=CH)
        r_hid = [[[Res() for _ in range(5)] for _ in range(CH)] for _ in range(2)]

        def ffn(l, j, nxt=None, mid_cb=None, after_norm=None, extra=None):
            wg, wu, wd = din["ffn_gate"][l, j], din["ffn_up"][l, j], din["ffn_down"][l, j]
            norm_to_h(l, 0 if j == 0 else 2)
            if after_norm is not None:
                after_norm()
            chunks = [[0, 1, 2, 3], [4, 5, 6, 7], [8, 9, 10, 11], [12, 13, 14, 15, 16], [17, 18, 19, 20, 21]]

            def gu(ci):
                hb = ci % 2
                for fi, ft in enumerate(chunks[ci]):
                    ug = wunit([(lambda r: r.rearrange("p (k f) -> p k f", k=8),
                                 wg[:, ft * 128:(ft + 1) * 128].rearrange("(k p) f -> p k f", p=128))])
                    uu = wunit([(lambda r: r.rearrange("p (k f) -> p k f", k=8),
                                 wu[:, ft * 128:(ft + 1) * 128].rearrange("(k p) f -> p k f", p=128))])
                    for tt, (t0, n) in enumerate(TTS):
                        par = (fi * 5 + tt) % 2
                        pg, pu = 2 * par, 2 * par + 1
                        for (un, pb) in ((ug, pg), (uu, pu)):
                            for k in range(DT):
                                mm(psum[:, pb, 0:n], ring[:, un, k * 128:(k + 1) * 128], hT[:, k, t0:t0 + n],
                                   k == 0, k == DT - 1, [r_ring[un], r_h[k][tt]], [r_ps[pb]])
                        ti = 4 + par
                        sg = tmpa[:, ti, 0:n]
                        S.op("act", lambda e, sg=sg, pg=pg, n=n: e.activation(out=sg, in_=psum[:, pg, 0:n], func=AF.Silu),
                             reads=[r_ps[pg]], writes=[r_tmp[ti]])
                        S.op("dve", lambda e, sg=sg, pu=pu, n=n, hb=hb, fi=fi, t0=t0: e.tensor_tensor(
                            out=hid[:, hb, fi, t0:t0 + n], in0=sg, in1=psum[:, pu, 0:n], op=ALU.mult),
                            reads=[r_tmp[ti], r_ps[pu]], writes=[r_hid[hb][fi][tt]])

            def down(ci, last=False):
                hb = ci % 2
                fts = chunks[ci]
                us = [wunit([(lambda r: r, wd[ft * 128:(ft + 1) * 128, :])]) for ft in fts]
                cnt = [0]

                dbanks = (4, 5, 0, 1) if last else (4, 5, 6, 7)

                def group(d, tt):
                    t0, n = TTS[tt]
                    pb = dbanks[cnt[0] % 4]
                    cnt[0] += 1
                    for fi in range(len(fts)):
                        mm(psum[:, pb, 0:n], ring[:, us[fi], d * 128:(d + 1) * 128], hid[:, hb, fi, t0:t0 + n],
                           fi == 0, fi == len(fts) - 1, [r_ring[us[fi]], r_hid[hb][fi][tt]], [r_ps[pb]])
                    S.op("dve", lambda e: e.scalar_tensor_tensor(
                        out=xT[:, d, t0:t0 + n], in0=psum[:, pb, 0:n], scalar=0.5, in1=xT[:, d, t0:t0 + n],
                        op0=ALU.mult, op1=ALU.add),
                        reads=[r_ps[pb], r_x[d][tt]], writes=[r_x[d][tt]])

                if not last:
                    for d in range(DT):
                        for tt in range(5):
                            group(d, tt)
                else:
                    fused_tail(lambda tt: [group(d, tt) for d in range(DT)], nxt, extra)

            for ci in range(len(chunks)):
                gu(ci)
                if ci > 0:
                    down(ci - 1)
                if ci == 2 and mid_cb is not None:
                    mid_cb()
            down(len(chunks) - 1, last=True)

        def A_bf(off, n):
            return arena[:, off:off + n]

        def A_f32(off, n_f32):
            return arena[:, off:off + 2 * n_f32].bitcast(F32)

        tmp_bf = tmpa[:, 4:6, :].bitcast(BF16).rearrange("p a (b f) -> p (a b) f", b=2)

        def evac_add_x(pb, d, tt, t0, n):
            S.op("dve", lambda e: e.tensor_tensor(out=xT[:, d, t0:t0 + n], in0=psum[:, pb, 0:n], in1=xT[:, d, t0:t0 + n], op=ALU.add),
                 reads=[r_ps[pb], r_x[d][tt]], writes=[r_x[d][tt]])

        def kunit(w2d, c0, ncol=128):
            return wunit([(lambda r: r.rearrange("p (k f) -> p k f", k=8)[:, :, 0:ncol],
                           w2d[:, c0:c0 + ncol].rearrange("(k p) f -> p k f", p=128))])

        def ru(i):
            return ring[:, i, :].rearrange("p (k f) -> p k f", k=8)

        es_bc = sb("es_bc", [128, 4, 128], F32)
        tmpb = sb("tmpb", [128, 2, 512], F32)
        r_tmpb = [Res("tmpb0"), Res("tmpb1")]
        r_EB = Res("EB")
        r_es = Res("es")

        prep_state = {}

        def prep_bias():
            rb = tmpa[0:32, 4, 0:8]
            oh = tmpa[0:32, 5, 0:384]
            Lh = tmpa[0:32, 6:8, :].rearrange("p a f -> p (a f)").rearrange("p (h m) -> p h m", h=8)
            ers = [arena[:, 21120:21888].bitcast(F32), arena[:, 21888:22656].bitcast(F32)]
            r_er = [Res("er0"), Res("er1")]
            s_b = S.dsem("biasld")
            ld("sp", rb, din["rel_bias"], s_b, writes=[r_tmp[4]])
            ld("sp", oh, din["c_onehot"], S.dsem("ohld"), writes=[r_tmp[5]])
            S.op("act", lambda e: e.activation(out=rb, in_=rb, func=AF.Exp), reads=[r_tmp[4]], writes=[r_tmp[4]])
            S.op("dve", lambda e: e.tensor_copy(out=Lh, in_=rb.unsqueeze(2).broadcast_to([32, 8, 128])),
                 reads=[r_tmp[4]], writes=[r_tmp[6], r_tmp[7]])
            s_scr = S.dsem("scrw")
            r_scr = Res("scr")
            for h in range(8):
                pb = 2 + (h % 2)
                ti = h % 2
                mm(psum[:, pb, 0:384], Lh[:, h, :], oh, True, True, [r_tmp[5], r_tmp[6], r_tmp[7]], [r_ps[pb]])
                er = ers[ti]
                S.op("dve", lambda e, er=er, pb=pb: e.tensor_copy(out=er, in_=psum[:, pb, 0:384]), reads=[r_ps[pb]], writes=[r_er[ti]])
                ld("pool", scr[h, 0:128 * 384].rearrange("(k i) -> k i", i=384), er, s_scr, reads=[r_er[ti]], writes=[r_scr])
            prep_state["r_scr"] = r_scr

        def prep_bias_b():
            r_scr = prep_state["r_scr"]
            s_eb = S.dsem("ebld")
            for kb, off in ((0, 128), (1, 256)):
                src = scr[:, off:off + 128 * 383].rearrange("h (k i) -> k h i", i=383)[:, :, 0:128]
                ld("pool", EB[:, kb, :, :], src, s_eb, reads=[r_scr], writes=[r_EB])
            es8 = cols[:, 56:60]
            s_es = S.dsem("esld")
            for g in range(2):
                ld("sp", es8[g * 64:(g + 1) * 64, :], din["attn_sinks"][0, 4 * g:4 * g + 4].partition_broadcast(64), s_es, writes=[r_es])
            S.op("act", lambda e: e.activation(out=es8, in_=es8, func=AF.Exp), reads=[r_es], writes=[r_es])
            S.op("dve", lambda e: e.tensor_copy(out=es_bc[:], in_=es8.unsqueeze(2).broadcast_to([128, 4, 128])),
                 reads=[r_es], writes=[r_es])

        def odd_mixer(l, nxt=None):
            o = l // 2
            S.barrier()
            norm_to_h(l, 1)
            w_in, w_out = din["w_in_odd"][o], din["w_out_odd"][o]
            uT = A_bf(0, 8 * NTOK).rearrange("p (c t) -> p c t", c=8)
            vt32 = A_f32(16896, 2048).rearrange("p (a f) -> p a f", a=2)
            vn = A_bf(20992, 2048).rearrange("p (a f) -> p a f", a=2)
            wsT = A_bf(23040, 512).rearrange("p (g t) -> p g t", g=4)
            wsbd = A_bf(23552, 256).rearrange("p (g t) -> p g t", g=4)
            sgn = A_f32(23808, 1024)
            trim = A_f32(25856, 128)
            bdm = A_f32(26112, 64)
            sel4f = A_f32(26240, 64)
            sel4b = A_bf(26368, 64)
            bsp = tmpa[:, 6, :].rearrange("p (g t) -> p g t", g=4)
            wst = tmpa[:, 7, :].rearrange("p (g t) -> p g t", g=4)
            r_u = [[Res() for _ in range(5)] for _ in range(8)]
            r_vt = [Res(), Res()]
            r_vn = [Res(), Res()]
            r_c = Res("oddc")
            r_ws = Res("wsT")
            s_c2 = S.dsem("oddc")
            ld("sp", wst, w_sp_ap(o), S.dsem("wst"), writes=[r_tmp[7]])
            ld("sp", bsp, din["b_spatial"][o].partition_broadcast(128), S.dsem("bsp"), writes=[r_tmp[6]])
            ld("sp", sgn, din["sgu_norm"][o].partition_broadcast(128), s_c2, writes=[r_c])
            ld("sp", trim, din["c_trimask"], s_c2, writes=[r_c])
            ld("sp", bdm[0:64, :], din["c_bdmask"], s_c2, writes=[r_c])
            ld("sp", sel4f[0:4, :], din["c_sel4"], s_c2, writes=[r_c])
            cnt = 0
            for ft in range(8):
                un = kunit(w_in, ft * 128)
                for tt, (t0, n) in enumerate(TTS):
                    pb = cnt % 4
                    cnt += 1
                    for k in range(DT):
                        mm(psum[:, pb, 0:n], ru(un)[:, k, :], hT[:, k, t0:t0 + n], k == 0, k == DT - 1,
                           [r_ring[un], r_h[k][tt]], [r_ps[pb]])
                    S.op("act", lambda e, pb=pb, ft=ft, t0=t0, n=n: e.activation(out=uT[:, ft, t0:t0 + n], in_=psum[:, pb, 0:n],
                                                                                func=AF.Gelu_apprx_tanh),
                         reads=[r_ps[pb]], writes=[r_u[ft][tt]])
            S.op("dve", lambda e: e.tensor_copy(out=sel4b[0:4, :], in_=sel4f[0:4, :]), reads=[r_c], writes=[r_c])
            for g in range(4):
                S.op("pe", lambda e, g=g: e.transpose(out=psum[:, 6, g * 128:(g + 1) * 128], in_=wst[:, g, :], identity=ident[:]),
                     reads=[r_tmp[7], r_const], writes=[r_ps[6]], signal=(g == 3))
            S.op("dve", lambda e: e.tensor_tensor(out=wsT, in0=psum[:, 6, :].rearrange("p (g t) -> p g t", g=4),
                                                  in1=trim.unsqueeze(1).broadcast_to([128, 4, 128]), op=ALU.mult),
                 reads=[r_ps[6], r_c], writes=[r_ws])
            for g in range(4):
                mm(psum[0:64, 7, g * 64:(g + 1) * 64], sel4b[0:4, :],
                   wsT[0:4, g, 0:4].unsqueeze(1).broadcast_to([4, 16, 4]), True, True, [r_ws, r_c], [r_ps[7]], signal=(g == 3))
            S.op("dve", lambda e: e.tensor_tensor(out=wsbd[0:64, :, :], in0=psum[0:64, 7, 0:256].rearrange("p (g t) -> p g t", g=4),
                                                  in1=bdm[0:64, :].unsqueeze(1).broadcast_to([64, 4, 64]), op=ALU.mult),
                 reads=[r_ps[7], r_c], writes=[r_ws])
            uv = [wunit([(lambda r: r, w_in[k * 128:(k + 1) * 128, 1024:2048])]) for k in range(8)]
            junk = tmpa[:, 4, :].rearrange("p (a f) -> p a f", a=1)
            junk2 = tmpa[:, 4:6, :]
            r_ssq = [Res(), Res()]
            r_mix = [[Res(), Res()], [Res(), Res()]]
            s_sv = S.dsem("sv")
            def v_stage1(bi):
                t0, n = BLKS[bi]
                sl = bi % 2
                tt = min(t0 // 512, 4)
                vb = (0, 1) if sl == 0 else (4, 5)
                sbk = (2, 3) if sl == 0 else (6, 7)
                ssq = cols[:, 60 + 2 * sl:62 + 2 * sl]
                for half in range(2):
                    pb = vb[half]
                    for k in range(8):
                        mm(psum[0:n, pb, :], hT[:, k, t0:t0 + n], ring[:, uv[k], half * 512:(half + 1) * 512],
                           k == 0, k == 7, [r_ring[uv[k]], r_h[k][tt]], [r_ps[pb]])
                    S.op("act", lambda e, pb=pb, n=n, sl=sl, half=half: e.activation(
                        out=vt32[0:n, sl, half * 512:(half + 1) * 512], in_=psum[0:n, pb, :], func=AF.Gelu_apprx_tanh),
                        reads=[r_ps[pb]], writes=[r_vt[sl]])
                S.op("act", lambda e, n=n, sl=sl, ssq=ssq: e.activation(
                    out=junk2[0:n, :, :], in_=vt32[0:n, sl, :].rearrange("p (a f) -> p a f", a=2), func=AF.Square,
                    accum_out=ssq[0:n, 0:1]),
                    reads=[r_vt[sl]], writes=[r_ssq[sl]])
                S.op("act", lambda e, n=n, ssq=ssq: e.activation(out=ssq[0:n, 1:2], in_=ssq[0:n, 0:1], func=AF.Sqrt, bias=epsc[0:n, :], scale=1.0 / 1024),
                     reads=[r_ssq[sl], r_const], writes=[r_ssq[sl]])
                S.op("dve", lambda e, n=n, ssq=ssq: e.reciprocal(out=ssq[0:n, 1:2], in_=ssq[0:n, 1:2]), reads=[r_ssq[sl]], writes=[r_ssq[sl]])
                if bi < 16:
                    S.op("dve", lambda e, n=n, sl=sl, ssq=ssq: e.scalar_tensor_tensor(
                        out=vn[0:n, sl, :], in0=vt32[0:n, sl, :], scalar=ssq[0:n, 1:2], in1=sgn[0:n, :], op0=ALU.mult, op1=ALU.mult),
                        reads=[r_vt[sl], r_ssq[sl], r_c], writes=[r_vn[sl]])
                else:
                    S.op("dve", lambda e, n=n, sl=sl, ssq=ssq: e.scalar_tensor_tensor(
                        out=vt32[0:n, sl, :], in0=vt32[0:n, sl, :], scalar=ssq[0:n, 1:2], in1=sgn[0:n, :], op0=ALU.mult, op1=ALU.mult),
                        reads=[r_vt[sl], r_ssq[sl], r_c], writes=[r_vt[sl]])
                    S.op("dve", lambda e, n=n, sl=sl: e.tensor_copy(out=vn[0:n, sl, :], in_=vt32[0:n, sl, :]),
                         reads=[r_vt[sl]], writes=[r_vn[sl]])
                    ld("sp", o_sv[:, :], vt32[0:n, sl, :], s_sv, reads=[r_vt[sl]])

            def v_stage2(bi):
                t0, n = BLKS[bi]
                sl = bi % 2
                tt = min(t0 // 512, 4)
                vb = (0, 1) if sl == 0 else (4, 5)
                sbk = (2, 3) if sl == 0 else (6, 7)
                for half in range(2):
                    pb = sbk[half]
                    for c4 in range(4):
                        ct = half * 4 + c4
                        g = ct // 2
                        rhs = wsT[0:n, g, 0:n] if bi < 16 else wsbd[0:n, g, 0:n]
                        mm(psum[:, pb, c4 * 128:c4 * 128 + n], vn[0:n, sl, ct * 128:(ct + 1) * 128], rhs, True, True,
                           [r_vn[sl], r_ws], [r_ps[pb]], signal=(c4 == 3))
                    ti = 2 * sl + half
                    rm = r_tmp[ti]
                    mix = tmpa[:, ti, :].rearrange("p (c t) -> p c t", c=4)
                    pv = psum[:, pb, :].rearrange("p (c t) -> p c t", c=4)
                    for gg in range(2):
                        g = half * 2 + gg
                        if bi < 16:
                            in1 = bsp[:, g:g + 1, 0:n].broadcast_to([128, 2, n])
                            S.op("dve", lambda e, mix=mix, pv=pv, gg=gg, n=n, in1=in1: e.tensor_tensor(
                                out=mix[:, 2 * gg:2 * gg + 2, 0:n], in0=pv[:, 2 * gg:2 * gg + 2, 0:n], in1=in1, op=ALU.add),
                                reads=[r_ps[pb], r_tmp[6]], writes=[rm])
                        else:
                            in1 = bsp[:, g, 0:4].unsqueeze(1).unsqueeze(1).broadcast_to([128, 2, 16, 4])
                            S.op("dve", lambda e, mix=mix, pv=pv, gg=gg, n=n, in1=in1: e.tensor_tensor(
                                out=mix[:, 2 * gg:2 * gg + 2, 0:n].rearrange("p c (b t) -> p c b t", t=4),
                                in0=pv[:, 2 * gg:2 * gg + 2, 0:n].rearrange("p c (b t) -> p c b t", t=4), in1=in1, op=ALU.add),
                                reads=[r_ps[pb], r_tmp[6]], writes=[rm])
                    S.op("dve", lambda e, mix=mix, half=half, t0=t0, n=n: e.tensor_tensor(
                        out=uT[:, half * 4:(half + 1) * 4, t0:t0 + n], in0=mix[:, :, 0:n], in1=uT[:, half * 4:(half + 1) * 4, t0:t0 + n],
                        op=ALU.mult),
                        reads=[rm] + [r_u[half * 4 + c][tt] for c in range(4)],
                        writes=[r_u[half * 4 + c][tt] for c in range(4)])

            v_stage1(0)
            for bi in range(len(BLKS)):
                if bi + 1 < len(BLKS):
                    v_stage1(bi + 1)
                v_stage2(bi)
            ous = [kunit(w_out, d * 128) for d in range(DT)]
            ocnt = [0]

            def oproj_tile(tt):
                t0, n = TTS[tt]
                for d in range(DT):
                    pb = (4, 5, 0, 1)[ocnt[0] % 4]
                    ocnt[0] += 1
                    for k in range(8):
                        mm(psum[:, pb, 0:n], ru(ous[d])[:, k, :], uT[:, k, t0:t0 + n], k == 0, k == 7, [r_ring[ous[d]], r_u[k][tt]], [r_ps[pb]])
                    evac_add_x(pb, d, tt, t0, n)

            fused_tail(oproj_tile, nxt)
            S.barrier(dma_only=True)

        def w_sp_ap(o):
            return din["w_spatial"][o].rearrange("g t s -> t g s")

        def even_mixer(l, nxt=None):
            ev = l // 2
            S.barrier()
            norm_to_h(l, 1)
            w_in, w_out = din["w_in_even"][ev], din["w_out_even"][ev]
            if SUB < 2:
                return
            dT = A_bf(0, 4 * NTOK).rearrange("p (g t) -> p g t", g=4)
            utb = A_bf(8448, 3 * 512).rearrange("p (a c) -> p a c", a=3)
            ut32 = A_f32(9984, 512)
            poolA = A_bf(11008, 512).rearrange("p (g t) -> p g t", g=4)
            poolB = A_bf(11520, 512).rearrange("p (g t) -> p g t", g=4)
            poolA0 = A_f32(12032, 512).rearrange("p (g t) -> p g t", g=4)
            Us = A_f32(14784, 4 * 16 * 19).rearrange("p (g b t) -> p g b t", g=4, b=16)
            Ws1 = A_f32(17216, 304).rearrange("p (b t) -> p b t", b=16)
            Ws2 = A_f32(17824, 304).rearrange("p (b t) -> p b t", b=16)
            invc = A_f32(18432, 64).rearrange("p (g t) -> p g t", g=4)
            wpool = A_bf(25280, 512).rearrange("p (g d) -> p g d", g=4)
            ctxs = A_f32(19072, 1024).rearrange("p (a f) -> p a f", a=2)
            uptok = A_f32(19072, 1024).rearrange("p (a f) -> p a f", a=2)
            t16 = A_f32(18560, 16)
            r_dT = [[Res() for _ in range(5)] for _ in range(4)]
            r_utb = [Res() for _ in range(3)]
            r_ut32 = Res("ut32")
            r_pm = Res("poolmats")
            r_Us = [Res() for _ in range(4)]
            r_Ws = [Res(), Res()]
            r_pc = Res("poolc")
            r_ctx = [Res(), Res()]
            r_upt = Res("uptok")
            s_pc = S.dsem("poolc")
            s_ctx = [S.dsem("ctx0"), S.dsem("ctx1")]
            ld("sp", poolA0, din["c_poolA0"], s_pc, writes=[r_pc])
            s_wp = S.dsem("wpool")
            r_wp = Res("wpool")
            ld("pool", wpool, din["w_pool"][ev].rearrange("g c d -> c g d"), s_wp, writes=[r_wp])
            s_o = S.dsem("outs")
            ld("sp", o_ps[:, 0:11, :], din["state_pool"][:, 4:15, :], s_o)
            kbuf = A_bf(21184, 2048).rearrange("p (b c) -> p b c", b=16)
            skst = tmpa[:, 6, :].rearrange("p (a c) -> p a c", a=4)
            r_kb = Res("kbuf")
            def ctx_prep():
                for hb in range(2):
                    for g in range(4):
                        S.op("pe", lambda e, g=g, hb=hb: e.transpose(out=psum[:, 6, g * 128:g * 128 + 120], in_=ctxs[0:120, hb, g * 128:(g + 1) * 128],
                                                                    identity=ident[0:120, 0:120]),
                             reads=[r_ctx[hb], r_const], writes=[r_ps[6]], signal=(g == 3))
                    S.op("dve", lambda e, hb=hb: e.tensor_copy(
                        out=Us[:, :, hb * 8:(hb + 1) * 8, 0:15],
                        in_=psum[:, 6, :].rearrange("p (g t) -> p g t", g=4)[:, :, 0:120].rearrange("p g (b r) -> p g b r", r=15)),
                        reads=[r_ps[6]], writes=r_Us)

            for hb in range(2):
                ld("sp", ctxs[0:120, hb, :], din["state_pool"][hb * 8:(hb + 1) * 8].rearrange("b r c -> (b r) c"), s_ctx[hb], writes=[r_ctx[hb]])
            s_sk = [S.dsem(f"sk{i}") for i in range(4)]
            r_sk = [Res() for _ in range(4)]

            def kbuf_prep(b4):
                for j in range(4):
                    b = b4 * 4 + j
                    ld("sp", skst[:, j, :], din["state_win_k"][b], s_sk[j], writes=[r_sk[j]])
                    S.op("pe", lambda e, j=j: e.transpose(out=psum[:, 7, j * 128:(j + 1) * 128], in_=skst[:, j, :], identity=ident[:]),
                         reads=[r_sk[j], r_const], writes=[r_ps[7]], signal=True)
                S.op("dve", lambda e: e.tensor_copy(out=kbuf[:, b4 * 4:(b4 + 1) * 4, :], in_=psum[:, 7, :].rearrange("p (a c) -> p a c", a=4)),
                     reads=[r_ps[7]], writes=[r_kb])

            if SUB < 2.2:
                return
            uns = [kunit(w_in, 768 + g * 128) for g in range(4)]
            urs = [wunit([(lambda r: r.rearrange("p (a c) -> p a c", a=2),
                           w_in[2 * i * 128:(2 * i + 2) * 128, 768:1280].rearrange("(a p) c -> p a c", p=128))]) for i in range(4)]
            s_pm = S.dsem("poolmats")
            allhid = [r for a_ in r_hid for b_ in a_ for r in b_]
            ld("pool", poolA, din["c_poolA"], s_pm, writes=[r_pm] + allhid)
            ld("pool", poolB, din["c_poolB"], s_pm, writes=[r_pm])
            pcnt = [0]

            def pool_proj(g, tt):
                t0, n = TTS[tt]
                pb = 6 + (pcnt[0] % 2)
                pcnt[0] += 1
                mm(psum[:, pb, 0:n], wpool[:, g, :], dT[:, g, t0:t0 + n], True, True, [r_wp, r_dT[g][tt]], [r_ps[pb]])
                S.op("act", lambda e: e.activation(out=dT[:, g, t0:t0 + n], in_=psum[:, pb, 0:n], func=AF.Identity,
                                                   scale=cols[:, 48 + g:49 + g]),
                     reads=[r_ps[pb], r_cols], writes=[r_dT[g][tt]])

            def pool_block_a(b):
                t0 = b * 128
                tt = b // 4
                bx = b % 2
                ui = b % 3
                for k in range(DT):
                    un = urs[k // 2]
                    mm(psum[:, bx, :], hT[:, k, t0:t0 + 128], ring[:, un, (k % 2) * 512:(k % 2 + 1) * 512], k == 0, k == DT - 1,
                       [r_ring[un], r_h[k][tt]], [r_ps[bx]])
                S.op("act", lambda e: e.copy(out=utb[:, ui, :], in_=psum[:, bx, :]), reads=[r_ps[bx]], writes=[r_utb[ui]])
                if b == 0:
                    S.op("act", lambda e: e.copy(out=ut32, in_=psum[:, bx, :]), reads=[r_ps[bx]], writes=[r_ut32])
                if b == 15:
                    S.op("act", lambda e: e.copy(out=uptok[:, 0, :], in_=psum[:, bx, :]), reads=[r_ps[bx]], writes=[r_upt])

            def pool_block_b(b):
                t0 = b * 128
                tt = b // 4
                by = 2 + (b % 2)
                ui = b % 3
                for g in range(4):
                    gc = slice(g * 128, (g + 1) * 128)
                    if b == 0:
                        mm(psum[:, by, gc], ut32[:, gc], poolA0[:, g, :], True, True, [r_ut32, r_pc], [r_ps[by]], signal=(g == 3))
                    else:
                        mm(psum[:, by, gc], utb[:, ui, gc], poolA[:, g, :], True, False, [r_utb[ui], r_pm], [r_ps[by]], signal=False)
                        mm(psum[:, by, gc], utb[:, (b - 1) % 3, gc], poolB[:, g, :], False, True, [r_utb[(b - 1) % 3], r_pm], [r_ps[by]],
                           signal=(g == 3))
                S.op("dve", lambda e: e.tensor_copy(out=dT[:, :, t0:t0 + 128], in_=psum[:, by, :].rearrange("p (g t) -> p g t", g=4)),
                     reads=[r_ps[by]], writes=[r_dT[g][tt] for g in range(4)])

            def pool_step(g, tt):
                w = 2 ** (g + 1)
                un = uns[g]
                t0, n = TTS[tt]
                pb = g
                for k in range(DT):
                    mm(psum[:, pb, 0:n], ru(un)[:, k, :], hT[:, k, t0:t0 + n], k == 0, k == DT - 1, [r_ring[un], r_h[k][tt]], [r_ps[pb]])
                Ug = Us[:, g, :, :]
                S.op("act", lambda e: e.copy(out=Ug[:, :, 15:19], in_=psum[:, pb, 0:64].rearrange("p (b t) -> p b t", t=4)),
                     reads=[r_ps[pb]], writes=[r_Us[g]])
                bufs = [Ws1, Ws2]
                src, rs = Ug, r_Us[g]
                sh = 1
                lo = 1
                for step in range(g + 1):
                    dst, rd = bufs[step % 2], r_Ws[step % 2]
                    S.op("dve", lambda e, dst=dst, src=src, lo=lo, sh=sh: e.tensor_tensor(
                        out=dst[:, :, lo:19], in0=src[:, :, lo:19], in1=src[:, :, lo - sh:19 - sh], op=ALU.add),
                        reads=[rs], writes=[rd])
                    src, rs = dst, rd
                    sh *= 2
                    lo += sh
                S.op("dve", lambda e, src=src: e.scalar_tensor_tensor(
                    out=dT[:, g, 2048:2112].rearrange("p (b t) -> p b t", t=4), in0=src[:, :, 15:19], scalar=1.0 / w,
                    in1=Ug[:, :, 15:19], op0=ALU.mult, op1=ALU.subtract),
                    reads=[rs, r_Us[g]], writes=[r_dT[g][4]])

            pool_block_a(0)
            for b in range(16):
                if b + 1 < 16:
                    pool_block_a(b + 1)
                pool_block_b(b)
                if b % 4 == 1 and b >= 5:
                    for g in range(4):
                        pool_proj(g, b // 4 - 1)
                if b % 4 == 2:
                    kbuf_prep(b // 4)
                if b == 11:
                    ctx_prep()
            for g in range(4):
                pool_step(g, 4)
            for g in range(4):
                pool_proj(g, 3)
            for g in range(4):
                for (row, t0, n, tt) in ((1, 2048, 64, 4),):
                    pb = 4 + row
                    for k in range(DT):
                        mm(psum[0:n, pb, g * 128:(g + 1) * 128], hT[:, k, t0:t0 + n], ru(uns[g])[:, k, :], k == 0, k == DT - 1,
                           [r_ring[uns[g]], r_h[k][tt]], [r_ps[pb]])
            for g in range(4):
                pool_proj(g, 4)
            for row, n in ((1, 64),):
                S.op("act", lambda e, row=row, n=n: e.copy(out=uptok[0:n, row, :], in_=psum[0:n, 4 + row, :]), reads=[r_ps[4 + row]], writes=[r_upt])
            s_up = S.dsem("uptok_out")
            ld("sp", o_pp[:, :], uptok[113:128, 0, :], s_up, reads=[r_upt])
            for b in range(16):
                ld("sp", o_ps[b, 11:15, :], uptok[4 * b:4 * b + 4, 1, :], s_up, reads=[r_upt])
            if SUB < 3:
                return
            q = A_bf(8448, 4 * NTOK).rearrange("p (h t) -> p h t", h=4)
            kT = A_bf(16896, NTOK)
            vtok = A_bf(19008, 17 * 128).rearrange("p (b c) -> p b c", b=17)
            vbuf = A_bf(23232, 2048).rearrange("p (b c) -> p b c", b=16)
            EBS = A_f32(25280, 512).rearrange("p (h t) -> p h t", h=8)
            tokst = tmpa[:, 7, :].rearrange("p (a c) -> p a c", a=4)
            PT = tmp_bf
            r_q = [[Res() for _ in range(5)] for _ in range(4)]
            r_k = [Res() for _ in range(5)]
            r_vt = [Res() for _ in range(17)]
            r_vb, r_ebs, r_tok = Res(), Res(), [Res() for _ in range(4)]
            S.barrier(skip=(s_up,))
            s_vb = S.dsem("vbuf")
            ld("sp", o_ks[:, 0:124, :], din["state_win_k"][:, 4:128, :], s_o)
            ld("sp", o_vs[:, 0:124, :], din["state_win_v"][:, 4:128, :], s_o)
            if SUB < 3.3:
                return
            cnt = 0
            for hh in range(4):
                un = wunit([(lambda r: r.rearrange("p (k f) -> p k f", k=8)[:, :, 0:64],
                             w_in[:, hh * 64:(hh + 1) * 64].rearrange("(k p) f -> p k f", p=128)),
                            (lambda r: r.rearrange("p (k f) -> p k f", k=8)[:, :, 64:128],
                             w_in[:, (4 + hh) * 64:(5 + hh) * 64].rearrange("(k p) f -> p k f", p=128))])
                for tt, (t0, n) in enumerate(TTS):
                    pb = cnt % 4
                    cnt += 1
                    for k in range(DT):
                        mm(psum[:, pb, 0:n], ru(un)[:, k, :], hT[:, k, t0:t0 + n], k == 0, k == DT - 1, [r_ring[un], r_h[k][tt]], [r_ps[pb]])
                    S.op("act", lambda e, pb=pb, hh=hh, t0=t0, n=n: e.mul(out=q[:, hh, t0:t0 + n], in_=psum[:, pb, 0:n], mul=0.125),
                         reads=[r_ps[pb]], writes=[r_q[hh][tt]])
            if SUB < 3.4:
                return
            un = kunit(w_in, 512)
            for tt, (t0, n) in enumerate(TTS):
                pb = cnt % 4
                cnt += 1
                for k in range(DT):
                    mm(psum[:, pb, 0:n], ru(un)[:, k, :], hT[:, k, t0:t0 + n], k == 0, k == DT - 1, [r_ring[un], r_h[k][tt]], [r_ps[pb]])
                S.op("dve", lambda e, pb=pb, t0=t0, n=n: e.tensor_copy(out=kT[:, t0:t0 + n], in_=psum[:, pb, 0:n]), reads=[r_ps[pb]], writes=[r_k[tt]])
            for (row, t0, n, tt) in ((0, 1920, 128, 3), (1, 2048, 64, 4)):
                for k in range(DT):
                    mm(psum[0:n, 4, row * 128:(row + 1) * 128], hT[:, k, t0:t0 + n], ru(un)[:, k, :], k == 0, k == DT - 1,
                       [r_ring[un], r_h[k][tt]], [r_ps[4]])
                S.op("act", lambda e, row=row, n=n: e.copy(out=tokst[0:n, row, :], in_=psum[0:n, 4, row * 128:(row + 1) * 128]),
                     reads=[r_ps[4]], writes=[r_tok[row]])
            s_tk = S.dsem("tokouts")
            ld("sp", o_kp[:, :], tokst[:, 0, :], s_tk, reads=[r_tok[0]])
            for b in range(16):
                ld("sp", o_ks[b, 124:128, :], tokst[4 * b:4 * b + 4, 1, :], s_tk, reads=[r_tok[1]])
            if SUB < 3.5:
                return
            un = kunit(w_in, 640)
            ld("pool", vbuf, din["state_win_v"].rearrange("b k c -> k b c"), s_vb, writes=[r_vb])
            for bi, (t0, n) in enumerate(BLKS):
                tt = min(t0 // 512, 4)
                pb = 5 + (bi % 2) * 2
                for k in range(DT):
                    mm(psum[0:n, pb, 0:128], hT[:, k, t0:t0 + n], ru(un)[:, k, :], k == 0, k == DT - 1, [r_ring[un], r_h[k][tt]], [r_ps[pb]])
                S.op("dve", lambda e, pb=pb, bi=bi, n=n: e.tensor_copy(out=vtok[0:n, bi, :], in_=psum[0:n, pb, 0:128]), reads=[r_ps[pb]], writes=[r_vt[bi], r_upt])
                if bi >= 15:
                    row = 2 + (bi - 15)
                    S.op("dve", lambda e, pb=pb, row=row, n=n: e.tensor_copy(out=tokst[0:n, row, :], in_=psum[0:n, pb, 0:128]), reads=[r_ps[pb]], writes=[r_tok[row]])
            if not os.environ.get("MK_NOVOUT"):
                ld("sp", o_vp[:, :], tokst[:, 2, :], s_tk, reads=[r_tok[2]])
                for b in range(16):
                    ld("sp", o_vs[b, 124:128, :], tokst[4 * b:4 * b + 4, 3, :], s_tk, reads=[r_tok[3]])
            S.op("dve", lambda e: e.memset(EBS[0:64, :, :], 0.0), writes=[r_ebs])
            s_ebs = S.dsem("ebs")
            for b in range(16):
                ld("sp", EBS[4 * b:4 * b + 4, :, 4 * b:4 * b + 4], EB[0:4, 0, :, 0:4], s_ebs, reads=[r_EB], writes=[r_ebs])
            attnT = hT
            etc = [0]

            PTB = tmpa[:, 6:8, :].bitcast(BF16).rearrange("p a (b f) -> p (a b) f", b=2)
            PTs = (PT, PTB)
            tok_all = list(r_tok)

            def pt_res(par, pb):
                return r_tmp[(4 if par == 0 else 6) + pb // 2]

            def att_scores(bi):
                t0 = bi * 128
                tt = min(t0 // 512, 4)
                par = bi % 2
                kbs = [(0, bi)] + ([(1, bi - 1)] if bi > 0 else [])
                for g in range(2):
                    gp = slice(g * 64, (g + 1) * 64)
                    for (kb, kblk) in kbs:
                        pb = g * 2 + kb
                        k0 = kblk * 128
                        ktt = min(k0 // 512, 4)
                        mm(psum[:, pb, :], kT[gp, k0:k0 + 128], q[gp, :, t0:t0 + 128], True, True,
                           [r_k[ktt]] + [r_q[hh][tt] for hh in range(4)], [r_ps[pb]])
                        ti = etc[0] % 4
                        etc[0] += 1
                        et = tmpa[:, ti, :] if ti < 2 else tmpb[:, ti - 2, :]
                        r_et = r_tmp[ti] if ti < 2 else r_tmpb[ti - 2]
                        S.op("act", lambda e, et=et, pb=pb: e.activation(out=et, in_=psum[:, pb, :], func=AF.Exp), reads=[r_ps[pb]], writes=[r_et])
                        S.op("dve", lambda e, et=et, pb=pb, kb=kb, g=g, par=par: e.tensor_tensor(
                            out=PTs[par][:, pb, :].rearrange("p (h t) -> p h t", h=4), in0=et.rearrange("p (h t) -> p h t", h=4),
                            in1=EB[:, kb, 4 * g:4 * g + 4, :], op=ALU.mult),
                            reads=[r_et, r_EB], writes=[pt_res(par, pb)] + (tok_all if par == 1 else []))

            def att_av(bi):
                t0 = bi * 128
                tt = min(t0 // 512, 4)
                par = bi % 2
                bo, bd = (4, 5) if par == 0 else (6, 7)
                kbs = [(0, bi)] + ([(1, bi - 1)] if bi > 0 else [])
                for (bank, is_den) in ((bo, False), (bd, True)):
                    for g in range(2):
                        gp = slice(g * 64, (g + 1) * 64)
                        for i, (kb, kblk) in enumerate(kbs):
                            pb = g * 2 + kb
                            lhs = ones_b[:, 0:64] if is_den else vtok[:, kblk, g * 64:(g + 1) * 64]
                            mm(psum[gp, bank, :], lhs, PTs[par][:, pb, :], i == 0, i == len(kbs) - 1,
                               [r_const if is_den else r_vt[kblk], pt_res(par, pb)], [r_ps[bank]], signal=(g == 1 and i == len(kbs) - 1))
                ds = 2 + par
                den = tmpa[:, ds, :]
                S.op("dve", lambda e, den=den, bd=bd: e.tensor_tensor(out=den, in0=psum[:, bd, :], in1=es_bc[:].rearrange("p h t -> p (h t)"), op=ALU.add),
                     reads=[r_ps[bd], r_es], writes=[r_tmp[ds]])
                S.op("act", lambda e, den=den: e.activation(out=den, in_=den, func=AF.Ln), reads=[r_tmp[ds]], writes=[r_tmp[ds]])
                S.op("act", lambda e, den=den: e.activation(out=den, in_=den, func=AF.Exp, scale=-1.0), reads=[r_tmp[ds]], writes=[r_tmp[ds]])
                S.op("dve", lambda e, t0=t0, den=den, bo=bo: e.tensor_tensor(out=attnT[:, 0:4, t0:t0 + 128], in0=psum[:, bo, :].rearrange("p (h t) -> p h t", h=4),
                                                                          in1=den.rearrange("p (h t) -> p h t", h=4), op=ALU.mult),
                     reads=[r_ps[bo], r_tmp[ds]], writes=[r_h[hh][tt] for hh in range(4)])

            if SUB < 4:
                return
            att_scores(0)
            for bi in range(16):
                if bi + 1 < 16:
                    att_scores(bi + 1)
                att_av(bi)
            if SUB < 5:
                return

            def sample_attn():
                t0, n, tt = 2048, 64, 4
                v4 = lambda ap: ap.rearrange("p (b h t) -> p b h t", b=16, h=4)
                for b in range(16):
                    for g in range(2):
                        gp = slice(g * 64, (g + 1) * 64)
                        mm(psum[:, g, b * 16:(b + 1) * 16], kbuf[gp, b, :], q[gp, :, t0 + 4 * b:t0 + 4 * b + 4], True, True,
                           [r_kb] + [r_q[hh][4] for hh in range(4)], [r_ps[g]], signal=(b == 15))
                for g in range(2):
                    gp = slice(g * 64, (g + 1) * 64)
                    mm(psum[0:64, 2 + g, 0:256], kT[gp, t0:t0 + 64], q[gp, :, t0:t0 + 64].rearrange("p h (b t) -> p b h t", t=4), True, True,
                       [r_k[4]] + [r_q[hh][4] for hh in range(4)], [r_ps[2 + g]], signal=True)
                et0, et1 = tmpa[:, 0, :], tmpa[:, 1, :]
                for g in range(2):
                    cs = slice(g * 256, (g + 1) * 256)
                    S.op("act", lambda e, g=g, cs=cs: e.activation(out=et0[:, cs], in_=psum[:, g, 0:256], func=AF.Exp),
                         reads=[r_ps[g]], writes=[r_tmp[0]])
                    S.op("dve", lambda e, g=g, cs=cs: e.tensor_tensor(
                        out=v4(PT[:, 0, cs]), in0=v4(et0[:, cs]),
                        in1=EB[:, 1, 4 * g:4 * g + 4, 0:4].unsqueeze(1).broadcast_to([128, 16, 4, 4]), op=ALU.mult),
                        reads=[r_tmp[0], r_EB], writes=[r_tmp[4]])
                for g in range(2):
                    cs = slice(g * 256, (g + 1) * 256)
                    S.op("act", lambda e, g=g, cs=cs: e.activation(out=et1[0:64, cs], in_=psum[0:64, 2 + g, 0:256], func=AF.Exp),
                         reads=[r_ps[2 + g]], writes=[r_tmp[1]])
                    S.op("dve", lambda e, g=g, cs=cs: e.tensor_tensor(
                        out=v4(PT[0:64, 2, cs]), in0=v4(et1[0:64, cs]),
                        in1=EBS[0:64, 4 * g:4 * g + 4, :].rearrange("p h (b t) -> p b h t", t=4), op=ALU.mult),
                        reads=[r_tmp[1], r_ebs], writes=[r_tmp[5]])
                for (bank, is_den) in ((4, False), (5, True)):
                    for g in range(2):
                        gp = slice(g * 64, (g + 1) * 64)
                        lhs_c = ones_b[0:64, 0:64] if is_den else vtok[0:64, 16, g * 64:(g + 1) * 64]
                        mm(psum[gp, bank, 0:256], lhs_c, PT[0:64, 2, g * 256:(g + 1) * 256], True, False,
                           [r_vt[16], r_tmp[5], r_const], [r_ps[bank]], signal=False)
                        for b in range(16):
                            lhs_p = ones_b[:, 0:64] if is_den else vbuf[:, b, g * 64:(g + 1) * 64]
                            mm(psum[gp, bank, b * 16:(b + 1) * 16], lhs_p, PT[:, 0, g * 256 + b * 16:g * 256 + (b + 1) * 16], False, b == 15,
                               [r_vb, r_tmp[4], r_const], [r_ps[bank]], signal=(b == 15 and g == 1))
                den = tmpa[:, 3, 0:256]
                S.op("dve", lambda e: e.tensor_tensor(out=v4(den), in0=v4(psum[:, 5, 0:256]),
                                                      in1=es_bc[:, :, 0:4].unsqueeze(1).broadcast_to([128, 16, 4, 4]), op=ALU.add),
                     reads=[r_ps[5], r_es], writes=[r_tmp[3]])
                S.op("dve", lambda e: e.reciprocal(out=den, in_=den), reads=[r_tmp[3]], writes=[r_tmp[3]])
                S.op("dve", lambda e: e.tensor_tensor(
                    out=attnT[:, 0:4, t0:t0 + 64].rearrange("p h (b t) -> p h b t", t=4),
                    in0=psum[:, 4, 0:256].rearrange("p (b h t) -> p h b t", b=16, h=4),
                    in1=den.rearrange("p (b h t) -> p h b t", b=16, h=4), op=ALU.mult),
                    reads=[r_ps[4], r_tmp[3]], writes=[r_h[hh][4] for hh in range(4)])

            sample_attn()
            if SUB < 6:
                return
            ous = []
            for d in range(DT):
                cs = slice(d * 128, (d + 1) * 128)
                ous.append(wunit([
                    (lambda r: r.rearrange("p (k f) -> p k f", k=8)[0:64, 0:4, :], w_out[0:256, cs].rearrange("(hh dd) c -> dd hh c", dd=64)),
                    (lambda r: r.rearrange("p (k f) -> p k f", k=8)[64:128, 0:4, :], w_out[256:512, cs].rearrange("(hh dd) c -> dd hh c", dd=64)),
                    (lambda r: r.rearrange("p (k f) -> p k f", k=8)[:, 4:8, :], w_out[512:1024, cs].rearrange("(g p) c -> p g c", p=128)),
                ]))
            ocnt = [0]

            def oproj_tile(tt):
                t0, n = TTS[tt]
                for d in range(DT):
                    pb = (4, 5, 0, 1)[ocnt[0] % 4]
                    ocnt[0] += 1
                    for k in range(8):
                        rhs = attnT[:, k, t0:t0 + n] if k < 4 else dT[:, k - 4, t0:t0 + n]
                        rr = r_h[k][tt] if k < 4 else r_dT[k - 4][tt]
                        mm(psum[:, pb, 0:n], ru(ous[d])[:, k, :], rhs, k == 0, k == 7, [r_ring[ous[d]], rr], [r_ps[pb]])
                    evac_add_x(pb, d, tt, t0, n)

            fused_tail(oproj_tile, nxt)
            S.barrier(dma_only=True)

        fg = tmpa[:, 0:2, :].rearrange("p a f -> p (a f)")
        s_fg = S.dsem("fg")
        osb = arena[:, 21120:21120 + 4096].bitcast(F32).rearrange("p (a f) -> p a f", a=2)
        r_os = [Res("os0"), Res("os1")]
        s_os = [S.dsem("os0"), S.dsem("os1")]
        r_fs = [Res(), Res()]
        junkf = tmpa[:, 2:4, :]

        def final_setup():
            ld("sp", fg, din["final_gain"].partition_broadcast(128), s_fg, writes=[r_tmp[0], r_tmp[1]])

        def final_tile(tile):
            blks = [16] if tile == 4 else list(range(4 * tile, 4 * tile + 4))
            for bi in blks:
                t0, n = BLKS[bi]
                sl = bi % 2
                tt = min(t0 // 512, 4)
                pbs = (6, 7) if bi % 2 == 0 else (2, 3)
                ssq = tmpa[:, 6, 16 * sl:16 * sl + 16]
                for half in range(2):
                    pb = pbs[half]
                    for j in range(4):
                        d = half * 4 + j
                        S.op("pe", lambda e, pb=pb, j=j, d=d, n=n, t0=t0: e.transpose(
                            out=psum[0:n, pb, j * 128:(j + 1) * 128], in_=xT[:, d, t0:t0 + n], identity=ident[:]),
                            reads=[r_x[d][tt], r_const], writes=[r_ps[pb]], signal=(j == 3))
                S.op("act", lambda e, n=n, p0=pbs[0], ssq=ssq: e.activation(
                    out=junkf[0:n, :, :], in_=psum[0:n, p0:p0 + 2, :], func=AF.Square, accum_out=ssq[0:n, 0:1]),
                    reads=[r_ps[pbs[0]], r_ps[pbs[1]]], writes=[r_fs[sl]])
                S.op("act", lambda e, n=n, ssq=ssq: e.activation(out=ssq[0:n, 1:2], in_=ssq[0:n, 0:1], func=AF.Sqrt, bias=epsc[0:n, :], scale=1.0 / D),
                     reads=[r_fs[sl], r_const], writes=[r_fs[sl]])
                S.op("dve", lambda e, n=n, ssq=ssq: e.reciprocal(out=ssq[0:n, 1:2], in_=ssq[0:n, 1:2]), reads=[r_fs[sl]], writes=[r_fs[sl]])
                for half in range(2):
                    pb = pbs[half]
                    S.op("dve", lambda e, pb=pb, n=n, half=half, sl=sl, ssq=ssq: e.scalar_tensor_tensor(
                        out=osb[0:n, sl, half * 512:(half + 1) * 512], in0=psum[0:n, pb, :], scalar=ssq[0:n, 1:2],
                        in1=fg[0:n, half * 512:(half + 1) * 512], op0=ALU.mult, op1=ALU.mult),
                        reads=[r_ps[pb], r_fs[sl], r_tmp[0], r_tmp[1]], writes=[r_os[sl]])
                dst = o_yp[t0:t0 + n, :] if bi < 16 else o_ys[:, :]
                ld("sp", dst, osb[0:n, sl, :], s_os[sl], reads=[r_os[sl]])

        if STAGE >= 99:
            load_x(norm=(0, 0), pre_cb=prep_bias)
            prep_bias_b()
            ffn(0, 0, nxt=(0, 1))
            even_mixer(0, nxt=(0, 2))
            ffn(0, 1, nxt=(1, 0))
            ffn(1, 0, nxt=(1, 1))
            odd_mixer(1, nxt=(1, 2))
            ffn(1, 1, after_norm=final_setup, extra=final_tile)
        else:
            if STAGE == 4:
                prep_bias()
            load_x()
            if STAGE == 4:
                prep_bias_b()
            if STAGE == 2:
                ffn(0, 0)
            elif STAGE == 3:
                odd_mixer(1)
            elif STAGE == 4:
                even_mixer(0)
            elif STAGE == 6:
                ffn(0, 0, nxt=(0, 2))
                ffn(0, 1)
            elif STAGE == 5:
                ffn(0, 0); ffn(0, 1); ffn(1, 0); ffn(1, 1)
            S.barrier()
            final_setup()
            for tile in range(5):
                final_tile(tile)

        S.emit(nc, st)
    return nc


_PROG = None


def kernel(**inputs):
    global _PROG
    if _PROG is None:
        _PROG = build_program()
    nc = _PROG
    consts = host_consts()
    f = lambda a: np.ascontiguousarray(np.asarray(a, dtype=np.float32))
    wts = {n_: f(inputs[n_]) for n_, s_ in W_SPECS}
    in_maps = []
    for c in range(NCORES):
        m = {}
        m["x_prompt"] = f(inputs["x_prompt"][c])
        m["x_sample"] = f(inputs["x_sample"][c * 16:(c + 1) * 16]).reshape(NS, D)
        m["state_win_k"] = f(inputs["state_win_k"][0, c * 16:(c + 1) * 16]).reshape(16, 128, 128)
        m["state_win_v"] = f(inputs["state_win_v"][0, c * 16:(c + 1) * 16]).reshape(16, 128, 128)
        m["state_pool"] = f(inputs["state_pool"][0, c * 16:(c + 1) * 16])
        for n_, s_ in W_SPECS:
            m[n_] = wts[n_]
        for n_ in CONST_SHAPES:
            m["c_" + n_] = consts[n_]
        in_maps.append(m)
    res = run_bass_kernel_spmd(nc, in_maps, core_ids=list(range(NCORES)))
    R = res.results
    y_prompt = np.stack([R[c]["y_prompt"] for c in range(NCORES)], 0)
    y_sample = np.concatenate([R[c]["y_sample"].reshape(16, 4, D) for c in range(NCORES)], 0)
    nk_p = np.stack([R[c]["nk_p"].reshape(128, 2, 64) for c in range(NCORES)], 0)[None]
    nv_p = np.stack([R[c]["nv_p"].reshape(128, 2, 64) for c in range(NCORES)], 0)[None]
    np_p = np.stack([R[c]["np_p"] for c in range(NCORES)], 0)[None]
    nk_s = np.concatenate([R[c]["nk_s"].reshape(16, 128, 2, 64) for c in range(NCORES)], 0)[None]
    nv_s = np.concatenate([R[c]["nv_s"].reshape(16, 128, 2, 64) for c in range(NCORES)], 0)[None]
    np_s = np.concatenate([R[c]["np_s"] for c in range(NCORES)], 0)[None]
    sgu_v = np.concatenate([R[c]["sgu_v"].reshape(16, 4, D) for c in range(NCORES)], 0)[None]
    return (y_prompt, y_sample, nk_p, nv_p, np_p, nk_s, nv_s, np_s, sgu_v)
```
